# Optimizing a Trainium2 kernel written in Bass

```python
import jax, jax.numpy as jnp
from jax import lax
import numpy as np

D_MODEL = 1024
BATCH = 8
SEQ = 4096
DEPTH = 1

HEAD_DIM = 64
SSD_HEADS = 16
SSD_D_INNER = SSD_HEADS * HEAD_DIM
SSD_GROUPS = 4
SSD_STATE = 128
SSD_CONV = 4
SSD_CHUNK = 128
CONV_CH = SSD_D_INNER + 2 * SSD_GROUPS * SSD_STATE
ATT_HEADS = 16
ATT_D = ATT_HEADS * HEAD_DIM
DILATED_PATTERNS = ((128, 1), (512, 4), (2048, 16))
MIX_WIDTH = SSD_D_INNER + ATT_D
IN_PROJ_WIDTH = SSD_D_INNER + CONV_CH + SSD_HEADS + 3 * ATT_D
D_FF = 4 * D_MODEL
N_MOD = 6
EPS = 1e-6

kernel_name = 'hybrid_ssd_dilated_attn_block'


def _rms(t):
    tf = t.astype(jnp.float32)
    return tf * lax.rsqrt(jnp.mean(tf * tf, axis=-1, keepdims=True) + EPS)


def rms_norm(t, w):
    return (_rms(t) * w.astype(jnp.float32)).astype(t.dtype)


def causal_depthwise_conv(u, w, bias):
    k = w.shape[0]
    s = u.shape[1]
    up = jnp.pad(u, ((0, 0), (k - 1, 0), (0, 0)))
    out = bias
    for i in range(k):
        out = out + w[i] * up[:, i:i + s]
    return out


def ssd_chunked_scan(xs, dt, a, bm, cm):
    b, s, h, p = xs.shape
    g, n = bm.shape[2], bm.shape[3]
    e = h // g
    nc = s // SSD_CHUNK
    L = SSD_CHUNK
    xdt = (xs.astype(jnp.float32) * dt[..., None]).reshape(b, nc, L, g, e, p)
    bc = bm.reshape(b, nc, L, g, n)
    cc = cm.reshape(b, nc, L, g, n)
    a_cs = jnp.cumsum((dt * a).reshape(b, nc, L, g, e), axis=2)
    seg = a_cs[:, :, :, None] - a_cs[:, :, None, :]
    tril = jnp.tril(jnp.ones((L, L), dtype=bool))[None, None, :, :, None, None]
    lmat = jnp.exp(jnp.where(tril, seg, -jnp.inf))
    cb = jnp.einsum('bclgn,bcsgn->bclsg', cc, bc)
    y_diag = jnp.einsum('bclsge,bcsgep->bclgep', cb[..., None] * lmat, xdt)
    decay_states = jnp.exp(a_cs[:, :, -1:] - a_cs)
    states = jnp.einsum('bclgn,bclge,bclgep->bcgepn', bc, decay_states, xdt)
    chunk_decay = jnp.exp(a_cs[:, :, -1])

    def step(h_prev, inp):
        st, dec = inp
        return h_prev * dec[..., None, None] + st, h_prev

    init = jnp.zeros_like(states[:, 0])
    _, prev_states = lax.scan(step, init, (jnp.moveaxis(states, 1, 0), jnp.moveaxis(chunk_decay, 1, 0)))
    prev_states = jnp.moveaxis(prev_states, 0, 1)
    y_off = jnp.einsum('bclgn,bcgepn,bclge->bclgep', cc, prev_states, jnp.exp(a_cs))
    return (y_diag + y_off).reshape(b, s, h, p)


def dilated_window_attention(q, k, v, window, dilation):
    b, s, h, d = q.shape
    nw = window // dilation
    blk = nw
    L = s // dilation
    nb = -(-L // blk)
    Lp = nb * blk

    def to_classes(t):
        t = t.reshape(b, L, dilation, h, d).transpose(0, 2, 3, 1, 4)
        return jnp.pad(t, ((0, 0), (0, 0), (0, 0), (0, Lp - L), (0, 0)))

    def with_prev(t):
        tp = jnp.pad(t, ((0, 0), (0, 0), (0, 0), (blk, 0), (0, 0)))
        prev = tp[:, :, :, :Lp].reshape(b, dilation, h, nb, blk, d)
        cur = t.reshape(b, dilation, h, nb, blk, d)
        return jnp.concatenate([prev, cur], axis=-2)

    qb = to_classes(q).reshape(b, dilation, h, nb, blk, d)
    kb = with_prev(to_classes(k))
    vb = with_prev(to_classes(v))
    scores = jnp.einsum('brhiqd,brhikd->brhiqk', qb, kb).astype(jnp.float32)
    qi = jnp.arange(blk)[:, None]
    ki = jnp.arange(2 * blk)[None, :]
    dist = blk + qi - ki
    band = (dist >= 0) & (dist <= nw)
    valid = (jnp.arange(nb)[:, None, None] > 0) | (ki[None] >= blk)
    mask = band[None] & valid
    scores = jnp.where(mask, scores, -jnp.inf)
    m = jnp.max(scores, axis=-1, keepdims=True)
    pr = jnp.exp(scores - m)
    denom = jnp.sum(pr, axis=-1, keepdims=True)
    o = jnp.einsum('brhiqk,brhikd->brhiqd', pr, vb.astype(jnp.float32)) / denom
    lse = (m + jnp.log(denom))[..., 0]
    o = o.reshape(b, dilation, h, Lp, d)[:, :, :, :L].transpose(0, 3, 1, 2, 4).reshape(b, s, h, d)
    lse = lse.reshape(b, dilation, h, Lp)[:, :, :, :L].transpose(0, 3, 1, 2).reshape(b, s, h)
    return o, lse


def hybrid_mixer(hn, w_in, conv_w, conv_b, dt_bias, a_log, d_skip, ssd_norm_w,
                 q_norm_w, k_norm_w, attn_norm_w, w_out):
    b, s, _ = hn.shape
    proj = hn @ w_in
    o1 = SSD_D_INNER
    o2 = o1 + CONV_CH
    o3 = o2 + SSD_HEADS
    o4 = o3 + ATT_D
    o5 = o4 + ATT_D
    z, xbc, dt_raw, q, k, v = jnp.split(proj, [o1, o2, o3, o4, o5], axis=-1)

    xbc = jax.nn.silu(causal_depthwise_conv(xbc, conv_w, conv_b))
    xs, bm, cm = jnp.split(xbc, [SSD_D_INNER, SSD_D_INNER + SSD_GROUPS * SSD_STATE], axis=-1)
    xs = xs.reshape(b, s, SSD_HEADS, HEAD_DIM)
    bm = bm.reshape(b, s, SSD_GROUPS, SSD_STATE)
    cm = cm.reshape(b, s, SSD_GROUPS, SSD_STATE)
    dt = jax.nn.softplus(dt_raw.astype(jnp.float32) + dt_bias.astype(jnp.float32))
    a = -jnp.exp(a_log.astype(jnp.float32))
    y = ssd_chunked_scan(xs, dt, a, bm, cm) + d_skip.astype(jnp.float32)[:, None] * xs
    y = y.reshape(b, s, SSD_D_INNER).astype(hn.dtype) * jax.nn.silu(z)
    y_ssd = (_rms(y.reshape(b, s, SSD_GROUPS, SSD_D_INNER // SSD_GROUPS)).reshape(b, s, SSD_D_INNER)
             * ssd_norm_w.astype(jnp.float32)).astype(hn.dtype)

    q = rms_norm(q.reshape(b, s, ATT_HEADS, HEAD_DIM), q_norm_w) * HEAD_DIM ** -0.5
    k = rms_norm(k.reshape(b, s, ATT_HEADS, HEAD_DIM), k_norm_w)
    v = v.reshape(b, s, ATT_HEADS, HEAD_DIM)
    branches = [dilated_window_attention(q, k, v, w, r) for (w, r) in DILATED_PATTERNS]
    outs = jnp.stack([br[0] for br in branches])
    lses = jnp.stack([br[1] for br in branches])
    alpha = jax.nn.softmax(lses, axis=0)
    o = jnp.sum(alpha[..., None] * outs, axis=0).reshape(b, s, ATT_D)
    y_att = rms_norm(o, attn_norm_w).astype(hn.dtype)

    return jnp.concatenate([y_ssd, y_att], axis=-1) @ w_out


def setup_inputs(seed: int = 0) -> dict:
    key = jax.random.key(seed)
    ks = jax.random.split(key, 20)
    f32 = jnp.float32

    def nrm(k, shape, scale):
        return jax.random.normal(k, shape, f32) * scale

    dt0 = jnp.exp(jax.random.uniform(ks[9], (DEPTH, SSD_HEADS), f32, np.log(1e-3), np.log(1e-1)))
    dt_bias = dt0 + jnp.log(-jnp.expm1(-dt0))
    return {
        'x': nrm(ks[0], (BATCH, SEQ, D_MODEL), 1.0),
        'c': nrm(ks[1], (BATCH, D_MODEL), 1.0),
        'norm1_w': 1.0 + nrm(ks[2], (DEPTH, D_MODEL), 0.02),
        'norm2_w': 1.0 + nrm(ks[3], (DEPTH, D_MODEL), 0.02),
        'w_ada': nrm(ks[4], (DEPTH, D_MODEL, N_MOD * D_MODEL), D_MODEL ** -0.5),
        'b_ada': nrm(ks[5], (DEPTH, N_MOD * D_MODEL), 0.01),
        'w_in': nrm(ks[6], (DEPTH, D_MODEL, IN_PROJ_WIDTH), D_MODEL ** -0.5),
        'conv_w': nrm(ks[7], (DEPTH, SSD_CONV, CONV_CH), SSD_CONV ** -0.5),
        'conv_b': nrm(ks[8], (DEPTH, CONV_CH), 0.01),
        'dt_bias': dt_bias,
        'a_log': jnp.log(jax.random.uniform(ks[10], (DEPTH, SSD_HEADS), f32, 1.0, 16.0)),
        'd_skip': 1.0 + nrm(ks[11], (DEPTH, SSD_HEADS), 0.1),
        'ssd_norm_w': 1.0 + nrm(ks[12], (DEPTH, SSD_D_INNER), 0.02),
        'q_norm_w': 1.0 + nrm(ks[13], (DEPTH, HEAD_DIM), 0.02),
        'k_norm_w': 1.0 + nrm(ks[14], (DEPTH, HEAD_DIM), 0.02),
        'attn_norm_w': 1.0 + nrm(ks[15], (DEPTH, ATT_D), 0.02),
        'w_out': nrm(ks[16], (DEPTH, MIX_WIDTH, D_MODEL), MIX_WIDTH ** -0.5),
        'w_ff1': nrm(ks[17], (DEPTH, D_MODEL, D_FF), D_MODEL ** -0.5),
        'w_ff2': nrm(ks[18], (DEPTH, D_FF, D_MODEL), D_FF ** -0.5),
    }


def reference(x, c, norm1_w, norm2_w, w_ada, b_ada, w_in, conv_w, conv_b, dt_bias, a_log,
              d_skip, ssd_norm_w, q_norm_w, k_norm_w, attn_norm_w, w_out, w_ff1, w_ff2):
    c_act = jax.nn.silu(c)
    for l in range(DEPTH):
        mod = c_act @ w_ada[l] + b_ada[l]
        shift1, scale1, gate1, shift2, scale2, gate2 = [t[:, None, :] for t in jnp.split(mod, N_MOD, axis=-1)]
        h1 = rms_norm(x, norm1_w[l]) * (1.0 + scale1) + shift1
        mix = hybrid_mixer(h1, w_in[l], conv_w[l], conv_b[l], dt_bias[l], a_log[l], d_skip[l],
                           ssd_norm_w[l], q_norm_w[l], k_norm_w[l], attn_norm_w[l], w_out[l])
        x = x + gate1 * mix
        h2 = rms_norm(x, norm2_w[l]) * (1.0 + scale2) + shift2
        ff = jnp.square(jax.nn.relu(h2 @ w_ff1[l])) @ w_ff2[l]
        x = x + gate2 * ff
    return x.astype(c.dtype)
```

```python
import numpy as np
from contextlib import ExitStack
import concourse.bass as bass
import concourse.mybir as mybir
from concourse.bass_utils import run_bass_kernel_spmd

F32 = mybir.dt.float32
BF16 = mybir.dt.bfloat16
U8 = mybir.dt.uint8
AF = mybir.ActivationFunctionType
ALU = mybir.AluOpType
AX = mybir.AxisListType

S = 4096
D = 1024
NT = 8
TT = 512
INW = 6160
EPS = 1e-6
DSIZE = {F32: 4, BF16: 2, U8: 1}

PHASES = "MASBC"
DEBUG = False
GROUPS = "dqxzv"
NTILES = 8


class Builder:
    def __init__(self, nc):
        self.nc = nc
        self.es = ExitStack()
        self.engs = ["sync", "act", "dve", "pool", "pe"]
        self.q = {n: [] for n in self.engs}
        self.cnt = {n: 0 for n in self.engs}
        self.waited = {n: {} for n in self.engs}
        self.sem = {n: self.es.enter_context(nc.semaphore("prog_" + n)) for n in self.engs}
        self.arena = self.es.enter_context(nc.sbuf_tensor("arena", [128, 204 * 1024], U8))
        self.psum = self.es.enter_context(nc.psum_tensor("psum", [128, 4096], F32))
        self.aoff = 0
        self.nsem = 0

    def alloc(self, free_shape, dtype):
        n = int(np.prod(free_shape)) * DSIZE[dtype]
        off = (self.aoff + 63) // 64 * 64
        assert off + n <= 204 * 1024, ("SBUF arena overflow", off, n)
        self.aoff = off + n
        ap = self.arena[:, off:off + n].bitcast(dtype)
        if len(free_shape) == 2:
            ap = ap.rearrange("p (a b) -> p a b", a=free_shape[0])
        elif len(free_shape) == 3:
            ap = ap.rearrange("p (a b c) -> p a b c", a=free_shape[0], b=free_shape[1])
        elif len(free_shape) == 4:
            ap = ap.rearrange("p (a b c d) -> p a b c d", a=free_shape[0], b=free_shape[1], c=free_shape[2])
        return ap

    def mark(self):
        return self.aoff

    def release(self, m):
        self.aoff = m

    def bank(self, b, n=512, dtype=F32):
        ap = self.psum[:, b * 512:(b + 1) * 512]
        if dtype == BF16:
            return ap.bitcast(BF16)[:, 0:n]
        return ap[:, 0:n]

    def new_sem(self, name):
        self.nsem += 1
        return self.es.enter_context(self.nc.semaphore(f"{name}_{self.nsem}"))

    def _waits(self, eng, deps):
        waits = []
        for d in deps:
            if d is None:
                continue
            if isinstance(d, list):
                for dd in d:
                    waits += self._waits(eng, [dd])
                continue
            s, v = d
            key = s.num
            if self.waited[eng].get(key, 0) < v:
                self.waited[eng][key] = v
                waits.append((s, v))
        return waits

    def op(self, eng, fn, deps=(), track=True):
        waits = self._waits(eng, deps)
        h = None
        sem = self.sem[eng]
        if track:
            self.cnt[eng] += 1
            h = (sem, self.cnt[eng])

        def run(e, fn=fn, waits=waits, track=track, sem=sem):
            for s, v in waits:
                e.wait_ge(s, v)
            ins = fn(e)
            if track:
                ins.then_inc(sem, 1)
        self.q[eng].append(run)
        return h

    def dma(self, eng, out, in_, deps=(), dsem=None, **kw):
        waits = self._waits(eng, deps)
        dsem[1] += 16
        h = (dsem[0], dsem[1])
        sem = dsem[0]

        def run(e, waits=waits, sem=sem, out=out, in_=in_, kw=kw):
            for s, v in waits:
                e.wait_ge(s, v)
            e.dma_start(out=out, in_=in_, **kw).then_inc(sem, 16)
        self.q[eng].append(run)
        return h

    def dsem(self, name):
        return [self.new_sem(name), 0]

    def final_wait(self, eng, deps):
        waits = self._waits(eng, deps)

        def run(e, waits=waits):
            for s, v in waits:
                e.wait_ge(s, v)
        self.q[eng].append(run)

    def phase_barrier(self, extra=()):
        hs = self.barrier_handles() + list(extra)
        for n in self.engs:
            self.final_wait(n, hs)

    def barrier_handles(self):
        return [(self.sem[n], self.cnt[n]) for n in ["act", "dve", "pool", "pe"] if self.cnt[n] > 0]

    def finish(self):
        nc = self.nc
        with nc.Block() as block:
            @block.sync
            def _(e):
                for f in self.q["sync"]:
                    f(e)

            @block.scalar
            def _(e):
                for f in self.q["act"]:
                    f(e)

            @block.vector
            def _(e):
                for f in self.q["dve"]:
                    f(e)

            @block.gpsimd
            def _(e):
                for f in self.q["pool"]:
                    f(e)

            @block.tensor
            def _(e):
                for f in self.q["pe"]:
                    f(e)
        self.es.close()


def build_program(phases=PHASES, debug=DEBUG):
    nc = bass.Bass("TRN2", target_bir_lowering=False)
    dr = {}

    def din(name, shape, dt=F32):
        dr[name] = nc.dram_tensor(name, list(shape), dt, kind="ExternalInput").ap()
        return dr[name]

    def dscr(name, shape, dt):
        kind = "ExternalOutput" if (debug and name in debug) else "Internal"
        dr[name] = nc.dram_tensor(name, list(shape), dt, kind=kind).ap()
        return dr[name]

    xT = din("xT", [D, S])
    cT = din("cT", [128, 8])
    w_ada = din("w_ada", [D, 6 * D])
    b_adaT = din("b_adaT", [128, 48])
    n1w = din("n1w", [128, 8])
    n2w = din("n2w", [128, 8])
    w_in = din("w_in", [D, INW])
    convw = din("convw", [128, 16, 4])
    convb = din("convb", [128, 16])
    dtb = din("dtb", [128, 16])
    alog = din("alog", [128, 16])
    dskip = din("dskip", [128, 16])
    ssdnw = din("ssdnw", [128, 1024])
    qkw = din("qkw", [128, 2])
    attnw = din("attnw", [128, 1024])
    w_out = din("w_out", [2 * D, D])
    w_ff1 = din("w_ff1", [D, 4 * D])
    w_ff2 = din("w_ff2", [4 * D, D])
    cmat = din("cmat", [128, 6, 128])
    outT = nc.dram_tensor("outT", [D, S], F32, kind="ExternalOutput").ap()

    ZS = dscr("ZS", [S, 1024], BF16)
    XS = dscr("XS", [S, 1024], BF16)
    BM = dscr("BM", [S, 512], BF16)
    BCT = dscr("BCT", [1024, S], BF16)
    QT = dscr("QT", [1024, S], BF16)
    KT = dscr("KT", [1024, S], BF16)
    VA = dscr("VA", [S, 16 * 65], BF16)
    DTS = dscr("DTS", [S, 32], F32)
    YS = dscr("YS", [S, 1024], BF16)
    OD = dscr("OD", [3, S, 16 * 65], F32)
    X1T = dscr("X1T", [D, S], F32)
    H2T = dscr("H2T", [D, S], BF16)
    MODT = dscr("MODT", [128, 48], F32)

    B = Builder(nc)
    op, dma = B.op, B.dma

    modT = B.alloc([48], F32)
    s1 = B.alloc([8], F32)
    s2 = B.alloc([8], F32)
    ident = B.alloc([128], BF16)
    ones_bf = B.alloc([128], BF16)
    blk64 = B.alloc([128], BF16)
    ones_f = B.alloc([128], F32)
    cst_f = B.alloc([6, 128], F32)
    convw_sb = B.alloc([16, 4], F32)
    convb_sb = B.alloc([16], F32)
    qkw_sb = B.alloc([2], F32)
    dtb_sb = B.alloc([16], F32)
    A_sb = B.alloc([16], F32)
    dskip_sb = B.alloc([16], F32)
    n1w_sb = B.alloc([8], F32)
    n2w_sb = B.alloc([8], F32)
    persist_mark = B.mark()

    ds_c = B.dsem("const")
    hc = []
    hc.append(dma("sync", cst_f, cmat, dsem=ds_c))
    hc.append(dma("sync", convw_sb, convw, dsem=ds_c))
    hc.append(dma("sync", convb_sb, convb, dsem=ds_c))
    hc.append(dma("sync", qkw_sb, qkw, dsem=ds_c))
    hc.append(dma("sync", dtb_sb, dtb, dsem=ds_c))
    hc.append(dma("sync", A_sb, alog, dsem=ds_c))
    hc.append(dma("sync", dskip_sb, dskip, dsem=ds_c))
    hc.append(dma("sync", n1w_sb, n1w, dsem=ds_c))
    hc.append(dma("sync", n2w_sb, n2w, dsem=ds_c))
    hconst = hc[-1]

    h_id = op("dve", lambda e: e.tensor_copy(out=ident, in_=cst_f[:, 0, :]), [hconst])
    h_ones = op("dve", lambda e: e.tensor_copy(out=ones_bf, in_=cst_f[:, 1, :]), [hconst])
    h_blk = op("dve", lambda e: e.tensor_copy(out=blk64, in_=cst_f[:, 2, :]), [hconst])
    h_onesf = op("dve", lambda e: e.tensor_copy(out=ones_f, in_=cst_f[:, 1, :]), [hconst])
    h_A = op("act", lambda e: e.activation(out=A_sb, in_=A_sb, func=AF.Exp), [hconst])
    h_A = op("dve", lambda e: e.tensor_scalar_mul(out=A_sb, in0=A_sb, scalar1=-1.0), [h_A])
    h_qw = op("dve", lambda e: e.tensor_scalar_mul(out=qkw_sb[:, 0:1], in0=qkw_sb[:, 0:1], scalar1=0.125), [hconst])
    h_setup = [h_id, h_ones, h_blk, h_onesf, h_A, h_qw]

    out_handles = []

    Wsb = B.alloc([8, INW], BF16)
    pieces = [(0, 1540), (1540, 3080), (3080, 4620), (4620, 6160)]
    hW = []
    for (c0, c1) in pieces:
        dsw = B.dsem("W")
        h = None
        for kc in range(8):
            h = dma("pool", Wsb[:, kc, c0:c1], w_in[kc * 128:(kc + 1) * 128, c0:c1], dsem=dsw)
        hW.append(h)

    def wdeps(a, b_):
        return [hW[i] for i, (c0, c1) in enumerate(pieces) if a < c1 and b_ > c0]

    w_mark = B.mark()

    if "M" in phases:
        m0 = B.mark()
        c_sb = B.alloc([8], F32)
        cact = B.alloc([8], F32)
        bada = B.alloc([48], F32)
        wslM = [B.alloc([8, 1024], F32) for _ in range(2)]
        accM = [B.alloc([1024], F32) for _ in range(2)]
        ds_m = B.dsem("m_in")
        hcin = dma("sync", c_sb, cT, dsem=ds_m)
        hcin = dma("sync", bada, b_adaT, dsem=ds_m)
        h = op("act", lambda e: e.activation(out=cact, in_=c_sb, func=AF.Exp, scale=-1.0), [hcin])
        h = op("dve", lambda e: e.tensor_scalar_add(out=cact, in0=cact, scalar1=1.0), [h])
        h = op("dve", lambda e: e.reciprocal(out=cact, in_=cact), [h])
        hcact = op("dve", lambda e: e.tensor_mul(out=cact, in0=cact, in1=c_sb), [h])
        ds_w = [B.dsem("m_w0"), B.dsem("m_w1")]
        wfreeM = [None, None]
        accfreeM = [None, None]
        psM = B.bank(0, 48)
        hmm = None
        for sl in range(6):
            b = sl % 2
            hw = None
            for kc in range(8):
                hw = dma("sync", wslM[b][:, kc, :], w_ada[kc * 128:(kc + 1) * 128, sl * 1024:(sl + 1) * 1024],
                         deps=[wfreeM[b]], dsem=ds_w[b])
            eng = "dve"
            h = op(eng, lambda e, b=b: e.tensor_scalar_mul(out=accM[b], in0=wslM[b][:, 0, :], scalar1=cact[:, 0:1]),
                   [hw, hcact, accfreeM[b]])
            for kc in range(1, 8):
                h = op(eng, lambda e, b=b, kc=kc: e.scalar_tensor_tensor(
                    out=accM[b], in0=wslM[b][:, kc, :], scalar=cact[:, kc:kc + 1], in1=accM[b],
                    op0=ALU.mult, op1=ALU.add), [h])
            wfreeM[b] = h
            for jt in range(8):
                col = sl * 8 + jt
                hmm = op("pe", lambda e, b=b, jt=jt, col=col: e.matmul(
                    psM[:, col:col + 1], lhsT=accM[b][:, jt * 128:(jt + 1) * 128], rhs=ones_f[:, 0:1],
                    start=True, stop=True), [h, h_onesf])
            accfreeM[b] = hmm
        hmod = op("dve", lambda e: e.tensor_add(out=modT, in0=psM, in1=bada), [hmm, hcin])
        h = op("dve", lambda e: e.scalar_tensor_tensor(out=s1, in0=modT[:, 8:16], scalar=1.0, in1=n1w_sb,
                                                      op0=ALU.add, op1=ALU.mult), [hmod, hconst])
        hmod2 = op("dve", lambda e: e.scalar_tensor_tensor(out=s2, in0=modT[:, 32:40], scalar=1.0, in1=n2w_sb,
                                                          op0=ALU.add, op1=ALU.mult), [h])
        ds_mo = B.dsem("m_out")
        out_handles.append(dma("sync", MODT, modT, deps=[hmod2], dsem=ds_mo))
        h_mod = hmod2
        B.release(m0)
    else:
        h_mod = None


    if "A" in phases:
        mA = B.mark()
        B.phase_barrier(out_handles)
        xt = B.alloc([8, 512], F32)
        sq = [B.alloc([512], BF16) for _ in range(2)]
        h1 = [B.alloc([8, 512], BF16) for _ in range(2)]
        rstd = B.alloc([512], F32)
        xr = [B.alloc([515], F32) for _ in range(3)]
        acc = [B.alloc([512], F32) for _ in range(3)]
        xc = [B.alloc([512], BF16) for _ in range(3)]
        halo = B.alloc([16, 3], F32)
        qsq = [B.alloc([512], BF16) for _ in range(2)]
        qraw = [B.alloc([512], F32) for _ in range(2)]
        qrs = [B.alloc([512], F32) for _ in range(2)]
        qn = [B.alloc([512], BF16) for _ in range(3)]
        zs = [B.alloc([512], BF16) for _ in range(3)]
        VAst = B.alloc([4, 16, 65], BF16)
        XSst = B.alloc([4, 1024], BF16)
        BMst = B.alloc([4, 512], BF16)
        tdt = B.alloc([4, 16], F32)
        adt = B.alloc([4, 16], F32)
        dts = B.alloc([4, 32], F32)

        h_va1 = op("pool", lambda e: e.memset(VAst, 1.0))
        h_halo0 = op("pool", lambda e: e.memset(halo, 0.0))
        ds_x = B.dsem("x")
        dsd = {}

        def sd(key):
            if key not in dsd:
                dsd[key] = B.dsem("o")
            return dsd[key]
        bank_free = {b: None for b in range(8)}
        free = {}

        def fr(name):
            return free.get(name)

        state = {"xt_free": None, "h1": [None, None], "h1_free": [None, None]}
        phaseA_out = []

        def prep(i):
            b = i % 2
            hx = None
            for kc in range(8):
                hx = dma("sync", xt[:, kc, :], xT[kc * 128:(kc + 1) * 128, i * 512:(i + 1) * 512],
                         deps=[state["xt_free"]], dsem=ds_x)
            hst = None
            for kc in range(8):
                sl = kc % 2
                hs = op("pool", lambda e, kc=kc, sl=sl: e.tensor_tensor(out=sq[sl], in0=xt[:, kc, :], in1=xt[:, kc, :],
                                                                      op=ALU.mult), [hx, fr(("sq", sl))])
                hst = op("pe", lambda e, kc=kc, sl=sl: e.matmul(B.bank(0), lhsT=ones_bf, rhs=sq[sl],
                                                               start=(kc == 0), stop=(kc == 7)),
                         [hs, h_ones, bank_free[0] if kc == 0 else None])
                free[("sq", sl)] = hst
            h = op("act", lambda e: e.activation(out=rstd, in_=B.bank(0), func=AF.Ln, scale=1.0 / D, bias=EPS),
                   [hst, fr("rstd")])
            bank_free[0] = h
            h = op("act", lambda e: e.activation(out=rstd, in_=rstd, func=AF.Exp, scale=-0.5), [h])
            hh = None
            for kc in range(8):
                hn = op("dve", lambda e, kc=kc: e.tensor_mul(out=xt[:, kc, :], in0=xt[:, kc, :], in1=rstd), [h, hst])
                hh = op("pool", lambda e, kc=kc, b=b: e.tensor_scalar(
                    out=h1[b][:, kc, :], in0=xt[:, kc, :], scalar1=s1[:, kc:kc + 1], scalar2=modT[:, kc:kc + 1],
                    op0=ALU.mult, op1=ALU.add), [hn, h_mod, state["h1_free"][b]])
            free["rstd"] = hn
            state["xt_free"] = hh
            state["h1"][b] = hh

        rot = {"main": 0, "qk": 0, "x3": 0, "qn": 0, "zs": 0, "tp": 0}
        MAIN_BANKS = [2, 3, 4, 5]

        def next_bank():
            b = MAIN_BANKS[rot["main"] % len(MAIN_BANKS)]
            rot["main"] += 1
            return b

        def mm_group_f(i, col0):
            b = i % 2
            bk = next_bank()
            h = None
            for kc in range(8):
                h = op("pe", lambda e, kc=kc, bk=bk, b=b: e.matmul(
                    B.bank(bk), lhsT=Wsb[:, kc, col0:col0 + 128], rhs=h1[b][:, kc, :],
                    start=(kc == 0), stop=(kc == 7)),
                    ([state["h1"][b], bank_free[bk]] + wdeps(col0, col0 + 128)) if kc == 0 else [],
                    track=(kc == 7))
            return bk, h

        def mm_group_t(i, st, col0, ncols):
            b = i % 2
            bk = next_bank()
            h = None
            for kc in range(8):
                h = op("pe", lambda e, kc=kc, bk=bk, b=b: e.matmul(
                    B.bank(bk, ncols), lhsT=h1[b][:, kc, st * 128:(st + 1) * 128], rhs=Wsb[:, kc, col0:col0 + ncols],
                    start=(kc == 0), stop=(kc == 7)),
                    ([state["h1"][b], bank_free[bk]] + wdeps(col0, col0 + ncols)) if kc == 0 else [],
                    track=(kc == 7))
            return bk, h

        for i in range(NTILES):
            if i == 0:
                prep(0)
            b = i % 2
            c0t = i * 512
            for st in (range(4) if "d" in GROUPS else []):
                bk, hm = mm_group_t(i, st, 3072, 16)
                h = op("dve", lambda e, bk=bk, st=st: e.tensor_add(out=tdt[:, st, :], in0=B.bank(bk, 16), in1=dtb_sb),
                       [hm, hconst, fr("dts")])
                bank_free[bk] = h
                h2 = op("act", lambda e, st=st: e.activation(out=adt[:, st, :], in_=tdt[:, st, :], func=AF.Abs), [h])
                h2 = op("act", lambda e, st=st: e.activation(out=adt[:, st, :], in_=adt[:, st, :], func=AF.Exp, scale=-1.0), [h2])
                h2 = op("act", lambda e, st=st: e.activation(out=adt[:, st, :], in_=adt[:, st, :], func=AF.Ln, bias=1.0), [h2])
                h2 = op("dve", lambda e, st=st: e.scalar_tensor_tensor(out=dts[:, st, 0:16], in0=tdt[:, st, :], scalar=0.0,
                                                                      in1=adt[:, st, :], op0=ALU.max, op1=ALU.add), [h2])
                h2 = op("dve", lambda e, st=st: e.tensor_mul(out=dts[:, st, 16:32], in0=dts[:, st, 0:16], in1=A_sb), [h2, h_A])
            if "d" in GROUPS:
                free["dts"] = dma("sync", DTS[c0t:c0t + 512, :].rearrange("(st p) c -> p st c", p=128), dts, deps=[h2], dsem=sd("dts"))
            for j in (range(16) if "q" in GROUPS else range(4)):
                if "q" not in GROUPS:
                    if j == 3 and i + 1 < NTILES:
                        prep(i + 1)
                    continue
                isq = j < 8
                col0 = (3088 if isq else 4112) + (j % 8) * 128
                bk, hm = mm_group_f(i, col0)
                s2_ = rot["qk"] % 2
                rot["qk"] += 1
                hs = op("act", lambda e, bk=bk, s2_=s2_: e.activation(out=qsq[s2_], in_=B.bank(bk), func=AF.Square),
                        [hm, fr(("qsq", s2_))])
                hr = op("act", lambda e, bk=bk, s2_=s2_: e.activation(out=qraw[s2_], in_=B.bank(bk), func=AF.Copy),
                        [hm, hs, fr(("qraw", s2_))])
                bank_free[bk] = [hs, hr]
                hp = op("pe", lambda e, s2_=s2_: e.matmul(B.bank(1), lhsT=blk64, rhs=qsq[s2_], start=True, stop=True),
                        [hs, h_blk, bank_free[1]])
                free[("qsq", s2_)] = hp
                hl = op("act", lambda e, s2_=s2_: e.activation(out=qrs[s2_], in_=B.bank(1), func=AF.Ln, bias=EPS),
                        [hp, fr(("qrs", s2_))])
                bank_free[1] = hl
                hl = op("act", lambda e, s2_=s2_: e.activation(out=qrs[s2_], in_=qrs[s2_], func=AF.Exp, scale=-0.5), [hl])
                s3 = rot["qn"] % 3
                rot["qn"] += 1
                wc = 0 if isq else 1
                hq = op("dve", lambda e, s2_=s2_, s3=s3, wc=wc: e.scalar_tensor_tensor(
                    out=qn[s3], in0=qraw[s2_], scalar=qkw_sb[:, wc:wc + 1], in1=qrs[s2_], op0=ALU.mult, op1=ALU.mult),
                    [hl, hr, h_qw, fr(("qn", s3))])
                free[("qraw", s2_)] = hq
                free[("qrs", s2_)] = hq
                dst = (QT if isq else KT)[(j % 8) * 128:(j % 8 + 1) * 128, c0t:c0t + 512]
                free[("qn", s3)] = dma("sync", dst, qn[s3], deps=[hq], dsem=sd(("qn", s3)))
                phaseA_out.append(free[("qn", s3)])
                if j == 3 and i + 1 < NTILES:
                    prep(i + 1)
            for ct in (range(16) if "x" in GROUPS else []):
                col0 = 1024 + ct * 128
                bk, hm = mm_group_f(i, col0)
                s3 = rot["x3"] % 3
                rot["x3"] += 1
                ha = op("act", lambda e, bk=bk, s3=s3, ct=ct: e.activation(
                    out=acc[s3], in_=B.bank(bk), func=AF.Identity, scale=convw_sb[:, ct, 3:4], bias=convb_sb[:, ct:ct + 1]),
                    [hm, hconst, fr(("acc", s3))])
                hc_ = op("act", lambda e, bk=bk, s3=s3: e.activation(out=xr[s3][:, 3:515], in_=B.bank(bk), func=AF.Copy),
                         [hm, fr(("xr", s3))])
                bank_free[bk] = hc_
                hh1 = op("pool", lambda e, s3=s3, ct=ct: e.tensor_copy(out=xr[s3][:, 0:3], in_=halo[:, ct, :]),
                         [fr(("xr", s3)), h_halo0, fr(("halo", ct))])
                hh2 = op("pool", lambda e, s3=s3, ct=ct: e.tensor_copy(out=halo[:, ct, :], in_=xr[s3][:, 512:515]), [hc_, hh1])
                free[("halo", ct)] = hh2
                ht = ha
                for k in range(3):
                    ht = op("dve", lambda e, s3=s3, ct=ct, k=k: e.scalar_tensor_tensor(
                        out=acc[s3], in0=xr[s3][:, k:k + 512], scalar=convw_sb[:, ct, k:k + 1], in1=acc[s3],
                        op0=ALU.mult, op1=ALU.add), [ht, hc_, hh1])
                free[("xr", s3)] = [ht, hh2]
                hx_ = op("act", lambda e, s3=s3: e.activation(out=xc[s3], in_=acc[s3], func=AF.Silu),
                         [ht, fr(("xc", s3))])
                free[("acc", s3)] = hx_
                last = [hx_]
                if ct < 12:
                    tpi = rot["tp"] % 2
                    rot["tp"] += 1
                    tp = B.bank(6 + tpi, 1024, BF16)[:, 0:512]
                    hp = None
                    for st in range(4):
                        hp = op("pe", lambda e, st=st, s3=s3, tp=tp: e.transpose(
                            out=tp[:, st * 128:(st + 1) * 128], in_=xc[s3][:, st * 128:(st + 1) * 128], identity=ident),
                            [hx_, h_id, fr(("tp", tpi))] if st == 0 else [], track=(st == 3))
                    if ct < 8:
                        dst_ap = XSst[:, :, ct * 128:(ct + 1) * 128]
                        fkey = "XSst"
                    else:
                        dst_ap = BMst[:, :, (ct - 8) * 128:(ct - 7) * 128]
                        fkey = "BMst"
                    he = op("dve", lambda e, tp=tp, dst_ap=dst_ap: e.tensor_copy(
                        out=dst_ap, in_=tp.rearrange("p (s c) -> p s c", s=4)), [hp, fr(fkey)])
                    free[("tp", tpi)] = he
                    last.append(hp)
                    free.setdefault("st_writes", []).append(he)
                if ct >= 8:
                    hd = dma("sync", BCT[(ct - 8) * 128:(ct - 7) * 128, c0t:c0t + 512], xc[s3], deps=[hx_], dsem=sd(("xc", s3)))
                    phaseA_out.append(hd)
                    last.append(hd)
                free[("xc", s3)] = last
            if "x" not in GROUPS:
                free["st_writes"] = []
            hd = dma("sync", XS[c0t:c0t + 512, :].rearrange("(st p) c -> p st c", p=128), XSst,
                     deps=free["st_writes"], dsem=sd("XSst"))
            free["XSst"] = hd
            phaseA_out.append(hd)
            hd = dma("sync", BM[c0t:c0t + 512, :].rearrange("(st p) c -> p st c", p=128), BMst,
                     deps=free["st_writes"], dsem=sd("BMst"))
            free["BMst"] = hd
            phaseA_out.append(hd)
            free["st_writes"] = []
            for st in (range(4) if "z" in GROUPS else []):
                for half in range(2):
                    bk, hm = mm_group_t(i, st, half * 512, 512)
                    s3 = rot["zs"] % 3
                    rot["zs"] += 1
                    hz = op("act", lambda e, bk=bk, s3=s3: e.activation(out=zs[s3], in_=B.bank(bk), func=AF.Silu),
                            [hm, fr(("zs", s3))])
                    bank_free[bk] = hz
                    r0 = c0t + st * 128
                    free[("zs", s3)] = dma("sync", ZS[r0:r0 + 128, half * 512:(half + 1) * 512], zs[s3], deps=[hz], dsem=sd(("zs", s3)))
                    phaseA_out.append(free[("zs", s3)])
            hv = []
            for st in (range(4) if "v" in GROUPS else []):
                for half in range(2):
                    bk, hm = mm_group_t(i, st, 5136 + half * 512, 512)
                    h = op("dve", lambda e, bk=bk, st=st, half=half: e.tensor_copy(
                        out=VAst[:, st, half * 8:(half + 1) * 8, 0:64],
                        in_=B.bank(bk).rearrange("p (h d) -> p h d", h=8)), [hm, h_va1, fr("VAst")])
                    bank_free[bk] = h
                    hv.append(h)
            free["VAst"] = dma("sync", VA[c0t:c0t + 512, :].rearrange("(st p) c -> p st c", p=128),
                               VAst.rearrange("p s h d -> p s (h d)"), deps=hv, dsem=sd("VAst"))
            phaseA_out.append(free["VAst"])
            state["h1_free"][b] = (B.sem["pe"], B.cnt["pe"])
        phaseA_done = [(v[0], v[1]) for v in dsd.values()]
        out_handles += phaseA_done
        B.release(mA)


    def phase_S():
        B.release(persist_mark)
        mS = B.mark()
        B.phase_barrier(out_handles)
        NCH = 32
        tri_f = cst_f[:, 3, :]
        strict_f = cst_f[:, 4, :]
        tri_b = B.alloc([128], BF16)
        ssdnw_sb = B.alloc([1024], F32)
        dts_all = B.alloc([NCH, 32], F32)
        xs_ = [B.alloc([16, 64], BF16) for _ in range(2)]
        bm_ = [B.alloc([512], BF16) for _ in range(2)]
        bct_ = [B.alloc([8, 128], BF16) for _ in range(2)]
        zs_ = [B.alloc([1024], BF16) for _ in range(2)]
        R = B.alloc([16, 128], F32)
        Lt = [B.alloc([16, 128], BF16) for _ in range(2)]
        cbm = [B.alloc([4, 128], BF16) for _ in range(2)]
        G = [B.alloc([16, 128], BF16) for _ in range(2)]
        xdt = [B.alloc([16, 64], BF16) for _ in range(2)]
        xdec = [B.alloc([16, 64], BF16) for _ in range(2)]
        t1 = [B.alloc([16, 64], F32) for _ in range(2)]
        yo = [B.alloc([1024], BF16) for _ in range(2)]
        sqj = B.alloc([256], BF16)
        ed = [B.alloc([32], F32) for _ in range(2)]
        ss = [B.alloc([4], F32) for _ in range(2)]
        Sst = B.alloc([16, 64], F32)
        Sb = B.alloc([16, 64], BF16)

        ds0 = B.dsem("s_c")
        hc0 = dma("sync", ssdnw_sb, ssdnw, dsem=ds0)
        hc0 = dma("sync", dts_all, DTS.rearrange("(c p) k -> p c k", p=128), dsem=ds0)
        h_trib = op("dve", lambda e: e.tensor_copy(out=tri_b, in_=tri_f))
        h_s0 = op("pool", lambda e: e.memset(Sst, 0.0))
        h_sb0 = op("pool", lambda e: e.memset(Sb, 0.0))
        ds_l = [B.dsem("s_l0"), B.dsem("s_l1")]
        ds_y = [B.dsem("s_y0"), B.dsem("s_y1")]
        fr_ = {}
        bfree = {b: None for b in range(8)}

        def F(k):
            return fr_.get(k)

        hS = h_s0
        hSb = h_sb0
        for c in range(NCH):
            b = c % 2
            r0 = c * 128
            hl = dma("sync", xs_[b].rearrange("p h d -> p (h d)"), XS[r0:r0 + 128, :], deps=[F(("ld", b))], dsem=ds_l[b])
            hl = dma("sync", bm_[b], BM[r0:r0 + 128, :], dsem=ds_l[b])
            hl = dma("sync", bct_[b], BCT[:, r0:r0 + 128].rearrange("(g p) t -> p g t", p=128), dsem=ds_l[b])
            hl = dma("sync", zs_[b], ZS[r0:r0 + 128, :], dsem=ds_l[b])
            dA = dts_all[:, c, 16:32]
            dtc = dts_all[:, c, 0:16]
            hR = op("pool", lambda e, dA=dA: e.tensor_tensor(
                out=R, in0=tri_f.unsqueeze(1).to_broadcast([128, 16, 128]),
                in1=dA.unsqueeze(2).to_broadcast([128, 16, 128]), op=ALU.mult), [hc0, hconst, F("R")])
            hseg = []
            for q4 in range(4):
                hseg.append(op("pe", lambda e, q4=q4: e.matmul(
                    B.bank(q4), lhsT=strict_f, rhs=R[:, q4 * 4:(q4 + 1) * 4, :].rearrange("p h l -> p (h l)"),
                    start=True, stop=True), [hR, bfree[q4]]))
            fr_["R"] = hseg[-1]
            hsm = op("pe", lambda e, dA=dA: e.matmul(B.bank(5, 16), lhsT=tri_f, rhs=dA, start=True, stop=True),
                     [hc0, bfree[5]], track=False)
            hsm = op("pe", lambda e, dA=dA: e.matmul(B.bank(5, 32)[:, 16:32], lhsT=ones_f, rhs=dA, start=True, stop=True),
                     [h_onesf])
            hL = None
            for q4 in range(4):
                hL = op("act", lambda e, q4=q4, b=b: e.activation(
                    out=Lt[b][:, q4 * 4:(q4 + 1) * 4, :].rearrange("p h l -> p (h l)"), in_=B.bank(q4), func=AF.Exp),
                    [hseg[q4], F(("Lt", b))])
                bfree[q4] = hL
            hed = op("act", lambda e, b=b: e.activation(out=ed[b], in_=B.bank(5, 32), func=AF.Exp), [hsm, F(("ed", b))])
            bfree[5] = hed
            hcb = None
            for g in range(4):
                hcb = op("pe", lambda e, g=g, b=b: e.matmul(
                    B.bank(4)[:, g * 128:(g + 1) * 128], lhsT=bct_[b][:, g, :], rhs=bct_[b][:, 4 + g, :],
                    start=True, stop=True), [hl, bfree[4]] if g == 0 else [], track=(g == 3))
            hcbm = op("dve", lambda e, b=b: e.tensor_tensor(
                out=cbm[b], in0=B.bank(4).rearrange("p (g l) -> p g l", g=4),
                in1=tri_b.unsqueeze(1).to_broadcast([128, 4, 128]), op=ALU.mult), [hcb, h_trib, F(("cbm", b))])
            bfree[4] = hcbm
            hG = None
            for g in range(4):
                hG = op("dve", lambda e, g=g, b=b: e.tensor_tensor(
                    out=G[b][:, 4 * g:4 * g + 4, :], in0=Lt[b][:, 4 * g:4 * g + 4, :],
                    in1=cbm[b][:, g, :].unsqueeze(1).to_broadcast([128, 4, 128]), op=ALU.mult),
                    [hL, hcbm, F(("G", b))])
            hxdt = op("pool", lambda e, b=b, dtc=dtc: e.tensor_tensor(
                out=xdt[b], in0=xs_[b], in1=dtc.unsqueeze(2).to_broadcast([128, 16, 64]), op=ALU.mult),
                [hl, hc0, F(("xdt", b))])
            hxdec = op("pool", lambda e, b=b: e.tensor_tensor(
                out=xdec[b], in0=xdt[b], in1=Lt[b][:, :, 127:128].to_broadcast([128, 16, 64]), op=ALU.mult),
                [hxdt, hL, F(("xdec", b))])
            hyd = None
            for h_ in range(16):
                hyd = op("pe", lambda e, h_=h_, b=b: e.matmul(
                    B.psum[:, 3072 + h_ * 64:3072 + (h_ + 1) * 64], lhsT=G[b][:, h_, :], rhs=xdt[b][:, h_, :],
                    start=True, stop=True), [hG, hxdt, bfree[6], bfree[7]] if h_ == 0 else [], track=(h_ == 15))
            fr_[("G", b)] = hyd
            hyo = None
            for g in range(4):
                hyo = op("pe", lambda e, g=g, b=b: e.matmul(
                    B.psum[:, g * 256:(g + 1) * 256], lhsT=bct_[b][:, 4 + g, :],
                    rhs=Sb[:, 4 * g:4 * g + 4, :].rearrange("p h d -> p (h d)"), start=True, stop=True),
                    [hSb, bfree[0], bfree[1]] if g == 0 else [], track=(g == 3))
            hst = None
            for g in range(4):
                hst = op("pe", lambda e, g=g, b=b: e.matmul(
                    B.psum[:, 1024 + g * 256:1024 + (g + 1) * 256], lhsT=bm_[b][:, g * 128:(g + 1) * 128],
                    rhs=xdec[b][:, 4 * g:4 * g + 4, :].rearrange("p h d -> p (h d)"), start=True, stop=True),
                    [hxdec, bfree[2], bfree[3]] if g == 0 else [], track=(g == 3))
            fr_[("xdec", b)] = hst
            fr_[("xdt", b)] = [hyd, hxdec]
            fr_[("Lt", b)] = [hG, hxdec]
            fr_[("cbm", b)] = hG
            yoff = B.psum[:, 0:1024].rearrange("p (h d) -> p h d", h=16)
            ydg = B.psum[:, 3072:4096].rearrange("p (h d) -> p h d", h=16)
            h1_ = op("dve", lambda e, b=b: e.tensor_tensor(
                out=t1[b], in0=yoff, in1=ed[b][:, 0:16].unsqueeze(2).to_broadcast([128, 16, 64]), op=ALU.mult),
                [hyo, hed, F(("t1", b))])
            bfree[0] = h1_
            bfree[1] = h1_
            h2_ = op("dve", lambda e, b=b: e.tensor_tensor(out=t1[b], in0=t1[b], in1=ydg, op=ALU.add), [h1_, hyd])
            bfree[6] = h2_
            bfree[7] = h2_
            h3_ = op("dve", lambda e, b=b: e.tensor_tensor(
                out=yo[b].rearrange("p (h d) -> p h d", h=16), in0=xs_[b],
                in1=dskip_sb.unsqueeze(2).to_broadcast([128, 16, 64]), op=ALU.mult), [hl, hconst, F(("yo", b))])
            h3_ = op("dve", lambda e, b=b: e.tensor_tensor(
                out=t1[b], in0=t1[b], in1=yo[b].rearrange("p (h d) -> p h d", h=16), op=ALU.add), [h2_, h3_])
            h4_ = op("dve", lambda e, b=b: e.tensor_tensor(
                out=t1[b], in0=t1[b], in1=zs_[b].rearrange("p (h d) -> p h d", h=16), op=ALU.mult), [h3_, hl])
            t1f = t1[b].rearrange("p h d -> p (h d)")
            hq_ = None
            for g in range(4):
                hq_ = op("act", lambda e, g=g, b=b, t1f=t1f: e.activation(
                    out=sqj, in_=t1f[:, g * 256:(g + 1) * 256], func=AF.Square, accum_out=ss[b][:, g:g + 1]),
                    [h4_, hq_, F(("ss", b))])
            hq_ = op("act", lambda e, b=b: e.activation(out=ss[b], in_=ss[b], func=AF.Ln, scale=1.0 / 256, bias=EPS), [hq_])
            hq_ = op("act", lambda e, b=b: e.activation(out=ss[b], in_=ss[b], func=AF.Exp, scale=-0.5), [hq_])
            hy = None
            for g in range(4):
                hy = op("dve", lambda e, g=g, b=b, t1f=t1f: e.scalar_tensor_tensor(
                    out=yo[b][:, g * 256:(g + 1) * 256], in0=t1f[:, g * 256:(g + 1) * 256], scalar=ss[b][:, g:g + 1],
                    in1=ssdnw_sb[:, g * 256:(g + 1) * 256], op0=ALU.mult, op1=ALU.mult), [hq_, h4_, hc0])
            fr_[("ss", b)] = hy
            fr_[("t1", b)] = hy
            hdy = dma("sync", YS[r0:r0 + 128, :], yo[b], deps=[hy], dsem=ds_y[b])
            fr_[("yo", b)] = hdy
            hu = op("dve", lambda e, b=b: e.tensor_tensor(
                out=Sst, in0=Sst, in1=ed[b][:, 16:32].unsqueeze(2).to_broadcast([128, 16, 64]), op=ALU.mult),
                [hS, hed])
            hu = op("dve", lambda e: e.tensor_tensor(
                out=Sst, in0=Sst, in1=B.psum[:, 1024:2048].rearrange("p (h d) -> p h d", h=16), op=ALU.add), [hu, hst])
            bfree[2] = hu
            bfree[3] = hu
            hS = hu
            hSb = op("dve", lambda e: e.tensor_copy(out=Sb, in_=Sst), [hu, hyo])
            fr_[("ed", b)] = [h1_, hu]
            fr_[("ld", b)] = [hy, hyd, hst, hcb, hyo, h4_, hxdt]
        out_handles.append((ds_y[0][0], ds_y[0][1]))
        out_handles.append((ds_y[1][0], ds_y[1][1]))
        B.release(mS)

    if "S" in phases:
        phase_S()


    def phase_B():
        B.release(persist_mark)
        mB = B.mark()
        B.phase_barrier(out_handles)
        mask2 = B.alloc([2, 256], BF16)
        qt = [B.alloc([4096], BF16) for _ in range(2)]
        kt = [B.alloc([4096], BF16) for _ in range(2)]
        vv = [[B.alloc([32, 130], BF16) for _ in range(3)] for _ in range(2)]
        ost = [B.alloc([32, 130], F32) for _ in range(2)]
        P = [B.alloc([2, 256], BF16) for _ in range(4)]
        hm_ = None
        for h_ in range(2):
            hm_ = op("dve", lambda e, h_=h_: e.tensor_copy(out=mask2[:, h_, 0:128], in_=cst_f[:, 3, :]), [hconst])
            hm_ = op("dve", lambda e, h_=h_: e.tensor_copy(out=mask2[:, h_, 128:256], in_=cst_f[:, 5, :]), [hconst])
        ds_in = [B.dsem("b_in0"), B.dsem("b_in1")]
        ds_od = [B.dsem("b_od0"), B.dsem("b_od1")]
        VAv = VA
        ODv = OD
        fr_ = {}
        bfree = {b: None for b in range(8)}
        rot = {"S": 0, "O": 0, "P": 0, "ost": 0, "m": 0}
        DIL = [(1, 32), (4, 8), (16, 2)]

        def F(k):
            return fr_.get(k)

        for hp in range(8):
            b = hp % 2
            c0 = hp * 130
            hin = dma("sync", qt[b], QT[hp * 128:(hp + 1) * 128, :], deps=[F(("in", b))], dsem=ds_in[b])
            hin = dma("sync", kt[b], KT[hp * 128:(hp + 1) * 128, :], dsem=ds_in[b])
            for jq in range(4):
                hin = dma("sync", vv[b][0][:, jq * 8:(jq + 1) * 8, :],
                          VAv[jq * 1024:(jq + 1) * 1024, c0:c0 + 130].rearrange("(j i) c -> i j c", i=128), dsem=ds_in[b])
            for r in range(4):
                hin = dma("sync", vv[b][1][:, r * 8:(r + 1) * 8, :],
                          VAv[:, c0:c0 + 130].rearrange("(j i r) c -> i r j c", i=128, r=4)[:, r, :, :], dsem=ds_in[b])
            for r in range(16):
                hin = dma("sync", vv[b][2][:, r * 2:(r + 1) * 2, :],
                          VAv[:, c0:c0 + 130].rearrange("(j i r) c -> i r j c", i=128, r=16)[:, r, :, :], dsem=ds_in[b])
            last_reads = []
            for di, (d, NB) in enumerate(DIL):
                os_i = rot["ost"] % 2
                rot["ost"] += 1
                hev_all = []
                for r in range(d):
                    Pprev = None
                    hPprev = None
                    for j in range(NB):
                        nq = 256 if j + 1 < NB else 128
                        t0 = r + d * 128 * j
                        ktok = slice(t0, t0 + d * 127 + 1, d)
                        qtok = slice(t0, t0 + d * (nq - 1) + 1, d)
                        bs = rot["S"] % 2
                        rot["S"] += 1
                        hqk = None
                        for h_ in range(2):
                            hqk = op("pe", lambda e, h_=h_, b=b, bs=bs, ktok=ktok, qtok=qtok, nq=nq: e.matmul(
                                B.bank(2 * bs + h_)[:, 0:nq], lhsT=kt[b][64 * h_:64 * h_ + 64, ktok],
                                rhs=qt[b][64 * h_:64 * h_ + 64, qtok], start=True, stop=True),
                                [hin, bfree[bs]] if h_ == 0 else [], track=(h_ == 1))
                        pi = rot["P"] % 4
                        rot["P"] += 1
                        Pc = P[pi]
                        hp_ = op("act", lambda e, bs=bs, Pc=Pc, nq=nq: e.activation(
                            out=Pc[:, :, 0:nq],
                            in_=B.psum[:, 2 * bs * 512:(2 * bs + 2) * 512].rearrange("p (h q) -> p h q", h=2)[:, :, 0:nq],
                            func=AF.Exp),
                            [hqk, F(("P", pi))])
                        bfree[bs] = hp_
                        meng = "dve" if rot["m"] % 2 == 0 else "pool"
                        rot["m"] += 1
                        hmk = op(meng, lambda e, Pc=Pc, nq=nq: e.tensor_tensor(
                            out=Pc[:, :, 0:nq], in0=Pc[:, :, 0:nq], in1=mask2[:, :, 0:nq], op=ALU.mult), [hp_, hm_])
                        bo = 4 + rot["O"] % 4
                        rot["O"] += 1
                        tile_c = r * NB + j
                        hpv = None
                        for h_ in range(2):
                            oap = B.bank(bo)[:, h_ * 65:(h_ + 1) * 65]
                            if j > 0:
                                op("pe", lambda e, h_=h_, oap=oap, Pprev=Pprev, b=b, di=di, tile_c=tile_c: e.matmul(
                                    oap, lhsT=Pprev[:, h_, 128:256], rhs=vv[b][di][:, tile_c - 1, h_ * 65:(h_ + 1) * 65],
                                    start=True, stop=False), [hPprev, hmk, bfree[bo]] if h_ == 0 else [], track=False)
                            hpv = op("pe", lambda e, h_=h_, oap=oap, Pc=Pc, b=b, di=di, tile_c=tile_c, j=j: e.matmul(
                                oap, lhsT=Pc[:, h_, 0:128], rhs=vv[b][di][:, tile_c, h_ * 65:(h_ + 1) * 65],
                                start=(j == 0), stop=True), [hmk, bfree[bo]] if (h_ == 0 and j == 0) else [],
                                track=(h_ == 1))
                        if Pprev is not None:
                            fr_[("P", Pprev_i)] = hpv
                        if j == NB - 1:
                            fr_[("P", pi)] = hpv
                        Pprev, hPprev, Pprev_i = Pc, hmk, pi
                        hev = op("act", lambda e, bo=bo, os_i=os_i, tile_c=tile_c: e.activation(
                            out=ost[os_i][:, tile_c, :], in_=B.bank(bo)[:, 0:130], func=AF.Copy),
                            [hpv, F(("ost", os_i))] if True else [])
                        bfree[bo] = hev
                        hev_all = [hev]
                        last_reads = [hpv]
                dst = ODv[di, :, c0:c0 + 130].rearrange("(j i r) c -> i r j c", i=128, r=d)
                hod = None
                for r in range(d):
                    for jq in range(0, NB, 8):
                        je = min(NB, jq + 8)
                        hod = dma("sync", dst[:, r, jq:je, :], ost[os_i][:, r * NB + jq:r * NB + je, :], deps=hev_all,
                                  dsem=ds_od[os_i])
                fr_[("ost", os_i)] = hod
            fr_[("in", b)] = last_reads
        out_handles.append((ds_od[0][0], ds_od[0][1]))
        out_handles.append((ds_od[1][0], ds_od[1][1]))
        B.release(mB)

    if "B" in phases:
        phase_B()


    def phase_C1():
        B.release(persist_mark)
        B.phase_barrier(out_handles)
        Wo = B.alloc([16, 1024], BF16)
        attnw_sb = B.alloc([1024], F32)
        od = [B.alloc([3, 1040], F32) for _ in range(2)]
        ys_ = [B.alloc([1024], BF16) for _ in range(2)]
        rden = [B.alloc([16], F32) for _ in range(2)]
        of = [B.alloc([16, 64], F32) for _ in range(2)]
        ya = [B.alloc([1024], BF16) for _ in range(2)]
        ssq = [B.alloc([2], F32) for _ in range(2)]
        sqj = B.alloc([1024], BF16)
        yT = [B.alloc([16, 512], BF16) for _ in range(2)]
        xt = B.alloc([8, 512], F32)
        sq = [B.alloc([512], BF16) for _ in range(2)]
        rstd = B.alloc([512], F32)
        tmp = [B.alloc([512], F32) for _ in range(2)]
        h2 = [B.alloc([8, 512], BF16) for _ in range(2)]
        ds_w = B.dsem("c1w")
        hw = None
        for kc in range(16):
            hw = dma("pool", Wo[:, kc, :], w_out[kc * 128:(kc + 1) * 128, :], dsem=ds_w)
        ds_aw = B.dsem("c1aw")
        haw = dma("sync", attnw_sb, attnw, dsem=ds_aw)
        ds_l = [B.dsem("c1l0"), B.dsem("c1l1")]
        ds_x = B.dsem("c1x")
        ds_o1 = B.dsem("c1o1")
        ds_o2 = [B.dsem("c1o2a"), B.dsem("c1o2b")]
        fr_ = {}
        bfree = {b: None for b in range(8)}
        rot = {"tp": 0, "mb": 0}

        def F(k):
            return fr_.get(k)

        sidx = 0
        for i in range(NT):
            b = i % 2
            c0t = i * 512
            hx = None
            for kc in range(8):
                hx = dma("sync", xt[:, kc, :], xT[kc * 128:(kc + 1) * 128, c0t:c0t + 512], deps=[F("xt")], dsem=ds_x)
            hyT = []
            for st in range(4):
                s_ = sidx % 2
                sidx += 1
                r0 = c0t + st * 128
                hl = dma("sync", od[s_], OD[:, r0:r0 + 128, :].rearrange("d t c -> t d c"), deps=[F(("od", s_))], dsem=ds_l[s_])
                hl = dma("sync", ys_[s_], YS[r0:r0 + 128, :], deps=[F(("ys", s_))], dsem=ds_l[s_])
                o0 = od[s_][:, 0, :]
                h = op("dve", lambda e, s_=s_, o0=o0: e.tensor_add(out=o0, in0=o0, in1=od[s_][:, 1, :]), [hl])
                h = op("dve", lambda e, s_=s_, o0=o0: e.tensor_add(out=o0, in0=o0, in1=od[s_][:, 2, :]), [h])
                o3 = o0.rearrange("p (h d) -> p h d", h=16)
                h = op("dve", lambda e, s_=s_, o3=o3: e.reciprocal(out=rden[s_], in_=o3[:, :, 64]), [h, F(("rden", s_))])
                h = op("dve", lambda e, s_=s_, o3=o3: e.tensor_tensor(
                    out=of[s_], in0=o3[:, :, 0:64], in1=rden[s_].unsqueeze(2).to_broadcast([128, 16, 64]), op=ALU.mult),
                    [h, F(("of", s_))])
                fr_[("od", s_)] = h
                off = of[s_].rearrange("p h d -> p (h d)")
                ha = op("act", lambda e, s_=s_, off=off: e.activation(out=sqj, in_=off, func=AF.Square,
                                                                     accum_out=ssq[s_][:, 0:1]), [h, F(("ssq", s_)), F("sqj")])
                fr_["sqj"] = ha
                ha = op("act", lambda e, s_=s_: e.activation(out=ssq[s_][:, 1:2], in_=ssq[s_][:, 0:1], func=AF.Ln,
                                                             scale=1.0 / 1024, bias=EPS), [ha])
                ha = op("act", lambda e, s_=s_: e.activation(out=ssq[s_][:, 1:2], in_=ssq[s_][:, 1:2], func=AF.Exp, scale=-0.5), [ha])
                hya = op("dve", lambda e, s_=s_, off=off: e.scalar_tensor_tensor(
                    out=ya[s_], in0=off, scalar=ssq[s_][:, 1:2], in1=attnw_sb, op0=ALU.mult, op1=ALU.mult),
                    [ha, haw, F(("ya", s_))])
                fr_[("ssq", s_)] = hya
                fr_[("of", s_)] = hya
                fr_[("rden", s_)] = hya
                hlast = None
                for q4 in range(4):
                    tpi = rot["tp"] % 2
                    rot["tp"] += 1
                    tp = B.bank(6 + tpi, 1024, BF16)[:, 0:512]
                    hp = None
                    for k4 in range(4):
                        kc = q4 * 4 + k4
                        src = ys_[s_][:, kc * 128:(kc + 1) * 128] if kc < 8 else ya[s_][:, (kc - 8) * 128:(kc - 7) * 128]
                        hp = op("pe", lambda e, k4=k4, tp=tp, src=src: e.transpose(
                            out=tp[:, k4 * 128:(k4 + 1) * 128], in_=src, identity=ident),
                            [hl, hya, h_id, F(("tp", tpi))] if k4 == 0 else [], track=(k4 == 3))
                    he = op("dve", lambda e, tp=tp, b=b, q4=q4, st=st: e.tensor_copy(
                        out=yT[b][:, q4 * 4:q4 * 4 + 4, st * 128:(st + 1) * 128],
                        in_=tp.rearrange("p (k t) -> p k t", k=4)), [hp, F(("yT", b))])
                    fr_[("tp", tpi)] = he
                    hlast = hp
                    hyT.append(he)
                fr_[("ys", s_)] = hlast
                fr_[("ya", s_)] = hlast
            hx1 = None
            hmm_last = None
            for f in range(8):
                bk = 2 + rot["mb"] % 4
                rot["mb"] += 1
                hm = None
                for kc in range(16):
                    hm = op("pe", lambda e, kc=kc, f=f, bk=bk, b=b: e.matmul(
                        B.bank(bk), lhsT=Wo[:, kc, f * 128:(f + 1) * 128], rhs=yT[b][:, kc, :],
                        start=(kc == 0), stop=(kc == 15)), (hyT + [hw, bfree[bk]]) if kc == 0 else [], track=(kc == 15))
                hx1 = op("dve", lambda e, f=f, bk=bk: e.scalar_tensor_tensor(
                    out=xt[:, f, :], in0=B.bank(bk), scalar=modT[:, 16 + f:17 + f], in1=xt[:, f, :],
                    op0=ALU.mult, op1=ALU.add), [hm, hx, h_mod])
                bfree[bk] = hx1
                hmm_last = hm
            fr_[("yT", b)] = hmm_last
            hst1 = None
            for kc in range(8):
                hst1 = dma("sync", X1T[kc * 128:(kc + 1) * 128, c0t:c0t + 512], xt[:, kc, :], deps=[hx1], dsem=ds_o1)
            hst = None
            for kc in range(8):
                sl = kc % 2
                hs = op("pool", lambda e, kc=kc, sl=sl: e.tensor_tensor(out=sq[sl], in0=xt[:, kc, :], in1=xt[:, kc, :],
                                                                      op=ALU.mult), [hx1, F(("sq", sl))])
                hst = op("pe", lambda e, kc=kc, sl=sl: e.matmul(B.bank(0), lhsT=ones_bf, rhs=sq[sl],
                                                               start=(kc == 0), stop=(kc == 7)),
                         [hs, h_ones, bfree[0] if kc == 0 else None])
                fr_[("sq", sl)] = hst
            h = op("act", lambda e: e.activation(out=rstd, in_=B.bank(0), func=AF.Ln, scale=1.0 / D, bias=EPS),
                   [hst, F("rstd")])
            bfree[0] = h
            h = op("act", lambda e: e.activation(out=rstd, in_=rstd, func=AF.Exp, scale=-0.5), [h])
            hh = None
            hn = None
            for kc in range(8):
                sl = kc % 2
                hn = op("dve", lambda e, kc=kc, sl=sl: e.tensor_mul(out=tmp[sl], in0=xt[:, kc, :], in1=rstd),
                        [h, hx1, F(("tmp", sl))])
                hh = op("pool", lambda e, kc=kc, b=b, sl=sl: e.tensor_scalar(
                    out=h2[b][:, kc, :], in0=tmp[sl], scalar1=s2[:, kc:kc + 1], scalar2=modT[:, 24 + kc:25 + kc],
                    op0=ALU.mult, op1=ALU.add), [hn, h_mod, F(("h2", b))])
                fr_[("tmp", sl)] = hh
            fr_["rstd"] = hn
            hd = None
            for kc in range(8):
                hd = dma("sync", H2T[kc * 128:(kc + 1) * 128, c0t:c0t + 512], h2[b][:, kc, :], deps=[hh], dsem=ds_o2[b])
            fr_[("h2", b)] = hd
            fr_["xt"] = [hst1, hn, hst]
        out_handles.append((ds_o1[0], ds_o1[1]))
        out_handles.append((ds_o2[0][0], ds_o2[0][1]))
        out_handles.append((ds_o2[1][0], ds_o2[1][1]))

    def phase_C2():
        B.release(persist_mark)
        B.phase_barrier(out_handles)
        W1 = B.alloc([8, 4096], BF16)
        W2 = B.alloc([32, 1024], BF16)
        TN = 256
        h2 = [B.alloc([8, TN], BF16) for _ in range(2)]
        u = B.alloc([32, TN], BF16)
        rr = [B.alloc([TN], F32) for _ in range(3)]
        x1f = [B.alloc([TN], F32) for _ in range(3)]
        oo = [B.alloc([TN], F32) for _ in range(3)]
        ds_w1 = B.dsem("c2w1")
        ds_w2 = B.dsem("c2w2")
        hw1 = None
        for kc in range(8):
            for hf in range(2):
                hw1 = dma("pool", W1[:, kc, hf * 2048:(hf + 1) * 2048],
                          w_ff1[kc * 128:(kc + 1) * 128, hf * 2048:(hf + 1) * 2048], dsem=ds_w1)
        hw2 = None
        for kc in range(32):
            hw2 = dma("pool", W2[:, kc, :], w_ff2[kc * 128:(kc + 1) * 128, :], dsem=ds_w2)
        ds_h = [B.dsem("c2h0"), B.dsem("c2h1")]
        ds_x = [B.dsem("c2x0"), B.dsem("c2x1"), B.dsem("c2x2")]
        ds_o = [B.dsem("c2o0"), B.dsem("c2o1"), B.dsem("c2o2")]
        fr_ = {}
        bfree = {b: None for b in range(8)}
        rot = {"mb": 0, "r": 0, "x": 0}

        def F(k):
            return fr_.get(k)

        for i in range(S // TN):
            b = i % 2
            c0 = i * TN
            hh = None
            for kc in range(8):
                hh = dma("sync", h2[b][:, kc, :], H2T[kc * 128:(kc + 1) * 128, c0:c0 + TN], deps=[F(("h2", b))], dsem=ds_h[b])
            hu_all = []
            hm = None
            for m in range(32):
                bk = rot["mb"] % 6
                rot["mb"] += 1
                for kc in range(8):
                    hm = op("pe", lambda e, kc=kc, m=m, bk=bk, b=b: e.matmul(
                        B.bank(bk, TN), lhsT=W1[:, kc, m * 128:(m + 1) * 128], rhs=h2[b][:, kc, :],
                        start=(kc == 0), stop=(kc == 7)), [hh, hw1, bfree[bk]] if kc == 0 else [], track=(kc == 7))
                ri = rot["r"] % 3
                rot["r"] += 1
                hr = op("act", lambda e, bk=bk, ri=ri: e.activation(out=rr[ri], in_=B.bank(bk, TN), func=AF.Relu),
                        [hm, F(("rr", ri))])
                bfree[bk] = hr
                hu = op("pool", lambda e, ri=ri, m=m: e.tensor_tensor(out=u[:, m, :], in0=rr[ri], in1=rr[ri], op=ALU.mult),
                        [hr, F("u")])
                fr_[("rr", ri)] = hu
                hu_all.append(hu)
            fr_[("h2", b)] = hm
            hm2 = None
            for f in range(8):
                bk = 6 + f % 2
                xi = rot["x"] % 3
                rot["x"] += 1
                hxl = dma("sync", x1f[xi], X1T[f * 128:(f + 1) * 128, c0:c0 + TN], deps=[F(("x1f", xi))], dsem=ds_x[xi])
                for kc in range(32):
                    hm2 = op("pe", lambda e, kc=kc, f=f, bk=bk: e.matmul(
                        B.bank(bk, TN), lhsT=W2[:, kc, f * 128:(f + 1) * 128], rhs=u[:, kc, :],
                        start=(kc == 0), stop=(kc == 31)), (hu_all + [hw2, bfree[bk]]) if kc == 0 else [], track=(kc == 31))
                ho = op("dve", lambda e, f=f, bk=bk, xi=xi: e.scalar_tensor_tensor(
                    out=oo[xi], in0=B.bank(bk, TN), scalar=modT[:, 40 + f:41 + f], in1=x1f[xi],
                    op0=ALU.mult, op1=ALU.add), [hm2, hxl, h_mod, F(("oo", xi))])
                bfree[bk] = ho
                fr_[("x1f", xi)] = ho
                hd = dma("sync", outT[f * 128:(f + 1) * 128, c0:c0 + TN], oo[xi], deps=[ho], dsem=ds_o[xi])
                fr_[("oo", xi)] = hd
            fr_["u"] = hm2
        for k in range(3):
            out_handles.append((ds_o[k][0], ds_o[k][1]))

    if "C" in phases:
        phase_C1()
        phase_C2()

    B.final_wait("sync", out_handles + B.barrier_handles())
    B.finish()
    return nc


def _host_consts():
    c = np.zeros((128, 6, 128), np.float32)
    c[:, 0, :] = np.eye(128)
    c[:, 1, :] = 1.0
    bd = np.zeros((128, 128), np.float32)
    bd[:64, :64] = 1.0 / 64
    bd[64:, 64:] = 1.0 / 64
    c[:, 2, :] = bd
    j = np.arange(128)
    c[:, 3, :] = (j[:, None] <= j[None, :]).astype(np.float32)
    c[:, 4, :] = (j[:, None] > j[None, :]).astype(np.float32)
    c[:, 5, :] = (j[:, None] >= j[None, :]).astype(np.float32)
    return c


def make_in_maps(inp):
    f = lambda a: np.ascontiguousarray(np.asarray(a, dtype=np.float32))
    x = f(inp["x"])
    c = f(inp["c"])
    shared = {
        "w_ada": f(inp["w_ada"][0]),
        "b_adaT": f(inp["b_ada"][0].reshape(48, 128).T),
        "n1w": f(inp["norm1_w"][0].reshape(8, 128).T),
        "n2w": f(inp["norm2_w"][0].reshape(8, 128).T),
        "w_in": f(inp["w_in"][0]),
        "convw": f(np.asarray(inp["conv_w"][0]).T.reshape(16, 128, 4).transpose(1, 0, 2)),
        "convb": f(np.asarray(inp["conv_b"][0]).reshape(16, 128).T),
        "dtb": f(np.broadcast_to(np.asarray(inp["dt_bias"][0])[None, :], (128, 16))),
        "alog": f(np.broadcast_to(np.asarray(inp["a_log"][0])[None, :], (128, 16))),
        "dskip": f(np.broadcast_to(np.asarray(inp["d_skip"][0])[None, :], (128, 16))),
        "ssdnw": f(np.broadcast_to(np.asarray(inp["ssd_norm_w"][0])[None, :], (128, 1024))),
        "qkw": f(np.stack([np.tile(np.asarray(inp["q_norm_w"][0]), 2),
                           np.tile(np.asarray(inp["k_norm_w"][0]), 2)], axis=1)),
        "attnw": f(np.broadcast_to(np.asarray(inp["attn_norm_w"][0])[None, :], (128, 1024))),
        "w_out": f(inp["w_out"][0]),
        "w_ff1": f(inp["w_ff1"][0]),
        "w_ff2": f(inp["w_ff2"][0]),
        "cmat": _host_consts(),
    }
    maps = []
    for b in range(8):
        m = dict(shared)
        m["xT"] = f(x[b].T)
        m["cT"] = f(c[b].reshape(8, 128).T)
        maps.append(m)
    return maps


_NC_CACHE = {}


def kernel(**inputs):
    key = (PHASES, DEBUG)
    if key not in _NC_CACHE:
        _NC_CACHE[key] = build_program(PHASES, DEBUG)
    nc = _NC_CACHE[key]
    in_maps = make_in_maps(inputs)
    res = run_bass_kernel_spmd(nc, in_maps, core_ids=list(range(8)))
    out = np.stack([np.ascontiguousarray(r["outT"].T) for r in res.results], axis=0)
    return out.astype(np.float32)
```

```python
import numpy as np
from contextlib import ExitStack
import concourse.bass as bass
import concourse.mybir as mybir
from concourse.bass_utils import run_bass_kernel_spmd

F32 = mybir.dt.float32
BF16 = mybir.dt.bfloat16
U8 = mybir.dt.uint8
AF = mybir.ActivationFunctionType
ALU = mybir.AluOpType
AX = mybir.AxisListType

S = 4096
D = 1024
NT = 8
TT = 512
INW = 6160
EPS = 1e-6
DSIZE = {F32: 4, BF16: 2, U8: 1}

PHASES = "MASBC"
DEBUG = False
GROUPS = "dqxzv"
NTILES = 8


class Builder:
    def __init__(self, nc):
        self.nc = nc
        self.es = ExitStack()
        self.engs = ["sync", "act", "dve", "pool", "pe"]
        self.q = {n: [] for n in self.engs}
        self.cnt = {n: 0 for n in self.engs}
        self.waited = {n: {} for n in self.engs}
        self.sem = {n: self.es.enter_context(nc.semaphore("prog_" + n)) for n in self.engs}
        self.arena = self.es.enter_context(nc.sbuf_tensor("arena", [128, 204 * 1024], U8))
        self.psum = self.es.enter_context(nc.psum_tensor("psum", [128, 4096], F32))
        self.aoff = 0
        self.nsem = 0

    def alloc(self, free_shape, dtype):
        n = int(np.prod(free_shape)) * DSIZE[dtype]
        off = (self.aoff + 63) // 64 * 64
        assert off + n <= 204 * 1024, ("SBUF arena overflow", off, n)
        self.aoff = off + n
        ap = self.arena[:, off:off + n].bitcast(dtype)
        if len(free_shape) == 2:
            ap = ap.rearrange("p (a b) -> p a b", a=free_shape[0])
        elif len(free_shape) == 3:
            ap = ap.rearrange("p (a b c) -> p a b c", a=free_shape[0], b=free_shape[1])
        elif len(free_shape) == 4:
            ap = ap.rearrange("p (a b c d) -> p a b c d", a=free_shape[0], b=free_shape[1], c=free_shape[2])
        return ap

    def mark(self):
        return self.aoff

    def release(self, m):
        self.aoff = m

    def bank(self, b, n=512, dtype=F32):
        ap = self.psum[:, b * 512:(b + 1) * 512]
        if dtype == BF16:
            return ap.bitcast(BF16)[:, 0:n]
        return ap[:, 0:n]

    def new_sem(self, name):
        self.nsem += 1
        return self.es.enter_context(self.nc.semaphore(f"{name}_{self.nsem}"))

    def _waits(self, eng, deps):
        waits = []
        for d in deps:
            if d is None:
                continue
            if isinstance(d, list):
                for dd in d:
                    waits += self._waits(eng, [dd])
                continue
            s, v = d
            key = s.num
            if self.waited[eng].get(key, 0) < v:
                self.waited[eng][key] = v
                waits.append((s, v))
        return waits

    def op(self, eng, fn, deps=(), track=True):
        waits = self._waits(eng, deps)
        h = None
        sem = self.sem[eng]
        if track:
            self.cnt[eng] += 1
            h = (sem, self.cnt[eng])

        def run(e, fn=fn, waits=waits, track=track, sem=sem):
            for s, v in waits:
                e.wait_ge(s, v)
            ins = fn(e)
            if track:
                ins.then_inc(sem, 1)
        self.q[eng].append(run)
        return h

    def dma(self, eng, out, in_, deps=(), dsem=None, **kw):
        waits = self._waits(eng, deps)
        dsem[1] += 16
        h = (dsem[0], dsem[1])
        sem = dsem[0]

        def run(e, waits=waits, sem=sem, out=out, in_=in_, kw=kw):
            for s, v in waits:
                e.wait_ge(s, v)
            e.dma_start(out=out, in_=in_, **kw).then_inc(sem, 16)
        self.q[eng].append(run)
        return h

    def dsem(self, name):
        return [self.new_sem(name), 0]

    def final_wait(self, eng, deps):
        waits = self._waits(eng, deps)

        def run(e, waits=waits):
            for s, v in waits:
                e.wait_ge(s, v)
        self.q[eng].append(run)

    def phase_barrier(self, extra=()):
        hs = self.barrier_handles() + list(extra)
        for n in self.engs:
            self.final_wait(n, hs)

    def barrier_handles(self):
        return [(self.sem[n], self.cnt[n]) for n in ["act", "dve", "pool", "pe"] if self.cnt[n] > 0]

    def finish(self):
        nc = self.nc
        with nc.Block() as block:
            @block.sync
            def _(e):
                for f in self.q["sync"]:
                    f(e)

            @block.scalar
            def _(e):
                for f in self.q["act"]:
                    f(e)

            @block.vector
            def _(e):
                for f in self.q["dve"]:
                    f(e)

            @block.gpsimd
            def _(e):
                for f in self.q["pool"]:
                    f(e)

            @block.tensor
            def _(e):
                for f in self.q["pe"]:
                    f(e)
        self.es.close()


def build_program(phases=PHASES, debug=DEBUG):
    nc = bass.Bass("TRN2", target_bir_lowering=False)
    dr = {}

    def din(name, shape, dt=F32):
        dr[name] = nc.dram_tensor(name, list(shape), dt, kind="ExternalInput").ap()
        return dr[name]

    def dscr(name, shape, dt):
        kind = "ExternalOutput" if (debug and name in debug) else "Internal"
        dr[name] = nc.dram_tensor(name, list(shape), dt, kind=kind).ap()
        return dr[name]

    xT = din("xT", [D, S])
    cT = din("cT", [128, 8])
    w_ada = din("w_ada", [D, 6 * D])
    b_adaT = din("b_adaT", [128, 48])
    n1w = din("n1w", [128, 8])
    n2w = din("n2w", [128, 8])
    w_in = din("w_in", [D, INW])
    convw = din("convw", [128, 16, 4])
    convb = din("convb", [128, 16])
    dtb = din("dtb", [128, 16])
    alog = din("alog", [128, 16])
    dskip = din("dskip", [128, 16])
    ssdnw = din("ssdnw", [128, 1024])
    qkw = din("qkw", [128, 2])
    attnw = din("attnw", [128, 1024])
    w_out = din("w_out", [2 * D, D])
    w_ff1 = din("w_ff1", [D, 4 * D])
    w_ff2 = din("w_ff2", [4 * D, D])
    cmat = din("cmat", [128, 6, 128])
    outT = nc.dram_tensor("outT", [D, S], F32, kind="ExternalOutput").ap()

    ZS = dscr("ZS", [S, 1024], BF16)
    XS = dscr("XS", [S, 1024], BF16)
    BM = dscr("BM", [S, 512], BF16)
    BCT = dscr("BCT", [1024, S], BF16)
    QT = dscr("QT", [1024, S], BF16)
    KT = dscr("KT", [1024, S], BF16)
    VA = dscr("VA", [S, 16 * 65], BF16)
    DTS = dscr("DTS", [S, 32], F32)
    YS = dscr("YS", [S, 1024], BF16)
    OD = dscr("OD", [3, S, 16 * 65], F32)
    X1T = dscr("X1T", [D, S], F32)
    H2T = dscr("H2T", [D, S], BF16)
    MODT = dscr("MODT", [128, 48], F32)

    B = Builder(nc)
    op, dma = B.op, B.dma

    modT = B.alloc([48], F32)
    s1 = B.alloc([8], F32)
    s2 = B.alloc([8], F32)
    ident = B.alloc([128], BF16)
    ones_bf = B.alloc([128], BF16)
    blk64 = B.alloc([128], BF16)
    ones_f = B.alloc([128], F32)
    cst_f = B.alloc([6, 128], F32)
    convw_sb = B.alloc([16, 4], F32)
    convb_sb = B.alloc([16], F32)
    qkw_sb = B.alloc([2], F32)
    dtb_sb = B.alloc([16], F32)
    A_sb = B.alloc([16], F32)
    dskip_sb = B.alloc([16], F32)
    n1w_sb = B.alloc([8], F32)
    n2w_sb = B.alloc([8], F32)
    persist_mark = B.mark()

    ds_c = B.dsem("const")
    hc = []
    hc.append(dma("sync", cst_f, cmat, dsem=ds_c))
    hc.append(dma("sync", convw_sb, convw, dsem=ds_c))
    hc.append(dma("sync", convb_sb, convb, dsem=ds_c))
    hc.append(dma("sync", qkw_sb, qkw, dsem=ds_c))
    hc.append(dma("sync", dtb_sb, dtb, dsem=ds_c))
    hc.append(dma("sync", A_sb, alog, dsem=ds_c))
    hc.append(dma("sync", dskip_sb, dskip, dsem=ds_c))
    hc.append(dma("sync", n1w_sb, n1w, dsem=ds_c))
    hc.append(dma("sync", n2w_sb, n2w, dsem=ds_c))
    hconst = hc[-1]

    h_id = op("dve", lambda e: e.tensor_copy(out=ident, in_=cst_f[:, 0, :]), [hconst])
    h_ones = op("dve", lambda e: e.tensor_copy(out=ones_bf, in_=cst_f[:, 1, :]), [hconst])
    h_blk = op("dve", lambda e: e.tensor_copy(out=blk64, in_=cst_f[:, 2, :]), [hconst])
    h_onesf = op("dve", lambda e: e.tensor_copy(out=ones_f, in_=cst_f[:, 1, :]), [hconst])
    h_A = op("act", lambda e: e.activation(out=A_sb, in_=A_sb, func=AF.Exp), [hconst])
    h_A = op("dve", lambda e: e.tensor_scalar_mul(out=A_sb, in0=A_sb, scalar1=-1.0), [h_A])
    h_qw = op("dve", lambda e: e.tensor_scalar_mul(out=qkw_sb[:, 0:1], in0=qkw_sb[:, 0:1], scalar1=0.125), [hconst])
    h_setup = [h_id, h_ones, h_blk, h_onesf, h_A, h_qw]

    out_handles = []

    Wsb = B.alloc([8, INW], BF16)
    pieces = [(0, 1540), (1540, 3080), (3080, 4620), (4620, 6160)]
    hW = []
    for (c0, c1) in pieces:
        dsw = B.dsem("W")
        h = None
        for kc in range(8):
            h = dma("pool", Wsb[:, kc, c0:c1], w_in[kc * 128:(kc + 1) * 128, c0:c1], dsem=dsw)
        hW.append(h)

    def wdeps(a, b_):
        return [hW[i] for i, (c0, c1) in enumerate(pieces) if a < c1 and b_ > c0]

    w_mark = B.mark()

    if "M" in phases:
        m0 = B.mark()
        c_sb = B.alloc([8], F32)
        cact = B.alloc([8], F32)
        bada = B.alloc([48], F32)
        wslM = [B.alloc([8, 1024], F32) for _ in range(2)]
        accM = [B.alloc([1024], F32) for _ in range(2)]
        ds_m = B.dsem("m_in")
        hcin = dma("sync", c_sb, cT, dsem=ds_m)
        hcin = dma("sync", bada, b_adaT, dsem=ds_m)
        h = op("act", lambda e: e.activation(out=cact, in_=c_sb, func=AF.Exp, scale=-1.0), [hcin])
        h = op("dve", lambda e: e.tensor_scalar_add(out=cact, in0=cact, scalar1=1.0), [h])
        h = op("dve", lambda e: e.reciprocal(out=cact, in_=cact), [h])
        hcact = op("dve", lambda e: e.tensor_mul(out=cact, in0=cact, in1=c_sb), [h])
        ds_w = [B.dsem("m_w0"), B.dsem("m_w1")]
        wfreeM = [None, None]
        accfreeM = [None, None]
        psM = B.bank(0, 48)
        hmm = None
        for sl in range(6):
            b = sl % 2
            hw = None
            for kc in range(8):
                hw = dma("sync", wslM[b][:, kc, :], w_ada[kc * 128:(kc + 1) * 128, sl * 1024:(sl + 1) * 1024],
                         deps=[wfreeM[b]], dsem=ds_w[b])
            eng = "dve"
            h = op(eng, lambda e, b=b: e.tensor_scalar_mul(out=accM[b], in0=wslM[b][:, 0, :], scalar1=cact[:, 0:1]),
                   [hw, hcact, accfreeM[b]])
            for kc in range(1, 8):
                h = op(eng, lambda e, b=b, kc=kc: e.scalar_tensor_tensor(
                    out=accM[b], in0=wslM[b][:, kc, :], scalar=cact[:, kc:kc + 1], in1=accM[b],
                    op0=ALU.mult, op1=ALU.add), [h])
            wfreeM[b] = h
            for jt in range(8):
                col = sl * 8 + jt
                hmm = op("pe", lambda e, b=b, jt=jt, col=col: e.matmul(
                    psM[:, col:col + 1], lhsT=accM[b][:, jt * 128:(jt + 1) * 128], rhs=ones_f[:, 0:1],
                    start=True, stop=True), [h, h_onesf])
            accfreeM[b] = hmm
        hmod = op("dve", lambda e: e.tensor_add(out=modT, in0=psM, in1=bada), [hmm, hcin])
        h = op("dve", lambda e: e.scalar_tensor_tensor(out=s1, in0=modT[:, 8:16], scalar=1.0, in1=n1w_sb,
                                                      op0=ALU.add, op1=ALU.mult), [hmod, hconst])
        hmod2 = op("dve", lambda e: e.scalar_tensor_tensor(out=s2, in0=modT[:, 32:40], scalar=1.0, in1=n2w_sb,
                                                          op0=ALU.add, op1=ALU.mult), [h])
        ds_mo = B.dsem("m_out")
        out_handles.append(dma("sync", MODT, modT, deps=[hmod2], dsem=ds_mo))
        h_mod = hmod2
        B.release(m0)
    else:
        h_mod = None


    if "A" in phases:
        mA = B.mark()
        B.phase_barrier(out_handles)
        xt = B.alloc([8, 512], F32)
        sq = [B.alloc([512], BF16) for _ in range(2)]
        h1 = [B.alloc([8, 512], BF16) for _ in range(2)]
        rstd = B.alloc([512], F32)
        xr = [B.alloc([515], F32) for _ in range(3)]
        acc = [B.alloc([512], F32) for _ in range(3)]
        xc = [B.alloc([512], BF16) for _ in range(5)]
        halo = B.alloc([16, 3], F32)
        qsq = [B.alloc([512], BF16) for _ in range(2)]
        qraw = [B.alloc([512], F32) for _ in range(2)]
        qrs = [B.alloc([512], F32) for _ in range(2)]
        qn = [B.alloc([512], BF16) for _ in range(3)]
        zs = [B.alloc([512], BF16) for _ in range(3)]
        VAst = B.alloc([4, 16, 65], BF16)
        XSst = B.alloc([4, 1024], BF16)
        BMst = B.alloc([4, 512], BF16)
        tdt = B.alloc([4, 16], F32)
        adt = B.alloc([4, 16], F32)
        dts = B.alloc([4, 32], F32)

        h_va1 = op("pool", lambda e: e.memset(VAst, 1.0))
        h_halo0 = op("pool", lambda e: e.memset(halo, 0.0))
        ds_x = B.dsem("x")
        dsd = {}

        def sd(key):
            if key not in dsd:
                dsd[key] = B.dsem("o")
            return dsd[key]
        bank_free = {b: None for b in range(8)}
        free = {}

        def fr(name):
            return free.get(name)

        state = {"xt_free": None, "h1": [None, None], "h1_free": [None, None]}
        phaseA_out = []

        def prep(i):
            b = i % 2
            hx = None
            for kc in range(8):
                hx = dma("sync", xt[:, kc, :], xT[kc * 128:(kc + 1) * 128, i * 512:(i + 1) * 512],
                         deps=[state["xt_free"]], dsem=ds_x)
            hst = None
            for kc in range(8):
                sl = kc % 2
                hs = op("pool", lambda e, kc=kc, sl=sl: e.tensor_tensor(out=sq[sl], in0=xt[:, kc, :], in1=xt[:, kc, :],
                                                                      op=ALU.mult), [hx, fr(("sq", sl))])
                hst = op("pe", lambda e, kc=kc, sl=sl: e.matmul(B.bank(0), lhsT=ones_bf, rhs=sq[sl],
                                                               start=(kc == 0), stop=(kc == 7)),
                         [hs, h_ones, bank_free[0] if kc == 0 else None])
                free[("sq", sl)] = hst
            h = op("act", lambda e: e.activation(out=rstd, in_=B.bank(0), func=AF.Ln, scale=1.0 / D, bias=EPS),
                   [hst, fr("rstd")])
            bank_free[0] = h
            h = op("act", lambda e: e.activation(out=rstd, in_=rstd, func=AF.Exp, scale=-0.5), [h])
            hh = None
            for kc in range(8):
                hn = op("dve", lambda e, kc=kc: e.tensor_mul(out=xt[:, kc, :], in0=xt[:, kc, :], in1=rstd), [h, hst])
                hh = op("pool", lambda e, kc=kc, b=b: e.tensor_scalar(
                    out=h1[b][:, kc, :], in0=xt[:, kc, :], scalar1=s1[:, kc:kc + 1], scalar2=modT[:, kc:kc + 1],
                    op0=ALU.mult, op1=ALU.add), [hn, h_mod, state["h1_free"][b]])
            free["rstd"] = hn
            state["xt_free"] = hh
            state["h1"][b] = hh

        rot = {"main": 0, "qk": 0, "x3": 0, "qn": 0, "zs": 0, "tp": 0}
        MAIN_BANKS = [2, 3, 4, 5]

        def next_bank():
            b = MAIN_BANKS[rot["main"] % len(MAIN_BANKS)]
            rot["main"] += 1
            return b

        def mm_group_f(i, col0):
            b = i % 2
            bk = next_bank()
            h = None
            for kc in range(8):
                h = op("pe", lambda e, kc=kc, bk=bk, b=b: e.matmul(
                    B.bank(bk), lhsT=Wsb[:, kc, col0:col0 + 128], rhs=h1[b][:, kc, :],
                    start=(kc == 0), stop=(kc == 7)),
                    ([state["h1"][b], bank_free[bk]] + wdeps(col0, col0 + 128)) if kc == 0 else [],
                    track=(kc == 7))
            return bk, h

        def mm_group_t(i, st, col0, ncols):
            b = i % 2
            bk = next_bank()
            h = None
            for kc in range(8):
                h = op("pe", lambda e, kc=kc, bk=bk, b=b: e.matmul(
                    B.bank(bk, ncols), lhsT=h1[b][:, kc, st * 128:(st + 1) * 128], rhs=Wsb[:, kc, col0:col0 + ncols],
                    start=(kc == 0), stop=(kc == 7)),
                    ([state["h1"][b], bank_free[bk]] + wdeps(col0, col0 + ncols)) if kc == 0 else [],
                    track=(kc == 7))
            return bk, h

        for i in range(NTILES):
            if i == 0:
                prep(0)
            b = i % 2
            c0t = i * 512
            for st in (range(4) if "d" in GROUPS else []):
                bk, hm = mm_group_t(i, st, 3072, 16)
                h = op("dve", lambda e, bk=bk, st=st: e.tensor_add(out=tdt[:, st, :], in0=B.bank(bk, 16), in1=dtb_sb),
                       [hm, hconst, fr("dts")])
                bank_free[bk] = h
                h2 = op("act", lambda e, st=st: e.activation(out=adt[:, st, :], in_=tdt[:, st, :], func=AF.Abs), [h])
                h2 = op("act", lambda e, st=st: e.activation(out=adt[:, st, :], in_=adt[:, st, :], func=AF.Exp, scale=-1.0), [h2])
                h2 = op("act", lambda e, st=st: e.activation(out=adt[:, st, :], in_=adt[:, st, :], func=AF.Ln, bias=1.0), [h2])
                h2 = op("dve", lambda e, st=st: e.scalar_tensor_tensor(out=dts[:, st, 0:16], in0=tdt[:, st, :], scalar=0.0,
                                                                      in1=adt[:, st, :], op0=ALU.max, op1=ALU.add), [h2])
                h2 = op("dve", lambda e, st=st: e.tensor_mul(out=dts[:, st, 16:32], in0=dts[:, st, 0:16], in1=A_sb), [h2, h_A])
            if "d" in GROUPS:
                free["dts"] = dma("sync", DTS[c0t:c0t + 512, :].rearrange("(st p) c -> p st c", p=128), dts, deps=[h2], dsem=sd("dts"))
            pend_qk = []
            for j in (range(16) if "q" in GROUPS else range(4)):
                if "q" not in GROUPS:
                    if j == 3 and i + 1 < NTILES:
                        prep(i + 1)
                    continue
                isq = j < 8
                col0 = (3088 if isq else 4112) + (j % 8) * 128
                bk, hm = mm_group_f(i, col0)
                s2_ = rot["qk"] % 2
                rot["qk"] += 1
                hs = op("act", lambda e, bk=bk, s2_=s2_: e.activation(out=qsq[s2_], in_=B.bank(bk), func=AF.Square),
                        [hm, fr(("qsq", s2_))])
                hr = op("act", lambda e, bk=bk, s2_=s2_: e.activation(out=qraw[s2_], in_=B.bank(bk), func=AF.Copy),
                        [hm, hs, fr(("qraw", s2_))])
                bank_free[bk] = [hs, hr]

                def qk_part2(s2_=s2_, hs=hs, hr=hr, isq=isq, j=j):
                    hp = op("pe", lambda e: e.matmul(B.bank(1), lhsT=blk64, rhs=qsq[s2_], start=True, stop=True),
                            [hs, h_blk, bank_free[1]])
                    free[("qsq", s2_)] = hp
                    hl = op("act", lambda e: e.activation(out=qrs[s2_], in_=B.bank(1), func=AF.Ln, bias=EPS),
                            [hp, fr(("qrs", s2_))])
                    bank_free[1] = hl
                    hl = op("act", lambda e: e.activation(out=qrs[s2_], in_=qrs[s2_], func=AF.Exp, scale=-0.5), [hl])
                    s3 = rot["qn"] % 3
                    rot["qn"] += 1
                    wc = 0 if isq else 1
                    hq = op("dve", lambda e: e.scalar_tensor_tensor(
                        out=qn[s3], in0=qraw[s2_], scalar=qkw_sb[:, wc:wc + 1], in1=qrs[s2_], op0=ALU.mult, op1=ALU.mult),
                        [hl, hr, h_qw, fr(("qn", s3))])
                    free[("qraw", s2_)] = hq
                    free[("qrs", s2_)] = hq
                    dst = (QT if isq else KT)[(j % 8) * 128:(j % 8 + 1) * 128, c0t:c0t + 512]
                    free[("qn", s3)] = dma("sync", dst, qn[s3], deps=[hq], dsem=sd(("qn", s3)))
                if pend_qk:
                    pend_qk.pop(0)()
                pend_qk.append(qk_part2)
                if j == 3 and i + 1 < NTILES:
                    prep(i + 1)
            while pend_qk:
                pend_qk.pop(0)()
            pend_x = []
            for ct in (range(16) if "x" in GROUPS else []):
                col0 = 1024 + ct * 128
                bk, hm = mm_group_f(i, col0)
                s3 = rot["x3"] % 3
                rot["x3"] += 1
                ha = op("act", lambda e, bk=bk, s3=s3, ct=ct: e.activation(
                    out=acc[s3], in_=B.bank(bk), func=AF.Identity, scale=convw_sb[:, ct, 3:4], bias=convb_sb[:, ct:ct + 1]),
                    [hm, hconst, fr(("acc", s3))])
                hc_ = op("act", lambda e, bk=bk, s3=s3: e.activation(out=xr[s3][:, 3:515], in_=B.bank(bk), func=AF.Copy),
                         [hm, fr(("xr", s3))])
                bank_free[bk] = hc_
                hh1 = op("pool", lambda e, s3=s3, ct=ct: e.tensor_copy(out=xr[s3][:, 0:3], in_=halo[:, ct, :]),
                         [fr(("xr", s3)), h_halo0, fr(("halo", ct))])
                hh2 = op("pool", lambda e, s3=s3, ct=ct: e.tensor_copy(out=halo[:, ct, :], in_=xr[s3][:, 512:515]), [hc_, hh1])
                free[("halo", ct)] = hh2
                ht = ha
                for k in range(3):
                    ht = op("dve", lambda e, s3=s3, ct=ct, k=k: e.scalar_tensor_tensor(
                        out=acc[s3], in0=xr[s3][:, k:k + 512], scalar=convw_sb[:, ct, k:k + 1], in1=acc[s3],
                        op0=ALU.mult, op1=ALU.add), [ht, hc_, hh1])
                free[("xr", s3)] = [ht, hh2]
                hx_ = op("act", lambda e, s3=s3: e.activation(out=xc[s3], in_=acc[s3], func=AF.Silu),
                         [ht, fr(("xc", s3))])
                free[("acc", s3)] = hx_
                def x_part2(ct=ct, s3=s3, hx_=hx_):
                    last = [hx_]
                    if ct < 12:
                        tpi = rot["tp"] % 2
                        rot["tp"] += 1
                        tp = B.bank(6 + tpi, 1024, BF16)[:, 0:512]
                        hp = None
                        for st in range(4):
                            hp = op("pe", lambda e, st=st: e.transpose(
                                out=tp[:, st * 128:(st + 1) * 128], in_=xc[s3][:, st * 128:(st + 1) * 128], identity=ident),
                                [hx_, h_id, fr(("tp", tpi))] if st == 0 else [], track=(st == 3))
                        if ct < 8:
                            dst_ap = XSst[:, :, ct * 128:(ct + 1) * 128]
                            fkey = "XSst"
                        else:
                            dst_ap = BMst[:, :, (ct - 8) * 128:(ct - 7) * 128]
                            fkey = "BMst"
                        he = op("dve", lambda e: e.tensor_copy(
                            out=dst_ap, in_=tp.rearrange("p (s c) -> p s c", s=4)), [hp, fr(fkey)])
                        free[("tp", tpi)] = he
                        last.append(hp)
                        free.setdefault("st_writes", []).append(he)
                    if ct >= 8:
                        hd = dma("sync", BCT[(ct - 8) * 128:(ct - 7) * 128, c0t:c0t + 512], xc[s3], deps=[hx_],
                                 dsem=sd(("xc", s3)))
                        last.append(hd)
                    free[("xc", s3)] = last
                pend_x.append(x_part2)
                if len(pend_x) > 2:
                    pend_x.pop(0)()
            while pend_x:
                pend_x.pop(0)()
            if "x" not in GROUPS:
                free["st_writes"] = []
            hd = dma("sync", XS[c0t:c0t + 512, :].rearrange("(st p) c -> p st c", p=128), XSst,
                     deps=free["st_writes"], dsem=sd("XSst"))
            free["XSst"] = hd
            phaseA_out.append(hd)
            hd = dma("sync", BM[c0t:c0t + 512, :].rearrange("(st p) c -> p st c", p=128), BMst,
                     deps=free["st_writes"], dsem=sd("BMst"))
            free["BMst"] = hd
            phaseA_out.append(hd)
            free["st_writes"] = []
            for st in (range(4) if "z" in GROUPS else []):
                for half in range(2):
                    bk, hm = mm_group_t(i, st, half * 512, 512)
                    s3 = rot["zs"] % 3
                    rot["zs"] += 1
                    hz = op("act", lambda e, bk=bk, s3=s3: e.activation(out=zs[s3], in_=B.bank(bk), func=AF.Silu),
                            [hm, fr(("zs", s3))])
                    bank_free[bk] = hz
                    r0 = c0t + st * 128
                    free[("zs", s3)] = dma("sync", ZS[r0:r0 + 128, half * 512:(half + 1) * 512], zs[s3], deps=[hz], dsem=sd(("zs", s3)))
                    phaseA_out.append(free[("zs", s3)])
            hv = []
            for st in (range(4) if "v" in GROUPS else []):
                for half in range(2):
                    bk, hm = mm_group_t(i, st, 5136 + half * 512, 512)
                    h = op("dve", lambda e, bk=bk, st=st, half=half: e.tensor_copy(
                        out=VAst[:, st, half * 8:(half + 1) * 8, 0:64],
                        in_=B.bank(bk).rearrange("p (h d) -> p h d", h=8)), [hm, h_va1, fr("VAst")])
                    bank_free[bk] = h
                    hv.append(h)
            free["VAst"] = dma("sync", VA[c0t:c0t + 512, :].rearrange("(st p) c -> p st c", p=128),
                               VAst.rearrange("p s h d -> p s (h d)"), deps=hv, dsem=sd("VAst"))
            phaseA_out.append(free["VAst"])
            state["h1_free"][b] = (B.sem["pe"], B.cnt["pe"])
        phaseA_done = [(v[0], v[1]) for v in dsd.values()]
        out_handles += phaseA_done
        B.release(mA)


    def phase_S():
        B.release(persist_mark)
        mS = B.mark()
        B.phase_barrier(out_handles)
        NCH = 32
        tri_f = cst_f[:, 3, :]
        strict_f = cst_f[:, 4, :]
        tri_b = B.alloc([128], BF16)
        ssdnw_sb = B.alloc([1024], F32)
        dts_all = B.alloc([NCH, 32], F32)
        xs_ = [B.alloc([16, 64], BF16) for _ in range(2)]
        bm_ = [B.alloc([512], BF16) for _ in range(2)]
        bct_ = [B.alloc([8, 128], BF16) for _ in range(2)]
        zs_ = [B.alloc([1024], BF16) for _ in range(2)]
        R = B.alloc([16, 128], F32)
        Lt = [B.alloc([16, 128], BF16) for _ in range(2)]
        cbm = [B.alloc([4, 128], BF16) for _ in range(2)]
        G = [B.alloc([16, 128], BF16) for _ in range(2)]
        xdt = [B.alloc([16, 64], BF16) for _ in range(2)]
        xdec = [B.alloc([16, 64], BF16) for _ in range(2)]
        t1 = [B.alloc([16, 64], F32) for _ in range(2)]
        yo = [B.alloc([1024], BF16) for _ in range(2)]
        sqj = B.alloc([256], BF16)
        ed = [B.alloc([32], F32) for _ in range(2)]
        ss = [B.alloc([4], F32) for _ in range(2)]
        Sst = B.alloc([16, 64], F32)
        Sb = B.alloc([16, 64], BF16)

        ds0 = B.dsem("s_c")
        hc0 = dma("sync", ssdnw_sb, ssdnw, dsem=ds0)
        hc0 = dma("sync", dts_all, DTS.rearrange("(c p) k -> p c k", p=128), dsem=ds0)
        h_trib = op("dve", lambda e: e.tensor_copy(out=tri_b, in_=tri_f))
        h_s0 = op("pool", lambda e: e.memset(Sst, 0.0))
        h_sb0 = op("pool", lambda e: e.memset(Sb, 0.0))
        ds_l = [B.dsem("s_l0"), B.dsem("s_l1")]
        ds_y = [B.dsem("s_y0"), B.dsem("s_y1")]
        fr_ = {}
        bfree = {b: None for b in range(8)}

        def F(k):
            return fr_.get(k)

        hS = h_s0
        hSb = h_sb0
        for c in range(NCH):
            b = c % 2
            r0 = c * 128
            hl = dma("sync", xs_[b].rearrange("p h d -> p (h d)"), XS[r0:r0 + 128, :], deps=[F(("ld", b))], dsem=ds_l[b])
            hl = dma("sync", bm_[b], BM[r0:r0 + 128, :], dsem=ds_l[b])
            hl = dma("sync", bct_[b], BCT[:, r0:r0 + 128].rearrange("(g p) t -> p g t", p=128), dsem=ds_l[b])
            hl = dma("sync", zs_[b], ZS[r0:r0 + 128, :], dsem=ds_l[b])
            dA = dts_all[:, c, 16:32]
            dtc = dts_all[:, c, 0:16]
            hR = op("pool", lambda e, dA=dA: e.tensor_tensor(
                out=R, in0=tri_f.unsqueeze(1).to_broadcast([128, 16, 128]),
                in1=dA.unsqueeze(2).to_broadcast([128, 16, 128]), op=ALU.mult), [hc0, hconst, F("R")])
            hseg = []
            for q4 in range(4):
                hseg.append(op("pe", lambda e, q4=q4: e.matmul(
                    B.bank(q4), lhsT=strict_f, rhs=R[:, q4 * 4:(q4 + 1) * 4, :].rearrange("p h l -> p (h l)"),
                    start=True, stop=True), [hR, bfree[q4]]))
            fr_["R"] = hseg[-1]
            hsm = op("pe", lambda e, dA=dA: e.matmul(B.bank(5, 16), lhsT=tri_f, rhs=dA, start=True, stop=True),
                     [hc0, bfree[5]], track=False)
            hsm = op("pe", lambda e, dA=dA: e.matmul(B.bank(5, 32)[:, 16:32], lhsT=ones_f, rhs=dA, start=True, stop=True),
                     [h_onesf])
            hL = None
            for q4 in range(4):
                hL = op("act", lambda e, q4=q4, b=b: e.activation(
                    out=Lt[b][:, q4 * 4:(q4 + 1) * 4, :].rearrange("p h l -> p (h l)"), in_=B.bank(q4), func=AF.Exp),
                    [hseg[q4], F(("Lt", b))])
                bfree[q4] = hL
            hed = op("act", lambda e, b=b: e.activation(out=ed[b], in_=B.bank(5, 32), func=AF.Exp), [hsm, F(("ed", b))])
            bfree[5] = hed
            hcb = None
            for g in range(4):
                hcb = op("pe", lambda e, g=g, b=b: e.matmul(
                    B.bank(4)[:, g * 128:(g + 1) * 128], lhsT=bct_[b][:, g, :], rhs=bct_[b][:, 4 + g, :],
                    start=True, stop=True), [hl, bfree[4]] if g == 0 else [], track=(g == 3))
            hcbm = op("dve", lambda e, b=b: e.tensor_tensor(
                out=cbm[b], in0=B.bank(4).rearrange("p (g l) -> p g l", g=4),
                in1=tri_b.unsqueeze(1).to_broadcast([128, 4, 128]), op=ALU.mult), [hcb, h_trib, F(("cbm", b))])
            bfree[4] = hcbm
            hG = None
            for g in range(4):
                hG = op("dve", lambda e, g=g, b=b: e.tensor_tensor(
                    out=G[b][:, 4 * g:4 * g + 4, :], in0=Lt[b][:, 4 * g:4 * g + 4, :],
                    in1=cbm[b][:, g, :].unsqueeze(1).to_broadcast([128, 4, 128]), op=ALU.mult),
                    [hL, hcbm, F(("G", b))])
            hxdt = op("pool", lambda e, b=b, dtc=dtc: e.tensor_tensor(
                out=xdt[b], in0=xs_[b], in1=dtc.unsqueeze(2).to_broadcast([128, 16, 64]), op=ALU.mult),
                [hl, hc0, F(("xdt", b))])
            hxdec = op("pool", lambda e, b=b: e.tensor_tensor(
                out=xdec[b], in0=xdt[b], in1=Lt[b][:, :, 127:128].to_broadcast([128, 16, 64]), op=ALU.mult),
                [hxdt, hL, F(("xdec", b))])
            hyd = None
            for h_ in range(16):
                hyd = op("pe", lambda e, h_=h_, b=b: e.matmul(
                    B.psum[:, 3072 + h_ * 64:3072 + (h_ + 1) * 64], lhsT=G[b][:, h_, :], rhs=xdt[b][:, h_, :],
                    start=True, stop=True), [hG, hxdt, bfree[6], bfree[7]] if h_ == 0 else [], track=(h_ == 15))
            fr_[("G", b)] = hyd
            hyo = None
            for g in range(4):
                hyo = op("pe", lambda e, g=g, b=b: e.matmul(
                    B.psum[:, g * 256:(g + 1) * 256], lhsT=bct_[b][:, 4 + g, :],
                    rhs=Sb[:, 4 * g:4 * g + 4, :].rearrange("p h d -> p (h d)"), start=True, stop=True),
                    [hSb, bfree[0], bfree[1]] if g == 0 else [], track=(g == 3))
            hst = None
            for g in range(4):
                hst = op("pe", lambda e, g=g, b=b: e.matmul(
                    B.psum[:, 1024 + g * 256:1024 + (g + 1) * 256], lhsT=bm_[b][:, g * 128:(g + 1) * 128],
                    rhs=xdec[b][:, 4 * g:4 * g + 4, :].rearrange("p h d -> p (h d)"), start=True, stop=True),
                    [hxdec, bfree[2], bfree[3]] if g == 0 else [], track=(g == 3))
            fr_[("xdec", b)] = hst
            fr_[("xdt", b)] = [hyd, hxdec]
            fr_[("Lt", b)] = [hG, hxdec]
            fr_[("cbm", b)] = hG
            yoff = B.psum[:, 0:1024].rearrange("p (h d) -> p h d", h=16)
            ydg = B.psum[:, 3072:4096].rearrange("p (h d) -> p h d", h=16)
            h1_ = op("dve", lambda e, b=b: e.tensor_tensor(
                out=t1[b], in0=yoff, in1=ed[b][:, 0:16].unsqueeze(2).to_broadcast([128, 16, 64]), op=ALU.mult),
                [hyo, hed, F(("t1", b))])
            bfree[0] = h1_
            bfree[1] = h1_
            h2_ = op("dve", lambda e, b=b: e.tensor_tensor(out=t1[b], in0=t1[b], in1=ydg, op=ALU.add), [h1_, hyd])
            bfree[6] = h2_
            bfree[7] = h2_
            h3_ = op("dve", lambda e, b=b: e.tensor_tensor(
                out=yo[b].rearrange("p (h d) -> p h d", h=16), in0=xs_[b],
                in1=dskip_sb.unsqueeze(2).to_broadcast([128, 16, 64]), op=ALU.mult), [hl, hconst, F(("yo", b))])
            h3_ = op("dve", lambda e, b=b: e.tensor_tensor(
                out=t1[b], in0=t1[b], in1=yo[b].rearrange("p (h d) -> p h d", h=16), op=ALU.add), [h2_, h3_])
            h4_ = op("dve", lambda e, b=b: e.tensor_tensor(
                out=t1[b], in0=t1[b], in1=zs_[b].rearrange("p (h d) -> p h d", h=16), op=ALU.mult), [h3_, hl])
            t1f = t1[b].rearrange("p h d -> p (h d)")
            hq_ = None
            for g in range(4):
                hq_ = op("act", lambda e, g=g, b=b, t1f=t1f: e.activation(
                    out=sqj, in_=t1f[:, g * 256:(g + 1) * 256], func=AF.Square, accum_out=ss[b][:, g:g + 1]),
                    [h4_, hq_, F(("ss", b))])
            hq_ = op("act", lambda e, b=b: e.activation(out=ss[b], in_=ss[b], func=AF.Ln, scale=1.0 / 256, bias=EPS), [hq_])
            hq_ = op("act", lambda e, b=b: e.activation(out=ss[b], in_=ss[b], func=AF.Exp, scale=-0.5), [hq_])
            hy = None
            for g in range(4):
                hy = op("dve", lambda e, g=g, b=b, t1f=t1f: e.scalar_tensor_tensor(
                    out=yo[b][:, g * 256:(g + 1) * 256], in0=t1f[:, g * 256:(g + 1) * 256], scalar=ss[b][:, g:g + 1],
                    in1=ssdnw_sb[:, g * 256:(g + 1) * 256], op0=ALU.mult, op1=ALU.mult), [hq_, h4_, hc0])
            fr_[("ss", b)] = hy
            fr_[("t1", b)] = hy
            hdy = dma("sync", YS[r0:r0 + 128, :], yo[b], deps=[hy], dsem=ds_y[b])
            fr_[("yo", b)] = hdy
            hu = op("dve", lambda e, b=b: e.tensor_tensor(
                out=Sst, in0=Sst, in1=ed[b][:, 16:32].unsqueeze(2).to_broadcast([128, 16, 64]), op=ALU.mult),
                [hS, hed])
            hu = op("dve", lambda e: e.tensor_tensor(
                out=Sst, in0=Sst, in1=B.psum[:, 1024:2048].rearrange("p (h d) -> p h d", h=16), op=ALU.add), [hu, hst])
            bfree[2] = hu
            bfree[3] = hu
            hS = hu
            hSb = op("dve", lambda e: e.tensor_copy(out=Sb, in_=Sst), [hu, hyo])
            fr_[("ed", b)] = [h1_, hu]
            fr_[("ld", b)] = [hy, hyd, hst, hcb, hyo, h4_, hxdt]
        out_handles.append((ds_y[0][0], ds_y[0][1]))
        out_handles.append((ds_y[1][0], ds_y[1][1]))
        B.release(mS)

    if "S" in phases:
        phase_S()


    def phase_B():
        B.release(persist_mark)
        mB = B.mark()
        B.phase_barrier(out_handles)
        mask2 = B.alloc([2, 256], BF16)
        qt = [B.alloc([4096], BF16) for _ in range(2)]
        kt = [B.alloc([4096], BF16) for _ in range(2)]
        vv = [[B.alloc([32, 130], BF16) for _ in range(3)] for _ in range(2)]
        ost = [B.alloc([32, 130], F32) for _ in range(2)]
        P = [B.alloc([2, 256], BF16) for _ in range(4)]
        hm_ = None
        for h_ in range(2):
            hm_ = op("dve", lambda e, h_=h_: e.tensor_copy(out=mask2[:, h_, 0:128], in_=cst_f[:, 3, :]), [hconst])
            hm_ = op("dve", lambda e, h_=h_: e.tensor_copy(out=mask2[:, h_, 128:256], in_=cst_f[:, 5, :]), [hconst])
        ds_in = [B.dsem("b_in0"), B.dsem("b_in1")]
        ds_od = [B.dsem("b_od0"), B.dsem("b_od1")]
        VAv = VA
        ODv = OD
        fr_ = {}
        bfree = {b: None for b in range(8)}
        rot = {"S": 0, "O": 0, "P": 0, "ost": 0, "m": 0}
        DIL = [(1, 32), (4, 8), (16, 2)]

        def F(k):
            return fr_.get(k)

        for hp in range(8):
            b = hp % 2
            c0 = hp * 130
            hin = dma("sync", qt[b], QT[hp * 128:(hp + 1) * 128, :], deps=[F(("in", b))], dsem=ds_in[b])
            hin = dma("sync", kt[b], KT[hp * 128:(hp + 1) * 128, :], dsem=ds_in[b])
            for jq in range(4):
                hin = dma("sync", vv[b][0][:, jq * 8:(jq + 1) * 8, :],
                          VAv[jq * 1024:(jq + 1) * 1024, c0:c0 + 130].rearrange("(j i) c -> i j c", i=128), dsem=ds_in[b])
            for r in range(4):
                hin = dma("sync", vv[b][1][:, r * 8:(r + 1) * 8, :],
                          VAv[:, c0:c0 + 130].rearrange("(j i r) c -> i r j c", i=128, r=4)[:, r, :, :], dsem=ds_in[b])
            for r in range(16):
                hin = dma("sync", vv[b][2][:, r * 2:(r + 1) * 2, :],
                          VAv[:, c0:c0 + 130].rearrange("(j i r) c -> i r j c", i=128, r=16)[:, r, :, :], dsem=ds_in[b])
            last_reads = []
            steps = []
            for di, (d, NB) in enumerate(DIL):
                os_i = rot["ost"] % 2
                rot["ost"] += 1
                for r in range(d):
                    for j in range(NB):
                        steps.append(dict(di=di, d=d, NB=NB, r=r, j=j, os_i=os_i, last=(r == d - 1 and j == NB - 1)))

            def emit_qk(st_, b=b):
                d, NB, r, j = st_["d"], st_["NB"], st_["r"], st_["j"]
                nq = 256 if j + 1 < NB else 128
                t0 = r + d * 128 * j
                ktok = slice(t0, t0 + d * 127 + 1, d)
                qtok = slice(t0, t0 + d * (nq - 1) + 1, d)
                bs = rot["S"] % 2
                rot["S"] += 1
                hqk = None
                for h_ in range(2):
                    hqk = op("pe", lambda e, h_=h_: e.matmul(
                        B.bank(2 * bs + h_)[:, 0:nq], lhsT=kt[b][64 * h_:64 * h_ + 64, ktok],
                        rhs=qt[b][64 * h_:64 * h_ + 64, qtok], start=True, stop=True),
                        [hin, bfree[2 * bs], bfree[2 * bs + 1]] if h_ == 0 else [], track=(h_ == 1))
                pi = rot["P"] % 4
                rot["P"] += 1
                Pc = P[pi]
                hp_ = op("act", lambda e: e.activation(
                    out=Pc[:, :, 0:nq],
                    in_=B.psum[:, 2 * bs * 512:(2 * bs + 2) * 512].rearrange("p (h q) -> p h q", h=2)[:, :, 0:nq],
                    func=AF.Exp), [hqk, F(("P", pi))])
                bfree[2 * bs] = hp_
                bfree[2 * bs + 1] = hp_
                meng = "dve" if rot["m"] % 2 == 0 else "pool"
                rot["m"] += 1
                hmk = op(meng, lambda e: e.tensor_tensor(
                    out=Pc[:, :, 0:nq], in0=Pc[:, :, 0:nq], in1=mask2[:, :, 0:nq], op=ALU.mult), [hp_, hm_])
                return dict(Pc=Pc, pi=pi, hmk=hmk)

            def emit_pv(st_, cur, prev, b=b, c0=c0):
                di, d, NB, r, j, os_i = st_["di"], st_["d"], st_["NB"], st_["r"], st_["j"], st_["os_i"]
                bo = 4 + rot["O"] % 4
                rot["O"] += 1
                tile_c = r * NB + j
                Pc = cur["Pc"]
                hpv = None
                for h_ in range(2):
                    oap = B.bank(bo)[:, h_ * 65:(h_ + 1) * 65]
                    if j > 0:
                        Pp = prev["Pc"]
                        op("pe", lambda e, h_=h_, oap=oap, Pp=Pp: e.matmul(
                            oap, lhsT=Pp[:, h_, 128:256], rhs=vv[b][di][:, tile_c - 1, h_ * 65:(h_ + 1) * 65],
                            start=True, stop=False), [prev["hmk"], cur["hmk"], bfree[bo]] if h_ == 0 else [], track=False)
                    hpv = op("pe", lambda e, h_=h_, oap=oap: e.matmul(
                        oap, lhsT=Pc[:, h_, 0:128], rhs=vv[b][di][:, tile_c, h_ * 65:(h_ + 1) * 65],
                        start=(j == 0), stop=True), [cur["hmk"], bfree[bo]] if (h_ == 0 and j == 0) else [],
                        track=(h_ == 1))
                if j > 0:
                    fr_[("P", prev["pi"])] = hpv
                fr_[("P", cur["pi"])] = hpv
                hev = op("dve", lambda e: e.tensor_copy(out=ost[os_i][:, tile_c, :], in_=B.bank(bo)[:, 0:130]),
                         [hpv, F(("ost", os_i))])
                bfree[bo] = hev
                last_reads[:] = [hpv]
                if st_["last"]:
                    dst = ODv[di, :, c0:c0 + 130].rearrange("(j i r) c -> i r j c", i=128, r=d)
                    hod = None
                    for r2 in range(d):
                        for jq in range(0, NB, 8):
                            je = min(NB, jq + 8)
                            hod = dma("sync", dst[:, r2, jq:je, :], ost[os_i][:, r2 * NB + jq:r2 * NB + je, :], deps=[hev],
                                      dsem=ds_od[os_i])
                    fr_[("ost", os_i)] = hod

            nst = len(steps)
            info = [None] * nst
            for t in range(nst + 1):
                if t < nst:
                    info[t] = emit_qk(steps[t])
                if t >= 1:
                    emit_pv(steps[t - 1], info[t - 1], info[t - 2] if steps[t - 1]["j"] > 0 else None)
            fr_[("in", b)] = last_reads
        out_handles.append((ds_od[0][0], ds_od[0][1]))
        out_handles.append((ds_od[1][0], ds_od[1][1]))
        B.release(mB)

    if "B" in phases:
        phase_B()


    def phase_C1():
        B.release(persist_mark)
        B.phase_barrier(out_handles)
        Wo = B.alloc([16, 1024], BF16)
        attnw_sb = B.alloc([1024], F32)
        od = [B.alloc([3, 1040], F32) for _ in range(2)]
        ys_ = [B.alloc([1024], BF16) for _ in range(2)]
        rden = [B.alloc([16], F32) for _ in range(2)]
        of = [B.alloc([16, 64], F32) for _ in range(2)]
        ya = [B.alloc([1024], BF16) for _ in range(2)]
        ssq = [B.alloc([2], F32) for _ in range(2)]
        sqj = B.alloc([1024], BF16)
        yT = [B.alloc([16, 512], BF16) for _ in range(2)]
        xt = B.alloc([8, 512], F32)
        sq = [B.alloc([512], BF16) for _ in range(2)]
        rstd = B.alloc([512], F32)
        tmp = [B.alloc([512], F32) for _ in range(2)]
        h2 = [B.alloc([8, 512], BF16) for _ in range(2)]
        ds_w = B.dsem("c1w")
        hw = None
        for kc in range(16):
            hw = dma("pool", Wo[:, kc, :], w_out[kc * 128:(kc + 1) * 128, :], dsem=ds_w)
        ds_aw = B.dsem("c1aw")
        haw = dma("sync", attnw_sb, attnw, dsem=ds_aw)
        ds_l = [B.dsem("c1l0"), B.dsem("c1l1")]
        ds_x = B.dsem("c1x")
        ds_o1 = B.dsem("c1o1")
        ds_o2 = [B.dsem("c1o2a"), B.dsem("c1o2b")]
        fr_ = {}
        bfree = {b: None for b in range(8)}
        rot = {"tp": 0, "mb": 0}

        def F(k):
            return fr_.get(k)

        sidx = 0
        for i in range(NT):
            b = i % 2
            c0t = i * 512
            hx = None
            for kc in range(8):
                hx = dma("sync", xt[:, kc, :], xT[kc * 128:(kc + 1) * 128, c0t:c0t + 512], deps=[F("xt")], dsem=ds_x)
            hyT = []
            for st in range(4):
                s_ = sidx % 2
                sidx += 1
                r0 = c0t + st * 128
                hl = dma("sync", od[s_], OD[:, r0:r0 + 128, :].rearrange("d t c -> t d c"), deps=[F(("od", s_))], dsem=ds_l[s_])
                hl = dma("sync", ys_[s_], YS[r0:r0 + 128, :], deps=[F(("ys", s_))], dsem=ds_l[s_])
                o0 = od[s_][:, 0, :]
                h = op("dve", lambda e, s_=s_, o0=o0: e.tensor_add(out=o0, in0=o0, in1=od[s_][:, 1, :]), [hl])
                h = op("dve", lambda e, s_=s_, o0=o0: e.tensor_add(out=o0, in0=o0, in1=od[s_][:, 2, :]), [h])
                o3 = o0.rearrange("p (h d) -> p h d", h=16)
                h = op("dve", lambda e, s_=s_, o3=o3: e.reciprocal(out=rden[s_], in_=o3[:, :, 64]), [h, F(("rden", s_))])
                h = op("dve", lambda e, s_=s_, o3=o3: e.tensor_tensor(
                    out=of[s_], in0=o3[:, :, 0:64], in1=rden[s_].unsqueeze(2).to_broadcast([128, 16, 64]), op=ALU.mult),
                    [h, F(("of", s_))])
                fr_[("od", s_)] = h
                off = of[s_].rearrange("p h d -> p (h d)")
                ha = op("act", lambda e, s_=s_, off=off: e.activation(out=sqj, in_=off, func=AF.Square,
                                                                     accum_out=ssq[s_][:, 0:1]), [h, F(("ssq", s_)), F("sqj")])
                fr_["sqj"] = ha
                ha = op("act", lambda e, s_=s_: e.activation(out=ssq[s_][:, 1:2], in_=ssq[s_][:, 0:1], func=AF.Ln,
                                                             scale=1.0 / 1024, bias=EPS), [ha])
                ha = op("act", lambda e, s_=s_: e.activation(out=ssq[s_][:, 1:2], in_=ssq[s_][:, 1:2], func=AF.Exp, scale=-0.5), [ha])
                hya = op("dve", lambda e, s_=s_, off=off: e.scalar_tensor_tensor(
                    out=ya[s_], in0=off, scalar=ssq[s_][:, 1:2], in1=attnw_sb, op0=ALU.mult, op1=ALU.mult),
                    [ha, haw, F(("ya", s_))])
                fr_[("ssq", s_)] = hya
                fr_[("of", s_)] = hya
                fr_[("rden", s_)] = hya
                hlast = None
                for q4 in range(4):
                    tpi = rot["tp"] % 2
                    rot["tp"] += 1
                    tp = B.bank(6 + tpi, 1024, BF16)[:, 0:512]
                    hp = None
                    for k4 in range(4):
                        kc = q4 * 4 + k4
                        src = ys_[s_][:, kc * 128:(kc + 1) * 128] if kc < 8 else ya[s_][:, (kc - 8) * 128:(kc - 7) * 128]
                        hp = op("pe", lambda e, k4=k4, tp=tp, src=src: e.transpose(
                            out=tp[:, k4 * 128:(k4 + 1) * 128], in_=src, identity=ident),
                            [hl, hya, h_id, F(("tp", tpi))] if k4 == 0 else [], track=(k4 == 3))
                    he = op("dve", lambda e, tp=tp, b=b, q4=q4, st=st: e.tensor_copy(
                        out=yT[b][:, q4 * 4:q4 * 4 + 4, st * 128:(st + 1) * 128],
                        in_=tp.rearrange("p (k t) -> p k t", k=4)), [hp, F(("yT", b))])
                    fr_[("tp", tpi)] = he
                    hlast = hp
                    hyT.append(he)
                fr_[("ys", s_)] = hlast
                fr_[("ya", s_)] = hlast
            hx1 = None
            hmm_last = None
            for f in range(8):
                bk = 2 + rot["mb"] % 4
                rot["mb"] += 1
                hm = None
                for kc in range(16):
                    hm = op("pe", lambda e, kc=kc, f=f, bk=bk, b=b: e.matmul(
                        B.bank(bk), lhsT=Wo[:, kc, f * 128:(f + 1) * 128], rhs=yT[b][:, kc, :],
                        start=(kc == 0), stop=(kc == 15)), (hyT + [hw, bfree[bk]]) if kc == 0 else [], track=(kc == 15))
                hx1 = op("dve", lambda e, f=f, bk=bk: e.scalar_tensor_tensor(
                    out=xt[:, f, :], in0=B.bank(bk), scalar=modT[:, 16 + f:17 + f], in1=xt[:, f, :],
                    op0=ALU.mult, op1=ALU.add), [hm, hx, h_mod])
                bfree[bk] = hx1
                hmm_last = hm
            fr_[("yT", b)] = hmm_last
            hst1 = None
            for kc in range(8):
                hst1 = dma("sync", X1T[kc * 128:(kc + 1) * 128, c0t:c0t + 512], xt[:, kc, :], deps=[hx1], dsem=ds_o1)
            hst = None
            for kc in range(8):
                sl = kc % 2
                hs = op("pool", lambda e, kc=kc, sl=sl: e.tensor_tensor(out=sq[sl], in0=xt[:, kc, :], in1=xt[:, kc, :],
                                                                      op=ALU.mult), [hx1, F(("sq", sl))])
                hst = op("pe", lambda e, kc=kc, sl=sl: e.matmul(B.bank(0), lhsT=ones_bf, rhs=sq[sl],
                                                               start=(kc == 0), stop=(kc == 7)),
                         [hs, h_ones, bfree[0] if kc == 0 else None])
                fr_[("sq", sl)] = hst
            h = op("act", lambda e: e.activation(out=rstd, in_=B.bank(0), func=AF.Ln, scale=1.0 / D, bias=EPS),
                   [hst, F("rstd")])
            bfree[0] = h
            h = op("act", lambda e: e.activation(out=rstd, in_=rstd, func=AF.Exp, scale=-0.5), [h])
            hh = None
            hn = None
            for kc in range(8):
                sl = kc % 2
                hn = op("dve", lambda e, kc=kc, sl=sl: e.tensor_mul(out=tmp[sl], in0=xt[:, kc, :], in1=rstd),
                        [h, hx1, F(("tmp", sl))])
                hh = op("pool", lambda e, kc=kc, b=b, sl=sl: e.tensor_scalar(
                    out=h2[b][:, kc, :], in0=tmp[sl], scalar1=s2[:, kc:kc + 1], scalar2=modT[:, 24 + kc:25 + kc],
                    op0=ALU.mult, op1=ALU.add), [hn, h_mod, F(("h2", b))])
                fr_[("tmp", sl)] = hh
            fr_["rstd"] = hn
            hd = None
            for kc in range(8):
                hd = dma("sync", H2T[kc * 128:(kc + 1) * 128, c0t:c0t + 512], h2[b][:, kc, :], deps=[hh], dsem=ds_o2[b])
            fr_[("h2", b)] = hd
            fr_["xt"] = [hst1, hn, hst]
        out_handles.append((ds_o1[0], ds_o1[1]))
        out_handles.append((ds_o2[0][0], ds_o2[0][1]))
        out_handles.append((ds_o2[1][0], ds_o2[1][1]))

    def phase_C2():
        B.release(persist_mark)
        B.phase_barrier(out_handles)
        W1 = B.alloc([8, 4096], BF16)
        W2 = B.alloc([32, 1024], BF16)
        TN = 256
        h2 = [B.alloc([8, TN], BF16) for _ in range(2)]
        u = B.alloc([32, TN], BF16)
        rr = [B.alloc([TN], F32) for _ in range(3)]
        x1f = [B.alloc([TN], F32) for _ in range(3)]
        oo = [B.alloc([TN], F32) for _ in range(3)]
        ds_w1 = B.dsem("c2w1")
        ds_w2 = B.dsem("c2w2")
        hw1 = None
        for kc in range(8):
            for hf in range(2):
                hw1 = dma("pool", W1[:, kc, hf * 2048:(hf + 1) * 2048],
                          w_ff1[kc * 128:(kc + 1) * 128, hf * 2048:(hf + 1) * 2048], dsem=ds_w1)
        hw2 = None
        for kc in range(32):
            hw2 = dma("pool", W2[:, kc, :], w_ff2[kc * 128:(kc + 1) * 128, :], dsem=ds_w2)
        ds_h = [B.dsem("c2h0"), B.dsem("c2h1")]
        ds_x = [B.dsem("c2x0"), B.dsem("c2x1"), B.dsem("c2x2")]
        ds_o = [B.dsem("c2o0"), B.dsem("c2o1"), B.dsem("c2o2")]
        fr_ = {}
        bfree = {b: None for b in range(8)}
        rot = {"mb": 0, "r": 0, "x": 0}

        def F(k):
            return fr_.get(k)

        for i in range(S // TN):
            b = i % 2
            c0 = i * TN
            hh = None
            for kc in range(8):
                hh = dma("sync", h2[b][:, kc, :], H2T[kc * 128:(kc + 1) * 128, c0:c0 + TN], deps=[F(("h2", b))], dsem=ds_h[b])
            hu_all = []
            hm = None
            for m in range(32):
                bk = rot["mb"] % 6
                rot["mb"] += 1
                for kc in range(8):
                    hm = op("pe", lambda e, kc=kc, m=m, bk=bk, b=b: e.matmul(
                        B.bank(bk, TN), lhsT=W1[:, kc, m * 128:(m + 1) * 128], rhs=h2[b][:, kc, :],
                        start=(kc == 0), stop=(kc == 7)), [hh, hw1, bfree[bk]] if kc == 0 else [], track=(kc == 7))
                ri = rot["r"] % 3
                rot["r"] += 1
                hr = op("act", lambda e, bk=bk, ri=ri: e.activation(out=rr[ri], in_=B.bank(bk, TN), func=AF.Relu),
                        [hm, F(("rr", ri))])
                bfree[bk] = hr
                hu = op("pool", lambda e, ri=ri, m=m: e.tensor_tensor(out=u[:, m, :], in0=rr[ri], in1=rr[ri], op=ALU.mult),
                        [hr, F("u")])
                fr_[("rr", ri)] = hu
                hu_all.append(hu)
            fr_[("h2", b)] = hm
            hm2 = None
            for f in range(8):
                bk = 6 + f % 2
                xi = rot["x"] % 3
                rot["x"] += 1
                hxl = dma("sync", x1f[xi], X1T[f * 128:(f + 1) * 128, c0:c0 + TN], deps=[F(("x1f", xi))], dsem=ds_x[xi])
                for kc in range(32):
                    hm2 = op("pe", lambda e, kc=kc, f=f, bk=bk: e.matmul(
                        B.bank(bk, TN), lhsT=W2[:, kc, f * 128:(f + 1) * 128], rhs=u[:, kc, :],
                        start=(kc == 0), stop=(kc == 31)), (hu_all + [hw2, bfree[bk]]) if kc == 0 else [], track=(kc == 31))
                ho = op("dve", lambda e, f=f, bk=bk, xi=xi: e.scalar_tensor_tensor(
                    out=oo[xi], in0=B.bank(bk, TN), scalar=modT[:, 40 + f:41 + f], in1=x1f[xi],
                    op0=ALU.mult, op1=ALU.add), [hm2, hxl, h_mod, F(("oo", xi))])
                bfree[bk] = ho
                fr_[("x1f", xi)] = ho
                hd = dma("sync", outT[f * 128:(f + 1) * 128, c0:c0 + TN], oo[xi], deps=[ho], dsem=ds_o[xi])
                fr_[("oo", xi)] = hd
            fr_["u"] = hm2
        for k in range(3):
            out_handles.append((ds_o[k][0], ds_o[k][1]))

    if "C" in phases:
        phase_C1()
        phase_C2()

    B.final_wait("sync", out_handles + B.barrier_handles())
    B.finish()
    return nc


def _host_consts():
    c = np.zeros((128, 6, 128), np.float32)
    c[:, 0, :] = np.eye(128)
    c[:, 1, :] = 1.0
    bd = np.zeros((128, 128), np.float32)
    bd[:64, :64] = 1.0 / 64
    bd[64:, 64:] = 1.0 / 64
    c[:, 2, :] = bd
    j = np.arange(128)
    c[:, 3, :] = (j[:, None] <= j[None, :]).astype(np.float32)
    c[:, 4, :] = (j[:, None] > j[None, :]).astype(np.float32)
    c[:, 5, :] = (j[:, None] >= j[None, :]).astype(np.float32)
    return c


def make_in_maps(inp):
    f = lambda a: np.ascontiguousarray(np.asarray(a, dtype=np.float32))
    x = f(inp["x"])
    c = f(inp["c"])
    shared = {
        "w_ada": f(inp["w_ada"][0]),
        "b_adaT": f(inp["b_ada"][0].reshape(48, 128).T),
        "n1w": f(inp["norm1_w"][0].reshape(8, 128).T),
        "n2w": f(inp["norm2_w"][0].reshape(8, 128).T),
        "w_in": f(inp["w_in"][0]),
        "convw": f(np.asarray(inp["conv_w"][0]).T.reshape(16, 128, 4).transpose(1, 0, 2)),
        "convb": f(np.asarray(inp["conv_b"][0]).reshape(16, 128).T),
        "dtb": f(np.broadcast_to(np.asarray(inp["dt_bias"][0])[None, :], (128, 16))),
        "alog": f(np.broadcast_to(np.asarray(inp["a_log"][0])[None, :], (128, 16))),
        "dskip": f(np.broadcast_to(np.asarray(inp["d_skip"][0])[None, :], (128, 16))),
        "ssdnw": f(np.broadcast_to(np.asarray(inp["ssd_norm_w"][0])[None, :], (128, 1024))),
        "qkw": f(np.stack([np.tile(np.asarray(inp["q_norm_w"][0]), 2),
                           np.tile(np.asarray(inp["k_norm_w"][0]), 2)], axis=1)),
        "attnw": f(np.broadcast_to(np.asarray(inp["attn_norm_w"][0])[None, :], (128, 1024))),
        "w_out": f(inp["w_out"][0]),
        "w_ff1": f(inp["w_ff1"][0]),
        "w_ff2": f(inp["w_ff2"][0]),
        "cmat": _host_consts(),
    }
    maps = []
    for b in range(8):
        m = dict(shared)
        m["xT"] = f(x[b].T)
        m["cT"] = f(c[b].reshape(8, 128).T)
        maps.append(m)
    return maps


_NC_CACHE = {}


def kernel(**inputs):
    key = (PHASES, DEBUG)
    if key not in _NC_CACHE:
        _NC_CACHE[key] = build_program(PHASES, DEBUG)
    nc = _NC_CACHE[key]
    in_maps = make_in_maps(inputs)
    res = run_bass_kernel_spmd(nc, in_maps, core_ids=list(range(8)))
    out = np.stack([np.ascontiguousarray(r["outT"].T) for r in res.results], axis=0)
    return out.astype(np.float32)
```

```python
import numpy as np
from contextlib import ExitStack
import concourse.bass as bass
import concourse.mybir as mybir
from concourse.bass_utils import run_bass_kernel_spmd

F32 = mybir.dt.float32
BF16 = mybir.dt.bfloat16
U8 = mybir.dt.uint8
AF = mybir.ActivationFunctionType
ALU = mybir.AluOpType
AX = mybir.AxisListType

S = 4096
D = 1024
NT = 8
TT = 512
INW = 6160
EPS = 1e-6
DSIZE = {F32: 4, BF16: 2, U8: 1}

PHASES = "MASBC"
DEBUG = False
GROUPS = "dqxzv"
NTILES = 8


class Builder:
    def __init__(self, nc):
        self.nc = nc
        self.es = ExitStack()
        self.engs = ["sync", "act", "dve", "pool", "pe"]
        self.q = {n: [] for n in self.engs}
        self.cnt = {n: 0 for n in self.engs}
        self.waited = {n: {} for n in self.engs}
        self.sem = {n: self.es.enter_context(nc.semaphore("prog_" + n)) for n in self.engs}
        self.arena = self.es.enter_context(nc.sbuf_tensor("arena", [128, 204 * 1024], U8))
        self.psum = self.es.enter_context(nc.psum_tensor("psum", [128, 4096], F32))
        self.aoff = 0
        self.nsem = 0

    def alloc(self, free_shape, dtype):
        n = int(np.prod(free_shape)) * DSIZE[dtype]
        off = (self.aoff + 63) // 64 * 64
        assert off + n <= 204 * 1024, ("SBUF arena overflow", off, n)
        self.aoff = off + n
        ap = self.arena[:, off:off + n].bitcast(dtype)
        if len(free_shape) == 2:
            ap = ap.rearrange("p (a b) -> p a b", a=free_shape[0])
        elif len(free_shape) == 3:
            ap = ap.rearrange("p (a b c) -> p a b c", a=free_shape[0], b=free_shape[1])
        elif len(free_shape) == 4:
            ap = ap.rearrange("p (a b c d) -> p a b c d", a=free_shape[0], b=free_shape[1], c=free_shape[2])
        return ap

    def mark(self):
        return self.aoff

    def release(self, m):
        self.aoff = m

    def bank(self, b, n=512, dtype=F32):
        ap = self.psum[:, b * 512:(b + 1) * 512]
        if dtype == BF16:
            return ap.bitcast(BF16)[:, 0:n]
        return ap[:, 0:n]

    def new_sem(self, name):
        self.nsem += 1
        return self.es.enter_context(self.nc.semaphore(f"{name}_{self.nsem}"))

    def _waits(self, eng, deps):
        waits = []
        for d in deps:
            if d is None:
                continue
            if isinstance(d, list):
                for dd in d:
                    waits += self._waits(eng, [dd])
                continue
            s, v = d
            key = s.num
            if self.waited[eng].get(key, 0) < v:
                self.waited[eng][key] = v
                waits.append((s, v))
        return waits

    def op(self, eng, fn, deps=(), track=True):
        waits = self._waits(eng, deps)
        h = None
        sem = self.sem[eng]
        if track:
            self.cnt[eng] += 1
            h = (sem, self.cnt[eng])

        def run(e, fn=fn, waits=waits, track=track, sem=sem):
            for s, v in waits:
                e.wait_ge(s, v)
            ins = fn(e)
            if track:
                ins.then_inc(sem, 1)
        self.q[eng].append(run)
        return h

    def dma(self, eng, out, in_, deps=(), dsem=None, **kw):
        waits = self._waits(eng, deps)
        dsem[1] += 16
        h = (dsem[0], dsem[1])
        sem = dsem[0]

        def run(e, waits=waits, sem=sem, out=out, in_=in_, kw=kw):
            for s, v in waits:
                e.wait_ge(s, v)
            e.dma_start(out=out, in_=in_, **kw).then_inc(sem, 16)
        self.q[eng].append(run)
        return h

    def dsem(self, name):
        return [self.new_sem(name), 0]

    def final_wait(self, eng, deps):
        waits = self._waits(eng, deps)

        def run(e, waits=waits):
            for s, v in waits:
                e.wait_ge(s, v)
        self.q[eng].append(run)

    def phase_barrier(self, extra=()):
        hs = self.barrier_handles() + list(extra)
        for n in self.engs:
            self.final_wait(n, hs)

    def barrier_handles(self):
        return [(self.sem[n], self.cnt[n]) for n in ["act", "dve", "pool", "pe"] if self.cnt[n] > 0]

    def finish(self):
        nc = self.nc
        with nc.Block() as block:
            @block.sync
            def _(e):
                for f in self.q["sync"]:
                    f(e)

            @block.scalar
            def _(e):
                for f in self.q["act"]:
                    f(e)

            @block.vector
            def _(e):
                for f in self.q["dve"]:
                    f(e)

            @block.gpsimd
            def _(e):
                for f in self.q["pool"]:
                    f(e)

            @block.tensor
            def _(e):
                for f in self.q["pe"]:
                    f(e)
        self.es.close()


def build_program(phases=PHASES, debug=DEBUG):
    nc = bass.Bass("TRN2", target_bir_lowering=False)
    dr = {}

    def din(name, shape, dt=F32):
        dr[name] = nc.dram_tensor(name, list(shape), dt, kind="ExternalInput").ap()
        return dr[name]

    def dscr(name, shape, dt):
        kind = "ExternalOutput" if (debug and name in debug) else "Internal"
        dr[name] = nc.dram_tensor(name, list(shape), dt, kind=kind).ap()
        return dr[name]

    xT = din("xT", [D, S])
    cT = din("cT", [128, 8])
    w_ada = din("w_ada", [D, 6 * D])
    b_adaT = din("b_adaT", [128, 48])
    n1w = din("n1w", [128, 8])
    n2w = din("n2w", [128, 8])
    w_in = din("w_in", [D, INW])
    convw = din("convw", [128, 16, 4])
    convb = din("convb", [128, 16])
    dtb = din("dtb", [128, 16])
    alog = din("alog", [128, 16])
    dskip = din("dskip", [128, 16])
    ssdnw = din("ssdnw", [128, 1024])
    qkw = din("qkw", [128, 2])
    attnw = din("attnw", [128, 1024])
    w_out = din("w_out", [2 * D, D])
    w_ff1 = din("w_ff1", [D, 4 * D])
    w_ff2 = din("w_ff2", [4 * D, D])
    cmat = din("cmat", [128, 6, 128])
    outT = nc.dram_tensor("outT", [D, S], F32, kind="ExternalOutput").ap()

    ZS = dscr("ZS", [S, 1024], BF16)
    XS = dscr("XS", [S, 1024], BF16)
    BM = dscr("BM", [S, 512], BF16)
    BCT = dscr("BCT", [1024, S], BF16)
    QT = dscr("QT", [1024, S], BF16)
    KT = dscr("KT", [1024, S], BF16)
    VA = dscr("VA", [S, 16 * 65], BF16)
    DTS = dscr("DTS", [S, 32], F32)
    YS = dscr("YS", [S, 1024], BF16)
    OD = dscr("OD", [3, S, 16 * 65], F32)
    X1T = dscr("X1T", [D, S], F32)
    H2T = dscr("H2T", [D, S], BF16)
    MODT = dscr("MODT", [128, 48], F32)
    WoB = dscr("WoB", [2 * D, D], BF16)
    W1B = dscr("W1B", [D, 4 * D], BF16)
    W2B = dscr("W2B", [4 * D, D], BF16)
    wcast = {"list": [], "sem": {}}

    B = Builder(nc)
    op, dma = B.op, B.dma

    modT = B.alloc([48], F32)
    s1 = B.alloc([8], F32)
    s2 = B.alloc([8], F32)
    ident = B.alloc([128], BF16)
    ones_bf = B.alloc([128], BF16)
    blk64 = B.alloc([128], BF16)
    ones_f = B.alloc([128], F32)
    cst_f = B.alloc([6, 128], F32)
    convw_sb = B.alloc([16, 4], F32)
    convb_sb = B.alloc([16], F32)
    qkw_sb = B.alloc([2], F32)
    dtb_sb = B.alloc([16], F32)
    A_sb = B.alloc([16], F32)
    dskip_sb = B.alloc([16], F32)
    n1w_sb = B.alloc([8], F32)
    n2w_sb = B.alloc([8], F32)
    persist_mark = B.mark()

    ds_c = B.dsem("const")
    hc = []
    hc.append(dma("sync", cst_f, cmat, dsem=ds_c))
    hc.append(dma("sync", convw_sb, convw, dsem=ds_c))
    hc.append(dma("sync", convb_sb, convb, dsem=ds_c))
    hc.append(dma("sync", qkw_sb, qkw, dsem=ds_c))
    hc.append(dma("sync", dtb_sb, dtb, dsem=ds_c))
    hc.append(dma("sync", A_sb, alog, dsem=ds_c))
    hc.append(dma("sync", dskip_sb, dskip, dsem=ds_c))
    hc.append(dma("sync", n1w_sb, n1w, dsem=ds_c))
    hc.append(dma("sync", n2w_sb, n2w, dsem=ds_c))
    hconst = hc[-1]

    h_id = op("dve", lambda e: e.tensor_copy(out=ident, in_=cst_f[:, 0, :]), [hconst])
    h_ones = op("dve", lambda e: e.tensor_copy(out=ones_bf, in_=cst_f[:, 1, :]), [hconst])
    h_blk = op("dve", lambda e: e.tensor_copy(out=blk64, in_=cst_f[:, 2, :]), [hconst])
    h_onesf = op("dve", lambda e: e.tensor_copy(out=ones_f, in_=cst_f[:, 1, :]), [hconst])
    h_A = op("act", lambda e: e.activation(out=A_sb, in_=A_sb, func=AF.Exp), [hconst])
    h_A = op("dve", lambda e: e.tensor_scalar_mul(out=A_sb, in0=A_sb, scalar1=-1.0), [h_A])
    h_qw = op("dve", lambda e: e.tensor_scalar_mul(out=qkw_sb[:, 0:1], in0=qkw_sb[:, 0:1], scalar1=0.125), [hconst])
    h_setup = [h_id, h_ones, h_blk, h_onesf, h_A, h_qw]

    out_handles = []

    Wsb = B.alloc([8, INW], BF16)
    pieces = [(0, 1540), (1540, 3080), (3080, 4620), (4620, 6160)]
    hW = []
    for (c0, c1) in pieces:
        dsw = B.dsem("W")
        h = None
        for kc in range(8):
            h = dma("pool", Wsb[:, kc, c0:c1], w_in[kc * 128:(kc + 1) * 128, c0:c1], dsem=dsw)
        hW.append(h)

    def wdeps(a, b_):
        return [hW[i] for i, (c0, c1) in enumerate(pieces) if a < c1 and b_ > c0]

    w_mark = B.mark()

    if "M" in phases:
        m0 = B.mark()
        c_sb = B.alloc([8], F32)
        cact = B.alloc([8], F32)
        bada = B.alloc([48], F32)
        wslM = [B.alloc([8, 1024], F32) for _ in range(2)]
        accM = [B.alloc([1024], F32) for _ in range(2)]
        ds_m = B.dsem("m_in")
        hcin = dma("sync", c_sb, cT, dsem=ds_m)
        hcin = dma("sync", bada, b_adaT, dsem=ds_m)
        h = op("act", lambda e: e.activation(out=cact, in_=c_sb, func=AF.Exp, scale=-1.0), [hcin])
        h = op("dve", lambda e: e.tensor_scalar_add(out=cact, in0=cact, scalar1=1.0), [h])
        h = op("dve", lambda e: e.reciprocal(out=cact, in_=cact), [h])
        hcact = op("dve", lambda e: e.tensor_mul(out=cact, in0=cact, in1=c_sb), [h])
        ds_w = [B.dsem("m_w0"), B.dsem("m_w1")]
        wfreeM = [None, None]
        accfreeM = [None, None]
        psM = B.bank(0, 48)
        hmm = None
        for sl in range(6):
            b = sl % 2
            hw = None
            for kc in range(8):
                hw = dma("sync", wslM[b][:, kc, :], w_ada[kc * 128:(kc + 1) * 128, sl * 1024:(sl + 1) * 1024],
                         deps=[wfreeM[b]], dsem=ds_w[b])
            eng = "dve"
            h = op(eng, lambda e, b=b: e.tensor_scalar_mul(out=accM[b], in0=wslM[b][:, 0, :], scalar1=cact[:, 0:1]),
                   [hw, hcact, accfreeM[b]])
            for kc in range(1, 8):
                h = op(eng, lambda e, b=b, kc=kc: e.scalar_tensor_tensor(
                    out=accM[b], in0=wslM[b][:, kc, :], scalar=cact[:, kc:kc + 1], in1=accM[b],
                    op0=ALU.mult, op1=ALU.add), [h])
            wfreeM[b] = h
            for jt in range(8):
                col = sl * 8 + jt
                hmm = op("pe", lambda e, b=b, jt=jt, col=col: e.matmul(
                    psM[:, col:col + 1], lhsT=accM[b][:, jt * 128:(jt + 1) * 128], rhs=ones_f[:, 0:1],
                    start=True, stop=True), [h, h_onesf])
            accfreeM[b] = hmm
        hmod = op("dve", lambda e: e.tensor_add(out=modT, in0=psM, in1=bada), [hmm, hcin])
        h = op("dve", lambda e: e.scalar_tensor_tensor(out=s1, in0=modT[:, 8:16], scalar=1.0, in1=n1w_sb,
                                                      op0=ALU.add, op1=ALU.mult), [hmod, hconst])
        hmod2 = op("dve", lambda e: e.scalar_tensor_tensor(out=s2, in0=modT[:, 32:40], scalar=1.0, in1=n2w_sb,
                                                          op0=ALU.add, op1=ALU.mult), [h])
        ds_mo = B.dsem("m_out")
        out_handles.append(dma("sync", MODT, modT, deps=[hmod2], dsem=ds_mo))
        h_mod = hmod2
        B.release(m0)
    else:
        h_mod = None


    if "A" in phases:
        mA = B.mark()
        B.phase_barrier(out_handles)
        xt = B.alloc([8, 512], F32)
        sq = [B.alloc([512], BF16) for _ in range(2)]
        h1 = [B.alloc([8, 512], BF16) for _ in range(2)]
        rstd = B.alloc([512], F32)
        xr = [B.alloc([515], F32) for _ in range(3)]
        acc = [B.alloc([512], F32) for _ in range(3)]
        xc = [B.alloc([512], BF16) for _ in range(5)]
        halo = B.alloc([16, 3], F32)
        qsq = [B.alloc([512], BF16) for _ in range(2)]
        qraw = [B.alloc([512], F32) for _ in range(2)]
        qrs = [B.alloc([512], F32) for _ in range(2)]
        qn = [B.alloc([512], BF16) for _ in range(3)]
        zs = [B.alloc([512], BF16) for _ in range(3)]
        VAst = B.alloc([4, 16, 65], BF16)
        XSst = B.alloc([4, 1024], BF16)
        BMst = B.alloc([4, 512], BF16)
        tdt = B.alloc([4, 16], F32)
        adt = B.alloc([4, 16], F32)
        dts = B.alloc([4, 32], F32)

        h_va1 = op("pool", lambda e: e.memset(VAst, 1.0))
        h_halo0 = op("pool", lambda e: e.memset(halo, 0.0))
        ds_x = B.dsem("x")
        dsd = {}

        def sd(key):
            if key not in dsd:
                dsd[key] = B.dsem("o")
            return dsd[key]
        bank_free = {b: None for b in range(8)}
        free = {}

        def fr(name):
            return free.get(name)

        state = {"xt_free": None, "h1": [None, None], "h1_free": [None, None]}
        phaseA_out = []

        def prep(i):
            b = i % 2
            hx = None
            for kc in range(8):
                hx = dma("sync", xt[:, kc, :], xT[kc * 128:(kc + 1) * 128, i * 512:(i + 1) * 512],
                         deps=[state["xt_free"]], dsem=ds_x)
            hst = None
            for kc in range(8):
                sl = kc % 2
                hs = op("pool", lambda e, kc=kc, sl=sl: e.tensor_tensor(out=sq[sl], in0=xt[:, kc, :], in1=xt[:, kc, :],
                                                                      op=ALU.mult), [hx, fr(("sq", sl))])
                hst = op("pe", lambda e, kc=kc, sl=sl: e.matmul(B.bank(0), lhsT=ones_bf, rhs=sq[sl],
                                                               start=(kc == 0), stop=(kc == 7)),
                         [hs, h_ones, bank_free[0] if kc == 0 else None])
                free[("sq", sl)] = hst
            h = op("act", lambda e: e.activation(out=rstd, in_=B.bank(0), func=AF.Ln, scale=1.0 / D, bias=EPS),
                   [hst, fr("rstd")])
            bank_free[0] = h
            h = op("act", lambda e: e.activation(out=rstd, in_=rstd, func=AF.Exp, scale=-0.5), [h])
            hh = None
            for kc in range(8):
                hn = op("dve", lambda e, kc=kc: e.tensor_mul(out=xt[:, kc, :], in0=xt[:, kc, :], in1=rstd), [h, hst])
                hh = op("pool", lambda e, kc=kc, b=b: e.tensor_scalar(
                    out=h1[b][:, kc, :], in0=xt[:, kc, :], scalar1=s1[:, kc:kc + 1], scalar2=modT[:, kc:kc + 1],
                    op0=ALU.mult, op1=ALU.add), [hn, h_mod, state["h1_free"][b]])
            free["rstd"] = hn
            state["xt_free"] = hh
            state["h1"][b] = hh

        rot = {"main": 0, "qk": 0, "x3": 0, "qn": 0, "zs": 0, "tp": 0}
        MAIN_BANKS = [2, 3, 4, 5]

        def next_bank():
            b = MAIN_BANKS[rot["main"] % len(MAIN_BANKS)]
            rot["main"] += 1
            return b

        def mm_group_f(i, col0):
            b = i % 2
            bk = next_bank()
            h = None
            for kc in range(8):
                h = op("pe", lambda e, kc=kc, bk=bk, b=b: e.matmul(
                    B.bank(bk), lhsT=Wsb[:, kc, col0:col0 + 128], rhs=h1[b][:, kc, :],
                    start=(kc == 0), stop=(kc == 7)),
                    ([state["h1"][b], bank_free[bk]] + wdeps(col0, col0 + 128)) if kc == 0 else [],
                    track=(kc == 7))
            return bk, h

        def mm_group_t(i, st, col0, ncols):
            b = i % 2
            bk = next_bank()
            h = None
            for kc in range(8):
                h = op("pe", lambda e, kc=kc, bk=bk, b=b: e.matmul(
                    B.bank(bk, ncols), lhsT=h1[b][:, kc, st * 128:(st + 1) * 128], rhs=Wsb[:, kc, col0:col0 + ncols],
                    start=(kc == 0), stop=(kc == 7)),
                    ([state["h1"][b], bank_free[bk]] + wdeps(col0, col0 + ncols)) if kc == 0 else [],
                    track=(kc == 7))
            return bk, h

        for i in range(NTILES):
            if i == 0:
                prep(0)
            b = i % 2
            c0t = i * 512
            for st in (range(4) if "d" in GROUPS else []):
                bk, hm = mm_group_t(i, st, 3072, 16)
                h = op("dve", lambda e, bk=bk, st=st: e.tensor_add(out=tdt[:, st, :], in0=B.bank(bk, 16), in1=dtb_sb),
                       [hm, hconst, fr("dts")])
                bank_free[bk] = h
                h2 = op("act", lambda e, st=st: e.activation(out=adt[:, st, :], in_=tdt[:, st, :], func=AF.Abs), [h])
                h2 = op("act", lambda e, st=st: e.activation(out=adt[:, st, :], in_=adt[:, st, :], func=AF.Exp, scale=-1.0), [h2])
                h2 = op("act", lambda e, st=st: e.activation(out=adt[:, st, :], in_=adt[:, st, :], func=AF.Ln, bias=1.0), [h2])
                h2 = op("dve", lambda e, st=st: e.scalar_tensor_tensor(out=dts[:, st, 0:16], in0=tdt[:, st, :], scalar=0.0,
                                                                      in1=adt[:, st, :], op0=ALU.max, op1=ALU.add), [h2])
                h2 = op("dve", lambda e, st=st: e.tensor_mul(out=dts[:, st, 16:32], in0=dts[:, st, 0:16], in1=A_sb), [h2, h_A])
            if "d" in GROUPS:
                free["dts"] = dma("sync", DTS[c0t:c0t + 512, :].rearrange("(st p) c -> p st c", p=128), dts, deps=[h2], dsem=sd("dts"))
            pend_qk = []
            for j in (range(16) if "q" in GROUPS else range(4)):
                if "q" not in GROUPS:
                    if j == 3 and i + 1 < NTILES:
                        prep(i + 1)
                    continue
                isq = j < 8
                col0 = (3088 if isq else 4112) + (j % 8) * 128
                bk, hm = mm_group_f(i, col0)
                s2_ = rot["qk"] % 2
                rot["qk"] += 1
                hs = op("act", lambda e, bk=bk, s2_=s2_: e.activation(out=qsq[s2_], in_=B.bank(bk), func=AF.Square),
                        [hm, fr(("qsq", s2_))])
                hr = op("act", lambda e, bk=bk, s2_=s2_: e.activation(out=qraw[s2_], in_=B.bank(bk), func=AF.Copy),
                        [hm, hs, fr(("qraw", s2_))])
                bank_free[bk] = [hs, hr]

                def qk_part2(s2_=s2_, hs=hs, hr=hr, isq=isq, j=j):
                    hp = op("pe", lambda e: e.matmul(B.bank(1), lhsT=blk64, rhs=qsq[s2_], start=True, stop=True),
                            [hs, h_blk, bank_free[1]])
                    free[("qsq", s2_)] = hp
                    hl = op("act", lambda e: e.activation(out=qrs[s2_], in_=B.bank(1), func=AF.Ln, bias=EPS),
                            [hp, fr(("qrs", s2_))])
                    bank_free[1] = hl
                    hl = op("act", lambda e: e.activation(out=qrs[s2_], in_=qrs[s2_], func=AF.Exp, scale=-0.5), [hl])
                    s3 = rot["qn"] % 3
                    rot["qn"] += 1
                    wc = 0 if isq else 1
                    hq = op("dve", lambda e: e.scalar_tensor_tensor(
                        out=qn[s3], in0=qraw[s2_], scalar=qkw_sb[:, wc:wc + 1], in1=qrs[s2_], op0=ALU.mult, op1=ALU.mult),
                        [hl, hr, h_qw, fr(("qn", s3))])
                    free[("qraw", s2_)] = hq
                    free[("qrs", s2_)] = hq
                    dst = (QT if isq else KT)[(j % 8) * 128:(j % 8 + 1) * 128, c0t:c0t + 512]
                    free[("qn", s3)] = dma("sync", dst, qn[s3], deps=[hq], dsem=sd(("qn", s3)))
                if pend_qk:
                    pend_qk.pop(0)()
                pend_qk.append(qk_part2)
                if j == 3 and i + 1 < NTILES:
                    prep(i + 1)
            while pend_qk:
                pend_qk.pop(0)()
            pend_x = []
            for ct in (range(16) if "x" in GROUPS else []):
                col0 = 1024 + ct * 128
                bk, hm = mm_group_f(i, col0)
                s3 = rot["x3"] % 3
                rot["x3"] += 1
                ha = op("act", lambda e, bk=bk, s3=s3, ct=ct: e.activation(
                    out=acc[s3], in_=B.bank(bk), func=AF.Identity, scale=convw_sb[:, ct, 3:4], bias=convb_sb[:, ct:ct + 1]),
                    [hm, hconst, fr(("acc", s3))])
                hc_ = op("act", lambda e, bk=bk, s3=s3: e.activation(out=xr[s3][:, 3:515], in_=B.bank(bk), func=AF.Copy),
                         [hm, fr(("xr", s3))])
                bank_free[bk] = hc_
                hh1 = op("pool", lambda e, s3=s3, ct=ct: e.tensor_copy(out=xr[s3][:, 0:3], in_=halo[:, ct, :]),
                         [fr(("xr", s3)), h_halo0, fr(("halo", ct))])
                hh2 = op("pool", lambda e, s3=s3, ct=ct: e.tensor_copy(out=halo[:, ct, :], in_=xr[s3][:, 512:515]), [hc_, hh1])
                free[("halo", ct)] = hh2
                ht = ha
                for k in range(3):
                    ht = op("dve", lambda e, s3=s3, ct=ct, k=k: e.scalar_tensor_tensor(
                        out=acc[s3], in0=xr[s3][:, k:k + 512], scalar=convw_sb[:, ct, k:k + 1], in1=acc[s3],
                        op0=ALU.mult, op1=ALU.add), [ht, hc_, hh1])
                free[("xr", s3)] = [ht, hh2]
                hx_ = op("act", lambda e, s3=s3: e.activation(out=xc[s3], in_=acc[s3], func=AF.Silu),
                         [ht, fr(("xc", s3))])
                free[("acc", s3)] = hx_
                def x_part2(ct=ct, s3=s3, hx_=hx_):
                    last = [hx_]
                    if ct < 12:
                        tpi = rot["tp"] % 2
                        rot["tp"] += 1
                        tp = B.bank(6 + tpi, 1024, BF16)[:, 0:512]
                        hp = None
                        for st in range(4):
                            hp = op("pe", lambda e, st=st: e.transpose(
                                out=tp[:, st * 128:(st + 1) * 128], in_=xc[s3][:, st * 128:(st + 1) * 128], identity=ident),
                                [hx_, h_id, fr(("tp", tpi))] if st == 0 else [], track=(st == 3))
                        if ct < 8:
                            dst_ap = XSst[:, :, ct * 128:(ct + 1) * 128]
                            fkey = "XSst"
                        else:
                            dst_ap = BMst[:, :, (ct - 8) * 128:(ct - 7) * 128]
                            fkey = "BMst"
                        he = op("dve", lambda e: e.tensor_copy(
                            out=dst_ap, in_=tp.rearrange("p (s c) -> p s c", s=4)), [hp, fr(fkey)])
                        free[("tp", tpi)] = he
                        last.append(hp)
                        free.setdefault("st_writes", []).append(he)
                    if ct >= 8:
                        hd = dma("sync", BCT[(ct - 8) * 128:(ct - 7) * 128, c0t:c0t + 512], xc[s3], deps=[hx_],
                                 dsem=sd(("xc", s3)))
                        last.append(hd)
                    free[("xc", s3)] = last
                pend_x.append(x_part2)
                if len(pend_x) > 2:
                    pend_x.pop(0)()
            while pend_x:
                pend_x.pop(0)()
            if "x" not in GROUPS:
                free["st_writes"] = []
            hd = dma("sync", XS[c0t:c0t + 512, :].rearrange("(st p) c -> p st c", p=128), XSst,
                     deps=free["st_writes"], dsem=sd("XSst"))
            free["XSst"] = hd
            phaseA_out.append(hd)
            hd = dma("sync", BM[c0t:c0t + 512, :].rearrange("(st p) c -> p st c", p=128), BMst,
                     deps=free["st_writes"], dsem=sd("BMst"))
            free["BMst"] = hd
            phaseA_out.append(hd)
            free["st_writes"] = []
            for st in (range(4) if "z" in GROUPS else []):
                for half in range(2):
                    bk, hm = mm_group_t(i, st, half * 512, 512)
                    s3 = rot["zs"] % 3
                    rot["zs"] += 1
                    hz = op("act", lambda e, bk=bk, s3=s3: e.activation(out=zs[s3], in_=B.bank(bk), func=AF.Silu),
                            [hm, fr(("zs", s3))])
                    bank_free[bk] = hz
                    r0 = c0t + st * 128
                    free[("zs", s3)] = dma("sync", ZS[r0:r0 + 128, half * 512:(half + 1) * 512], zs[s3], deps=[hz], dsem=sd(("zs", s3)))
                    phaseA_out.append(free[("zs", s3)])
            hv = []
            for st in (range(4) if "v" in GROUPS else []):
                for half in range(2):
                    bk, hm = mm_group_t(i, st, 5136 + half * 512, 512)
                    h = op("dve", lambda e, bk=bk, st=st, half=half: e.tensor_copy(
                        out=VAst[:, st, half * 8:(half + 1) * 8, 0:64],
                        in_=B.bank(bk).rearrange("p (h d) -> p h d", h=8)), [hm, h_va1, fr("VAst")])
                    bank_free[bk] = h
                    hv.append(h)
            free["VAst"] = dma("sync", VA[c0t:c0t + 512, :].rearrange("(st p) c -> p st c", p=128),
                               VAst.rearrange("p s h d -> p s (h d)"), deps=hv, dsem=sd("VAst"))
            phaseA_out.append(free["VAst"])
            state["h1_free"][b] = (B.sem["pe"], B.cnt["pe"])
        phaseA_done = [(v[0], v[1]) for v in dsd.values()]
        out_handles += phaseA_done
        B.release(mA)


    def phase_S():
        B.release(persist_mark)
        mS = B.mark()
        B.phase_barrier(out_handles)
        NCH = 32
        tri_f = cst_f[:, 3, :]
        strict_f = cst_f[:, 4, :]
        tri_b = B.alloc([128], BF16)
        ssdnw_sb = B.alloc([1024], F32)
        dts_all = B.alloc([NCH, 32], F32)
        xs_ = [B.alloc([16, 64], BF16) for _ in range(2)]
        bm_ = [B.alloc([512], BF16) for _ in range(2)]
        bct_ = [B.alloc([8, 128], BF16) for _ in range(2)]
        zs_ = [B.alloc([1024], BF16) for _ in range(2)]
        R = B.alloc([16, 128], F32)
        Lt = [B.alloc([16, 128], BF16) for _ in range(2)]
        cbm = [B.alloc([4, 128], BF16) for _ in range(2)]
        G = [B.alloc([16, 128], BF16) for _ in range(2)]
        xdt = [B.alloc([16, 64], BF16) for _ in range(2)]
        xdec = [B.alloc([16, 64], BF16) for _ in range(2)]
        t1 = [B.alloc([16, 64], F32) for _ in range(2)]
        yo = [B.alloc([1024], BF16) for _ in range(2)]
        sqj = B.alloc([256], BF16)
        ed = [B.alloc([32], F32) for _ in range(2)]
        ss = [B.alloc([4], F32) for _ in range(2)]
        Sst = B.alloc([16, 64], F32)
        Sb = B.alloc([16, 64], BF16)

        ds0 = B.dsem("s_c")
        hc0 = dma("sync", ssdnw_sb, ssdnw, dsem=ds0)
        hc0 = dma("sync", dts_all, DTS.rearrange("(c p) k -> p c k", p=128), dsem=ds0)
        h_trib = op("dve", lambda e: e.tensor_copy(out=tri_b, in_=tri_f))
        h_s0 = op("pool", lambda e: e.memset(Sst, 0.0))
        h_sb0 = op("pool", lambda e: e.memset(Sb, 0.0))
        ds_l = [B.dsem("s_l0"), B.dsem("s_l1")]
        ds_y = [B.dsem("s_y0"), B.dsem("s_y1")]
        fr_ = {}
        bfree = {b: None for b in range(8)}

        def F(k):
            return fr_.get(k)

        hS = h_s0
        hSb = h_sb0
        for c in range(NCH):
            b = c % 2
            r0 = c * 128
            hl = dma("sync", xs_[b].rearrange("p h d -> p (h d)"), XS[r0:r0 + 128, :], deps=[F(("ld", b))], dsem=ds_l[b])
            hl = dma("sync", bm_[b], BM[r0:r0 + 128, :], dsem=ds_l[b])
            hl = dma("sync", bct_[b], BCT[:, r0:r0 + 128].rearrange("(g p) t -> p g t", p=128), dsem=ds_l[b])
            hl = dma("sync", zs_[b], ZS[r0:r0 + 128, :], dsem=ds_l[b])
            dA = dts_all[:, c, 16:32]
            dtc = dts_all[:, c, 0:16]
            hR = op("pool", lambda e, dA=dA: e.tensor_tensor(
                out=R, in0=tri_f.unsqueeze(1).to_broadcast([128, 16, 128]),
                in1=dA.unsqueeze(2).to_broadcast([128, 16, 128]), op=ALU.mult), [hc0, hconst, F("R")])
            hseg = []
            for q4 in range(4):
                hseg.append(op("pe", lambda e, q4=q4: e.matmul(
                    B.bank(q4), lhsT=strict_f, rhs=R[:, q4 * 4:(q4 + 1) * 4, :].rearrange("p h l -> p (h l)"),
                    start=True, stop=True), [hR, bfree[q4]]))
            fr_["R"] = hseg[-1]
            hsm = op("pe", lambda e, dA=dA: e.matmul(B.bank(5, 16), lhsT=tri_f, rhs=dA, start=True, stop=True),
                     [hc0, bfree[5]], track=False)
            hsm = op("pe", lambda e, dA=dA: e.matmul(B.bank(5, 32)[:, 16:32], lhsT=ones_f, rhs=dA, start=True, stop=True),
                     [h_onesf])
            hL = None
            for q4 in range(4):
                hL = op("act", lambda e, q4=q4, b=b: e.activation(
                    out=Lt[b][:, q4 * 4:(q4 + 1) * 4, :].rearrange("p h l -> p (h l)"), in_=B.bank(q4), func=AF.Exp),
                    [hseg[q4], F(("Lt", b))])
                bfree[q4] = hL
            hed = op("act", lambda e, b=b: e.activation(out=ed[b], in_=B.bank(5, 32), func=AF.Exp), [hsm, F(("ed", b))])
            bfree[5] = hed
            hcb = None
            for g in range(4):
                hcb = op("pe", lambda e, g=g, b=b: e.matmul(
                    B.bank(4)[:, g * 128:(g + 1) * 128], lhsT=bct_[b][:, g, :], rhs=bct_[b][:, 4 + g, :],
                    start=True, stop=True), [hl, bfree[4]] if g == 0 else [], track=(g == 3))
            hcbm = op("dve", lambda e, b=b: e.tensor_tensor(
                out=cbm[b], in0=B.bank(4).rearrange("p (g l) -> p g l", g=4),
                in1=tri_b.unsqueeze(1).to_broadcast([128, 4, 128]), op=ALU.mult), [hcb, h_trib, F(("cbm", b))])
            bfree[4] = hcbm
            hG = None
            for g in range(4):
                hG = op("dve", lambda e, g=g, b=b: e.tensor_tensor(
                    out=G[b][:, 4 * g:4 * g + 4, :], in0=Lt[b][:, 4 * g:4 * g + 4, :],
                    in1=cbm[b][:, g, :].unsqueeze(1).to_broadcast([128, 4, 128]), op=ALU.mult),
                    [hL, hcbm, F(("G", b))])
            hxdt = op("pool", lambda e, b=b, dtc=dtc: e.tensor_tensor(
                out=xdt[b], in0=xs_[b], in1=dtc.unsqueeze(2).to_broadcast([128, 16, 64]), op=ALU.mult),
                [hl, hc0, F(("xdt", b))])
            hxdec = op("pool", lambda e, b=b: e.tensor_tensor(
                out=xdec[b], in0=xdt[b], in1=Lt[b][:, :, 127:128].to_broadcast([128, 16, 64]), op=ALU.mult),
                [hxdt, hL, F(("xdec", b))])
            hyd = None
            for h_ in range(16):
                hyd = op("pe", lambda e, h_=h_, b=b: e.matmul(
                    B.psum[:, 3072 + h_ * 64:3072 + (h_ + 1) * 64], lhsT=G[b][:, h_, :], rhs=xdt[b][:, h_, :],
                    start=True, stop=True), [hG, hxdt, bfree[6], bfree[7]] if h_ == 0 else [], track=(h_ == 15))
            fr_[("G", b)] = hyd
            hyo = None
            for g in range(4):
                hyo = op("pe", lambda e, g=g, b=b: e.matmul(
                    B.psum[:, g * 256:(g + 1) * 256], lhsT=bct_[b][:, 4 + g, :],
                    rhs=Sb[:, 4 * g:4 * g + 4, :].rearrange("p h d -> p (h d)"), start=True, stop=True),
                    [hSb, bfree[0], bfree[1]] if g == 0 else [], track=(g == 3))
            hst = None
            for g in range(4):
                hst = op("pe", lambda e, g=g, b=b: e.matmul(
                    B.psum[:, 1024 + g * 256:1024 + (g + 1) * 256], lhsT=bm_[b][:, g * 128:(g + 1) * 128],
                    rhs=xdec[b][:, 4 * g:4 * g + 4, :].rearrange("p h d -> p (h d)"), start=True, stop=True),
                    [hxdec, bfree[2], bfree[3]] if g == 0 else [], track=(g == 3))
            fr_[("xdec", b)] = hst
            fr_[("xdt", b)] = [hyd, hxdec]
            fr_[("Lt", b)] = [hG, hxdec]
            fr_[("cbm", b)] = hG
            yoff = B.psum[:, 0:1024].rearrange("p (h d) -> p h d", h=16)
            ydg = B.psum[:, 3072:4096].rearrange("p (h d) -> p h d", h=16)
            h1_ = op("dve", lambda e, b=b: e.tensor_tensor(
                out=t1[b], in0=yoff, in1=ed[b][:, 0:16].unsqueeze(2).to_broadcast([128, 16, 64]), op=ALU.mult),
                [hyo, hed, F(("t1", b))])
            bfree[0] = h1_
            bfree[1] = h1_
            h2_ = op("dve", lambda e, b=b: e.tensor_tensor(out=t1[b], in0=t1[b], in1=ydg, op=ALU.add), [h1_, hyd])
            bfree[6] = h2_
            bfree[7] = h2_
            h3_ = op("dve", lambda e, b=b: e.tensor_tensor(
                out=yo[b].rearrange("p (h d) -> p h d", h=16), in0=xs_[b],
                in1=dskip_sb.unsqueeze(2).to_broadcast([128, 16, 64]), op=ALU.mult), [hl, hconst, F(("yo", b))])
            h3_ = op("dve", lambda e, b=b: e.tensor_tensor(
                out=t1[b], in0=t1[b], in1=yo[b].rearrange("p (h d) -> p h d", h=16), op=ALU.add), [h2_, h3_])
            h4_ = op("pool", lambda e, b=b: e.tensor_tensor(
                out=t1[b], in0=t1[b], in1=zs_[b].rearrange("p (h d) -> p h d", h=16), op=ALU.mult), [h3_, hl])
            t1f = t1[b].rearrange("p h d -> p (h d)")
            hq_ = None
            for g in range(4):
                hq_ = op("act", lambda e, g=g, b=b, t1f=t1f: e.activation(
                    out=sqj, in_=t1f[:, g * 256:(g + 1) * 256], func=AF.Square, accum_out=ss[b][:, g:g + 1]),
                    [h4_, hq_, F(("ss", b))])
            hq_ = op("act", lambda e, b=b: e.activation(out=ss[b], in_=ss[b], func=AF.Ln, scale=1.0 / 256, bias=EPS), [hq_])
            hq_ = op("act", lambda e, b=b: e.activation(out=ss[b], in_=ss[b], func=AF.Exp, scale=-0.5), [hq_])
            hy = None
            for g in range(4):
                hy = op("dve", lambda e, g=g, b=b, t1f=t1f: e.scalar_tensor_tensor(
                    out=yo[b][:, g * 256:(g + 1) * 256], in0=t1f[:, g * 256:(g + 1) * 256], scalar=ss[b][:, g:g + 1],
                    in1=ssdnw_sb[:, g * 256:(g + 1) * 256], op0=ALU.mult, op1=ALU.mult), [hq_, h4_, hc0])
            fr_[("ss", b)] = hy
            fr_[("t1", b)] = hy
            hdy = dma("sync", YS[r0:r0 + 128, :], yo[b], deps=[hy], dsem=ds_y[b])
            fr_[("yo", b)] = hdy
            hu = op("dve", lambda e, b=b: e.tensor_tensor(
                out=Sst, in0=Sst, in1=ed[b][:, 16:32].unsqueeze(2).to_broadcast([128, 16, 64]), op=ALU.mult),
                [hS, hed])
            hu = op("dve", lambda e: e.tensor_tensor(
                out=Sst, in0=Sst, in1=B.psum[:, 1024:2048].rearrange("p (h d) -> p h d", h=16), op=ALU.add), [hu, hst])
            bfree[2] = hu
            bfree[3] = hu
            hS = hu
            hSb = op("dve", lambda e: e.tensor_copy(out=Sb, in_=Sst), [hu, hyo])
            fr_[("ed", b)] = [h1_, hu]
            fr_[("ld", b)] = [hy, hyd, hst, hcb, hyo, h4_, hxdt]
        out_handles.append((ds_y[0][0], ds_y[0][1]))
        out_handles.append((ds_y[1][0], ds_y[1][1]))
        B.release(mS)

    if "S" in phases:
        phase_S()


    def phase_B():
        B.release(persist_mark)
        mB = B.mark()
        B.phase_barrier(out_handles)
        mask2 = B.alloc([2, 256], BF16)
        qt = [B.alloc([4096], BF16) for _ in range(2)]
        kt = [B.alloc([4096], BF16) for _ in range(2)]
        vv = [[B.alloc([32, 130], BF16) for _ in range(3)] for _ in range(2)]
        ost = [B.alloc([32, 130], F32) for _ in range(2)]
        P = [B.alloc([2, 256], BF16) for _ in range(4)]
        hm_ = None
        for h_ in range(2):
            hm_ = op("dve", lambda e, h_=h_: e.tensor_copy(out=mask2[:, h_, 0:128], in_=cst_f[:, 3, :]), [hconst])
            hm_ = op("dve", lambda e, h_=h_: e.tensor_copy(out=mask2[:, h_, 128:256], in_=cst_f[:, 5, :]), [hconst])
        ds_in = [B.dsem("b_in0"), B.dsem("b_in1")]
        ds_od = [B.dsem("b_od0"), B.dsem("b_od1")]
        VAv = VA
        ODv = OD
        fr_ = {}
        bfree = {b: None for b in range(8)}
        rot = {"S": 0, "O": 0, "P": 0, "ost": 0, "m": 0}
        DIL = [(1, 32), (4, 8), (16, 2)]

        def F(k):
            return fr_.get(k)

        for nm in ("wo", "w1", "w2"):
            wcast["sem"][nm] = B.dsem("wc_" + nm)
        for kc in range(16):
            wcast["list"].append(("wo", WoB[kc * 128:(kc + 1) * 128, :], w_out[kc * 128:(kc + 1) * 128, :]))
        for cq in range(4):
            for kc in range(8):
                wcast["list"].append(("w1", W1B[kc * 128:(kc + 1) * 128, cq * 1024:(cq + 1) * 1024],
                                      w_ff1[kc * 128:(kc + 1) * 128, cq * 1024:(cq + 1) * 1024]))
        for kc in range(32):
            wcast["list"].append(("w2", W2B[kc * 128:(kc + 1) * 128, :], w_ff2[kc * 128:(kc + 1) * 128, :]))
        wcast["h"] = {}

        def drip(n):
            for _ in range(n):
                if wcast["list"]:
                    nm, o_, i_ = wcast["list"].pop(0)
                    wcast["h"][nm] = dma("pool", o_, i_, dsem=wcast["sem"][nm])

        for hp in range(8):
            b = hp % 2
            c0 = hp * 130
            hin = dma("sync", qt[b], QT[hp * 128:(hp + 1) * 128, :], deps=[F(("in", b))], dsem=ds_in[b])
            hin = dma("sync", kt[b], KT[hp * 128:(hp + 1) * 128, :], dsem=ds_in[b])
            for jq in range(4):
                hin = dma("sync", vv[b][0][:, jq * 8:(jq + 1) * 8, :],
                          VAv[jq * 1024:(jq + 1) * 1024, c0:c0 + 130].rearrange("(j i) c -> i j c", i=128), dsem=ds_in[b])
            for r in range(4):
                hin = dma("sync", vv[b][1][:, r * 8:(r + 1) * 8, :],
                          VAv[:, c0:c0 + 130].rearrange("(j i r) c -> i r j c", i=128, r=4)[:, r, :, :], dsem=ds_in[b])
            for r in range(16):
                hin = dma("sync", vv[b][2][:, r * 2:(r + 1) * 2, :],
                          VAv[:, c0:c0 + 130].rearrange("(j i r) c -> i r j c", i=128, r=16)[:, r, :, :], dsem=ds_in[b])
            last_reads = []
            steps = []
            for di, (d, NB) in enumerate(DIL):
                os_i = rot["ost"] % 2
                rot["ost"] += 1
                for r in range(d):
                    for j in range(NB):
                        steps.append(dict(di=di, d=d, NB=NB, r=r, j=j, os_i=os_i, last=(r == d - 1 and j == NB - 1)))

            def emit_qk(st_, b=b):
                d, NB, r, j = st_["d"], st_["NB"], st_["r"], st_["j"]
                nq = 256 if j + 1 < NB else 128
                t0 = r + d * 128 * j
                ktok = slice(t0, t0 + d * 127 + 1, d)
                qtok = slice(t0, t0 + d * (nq - 1) + 1, d)
                bs = rot["S"] % 2
                rot["S"] += 1
                hqk = None
                for h_ in range(2):
                    hqk = op("pe", lambda e, h_=h_: e.matmul(
                        B.bank(2 * bs + h_)[:, 0:nq], lhsT=kt[b][64 * h_:64 * h_ + 64, ktok],
                        rhs=qt[b][64 * h_:64 * h_ + 64, qtok], start=True, stop=True),
                        [hin, bfree[2 * bs], bfree[2 * bs + 1]] if h_ == 0 else [], track=(h_ == 1))
                pi = rot["P"] % 4
                rot["P"] += 1
                Pc = P[pi]
                hp_ = op("act", lambda e: e.activation(
                    out=Pc[:, :, 0:nq],
                    in_=B.psum[:, 2 * bs * 512:(2 * bs + 2) * 512].rearrange("p (h q) -> p h q", h=2)[:, :, 0:nq],
                    func=AF.Exp), [hqk, F(("P", pi))])
                bfree[2 * bs] = hp_
                bfree[2 * bs + 1] = hp_
                meng = "dve" if rot["m"] % 2 == 0 else "pool"
                rot["m"] += 1
                hmk = op(meng, lambda e: e.tensor_tensor(
                    out=Pc[:, :, 0:nq], in0=Pc[:, :, 0:nq], in1=mask2[:, :, 0:nq], op=ALU.mult), [hp_, hm_])
                return dict(Pc=Pc, pi=pi, hmk=hmk)

            def emit_pv(st_, cur, prev, b=b, c0=c0):
                di, d, NB, r, j, os_i = st_["di"], st_["d"], st_["NB"], st_["r"], st_["j"], st_["os_i"]
                bo = 4 + rot["O"] % 4
                rot["O"] += 1
                tile_c = r * NB + j
                Pc = cur["Pc"]
                hpv = None
                for h_ in range(2):
                    oap = B.bank(bo)[:, h_ * 65:(h_ + 1) * 65]
                    if j > 0:
                        Pp = prev["Pc"]
                        op("pe", lambda e, h_=h_, oap=oap, Pp=Pp: e.matmul(
                            oap, lhsT=Pp[:, h_, 128:256], rhs=vv[b][di][:, tile_c - 1, h_ * 65:(h_ + 1) * 65],
                            start=True, stop=False), [prev["hmk"], cur["hmk"], bfree[bo]] if h_ == 0 else [], track=False)
                    hpv = op("pe", lambda e, h_=h_, oap=oap: e.matmul(
                        oap, lhsT=Pc[:, h_, 0:128], rhs=vv[b][di][:, tile_c, h_ * 65:(h_ + 1) * 65],
                        start=(j == 0), stop=True), [cur["hmk"], bfree[bo]] if (h_ == 0 and j == 0) else [],
                        track=(h_ == 1))
                if j > 0:
                    fr_[("P", prev["pi"])] = hpv
                fr_[("P", cur["pi"])] = hpv
                hev = op("dve", lambda e: e.tensor_copy(out=ost[os_i][:, tile_c, :], in_=B.bank(bo)[:, 0:130]),
                         [hpv, F(("ost", os_i))])
                bfree[bo] = hev
                last_reads[:] = [hpv]
                if st_["last"]:
                    dst = ODv[di, :, c0:c0 + 130].rearrange("(j i r) c -> i r j c", i=128, r=d)
                    hod = None
                    for r2 in range(d):
                        for jq in range(0, NB, 8):
                            je = min(NB, jq + 8)
                            hod = dma("sync", dst[:, r2, jq:je, :], ost[os_i][:, r2 * NB + jq:r2 * NB + je, :], deps=[hev],
                                      dsem=ds_od[os_i])
                    fr_[("ost", os_i)] = hod

            nst = len(steps)
            info = [None] * nst
            for t in range(nst + 1):
                if t % 6 == 0:
                    drip(1 if hp > 0 else 2)
                if t < nst:
                    info[t] = emit_qk(steps[t])
                if t >= 1:
                    emit_pv(steps[t - 1], info[t - 1], info[t - 2] if steps[t - 1]["j"] > 0 else None)
            fr_[("in", b)] = last_reads
        drip(1000)
        out_handles.append((ds_od[0][0], ds_od[0][1]))
        out_handles.append((ds_od[1][0], ds_od[1][1]))
        B.release(mB)

    if "B" in phases:
        phase_B()


    def phase_C1():
        B.release(persist_mark)
        B.phase_barrier(out_handles)
        Wo = B.alloc([16, 1024], BF16)
        attnw_sb = B.alloc([1024], F32)
        od = [B.alloc([3, 1040], F32) for _ in range(2)]
        ys_ = [B.alloc([1024], BF16) for _ in range(2)]
        rden = [B.alloc([16], F32) for _ in range(2)]
        of = [B.alloc([16, 64], F32) for _ in range(2)]
        ya = [B.alloc([1024], BF16) for _ in range(2)]
        ssq = [B.alloc([2], F32) for _ in range(2)]
        sqj = B.alloc([1024], BF16)
        yT = [B.alloc([16, 512], BF16) for _ in range(2)]
        xt = B.alloc([8, 512], F32)
        sq = [B.alloc([512], BF16) for _ in range(2)]
        rstd = B.alloc([512], F32)
        tmp = [B.alloc([512], F32) for _ in range(2)]
        h2 = [B.alloc([8, 512], BF16) for _ in range(2)]
        ds_w = B.dsem("c1w")
        hw = None
        for kc in range(16):
            hw = dma("sync", Wo[:, kc, :], WoB[kc * 128:(kc + 1) * 128, :], deps=[wcast["h"].get("wo")], dsem=ds_w)
        ds_aw = B.dsem("c1aw")
        haw = dma("sync", attnw_sb, attnw, dsem=ds_aw)
        ds_l = [B.dsem("c1l0"), B.dsem("c1l1")]
        ds_x = B.dsem("c1x")
        ds_o1 = B.dsem("c1o1")
        ds_o2 = [B.dsem("c1o2a"), B.dsem("c1o2b")]
        fr_ = {}
        bfree = {b: None for b in range(8)}
        rot = {"tp": 0, "mb": 0}

        def F(k):
            return fr_.get(k)

        sidx_ = {"v": 0}
        hyT_of = {}

        def combine(i):
            b = i % 2
            c0t = i * 512
            hyT = []
            for st in range(4):
                s_ = sidx_["v"] % 2
                sidx_["v"] += 1
                r0 = c0t + st * 128
                hl = dma("sync", od[s_], OD[:, r0:r0 + 128, :].rearrange("d t c -> t d c"), deps=[F(("od", s_))], dsem=ds_l[s_])
                hl = dma("sync", ys_[s_], YS[r0:r0 + 128, :], deps=[F(("ys", s_))], dsem=ds_l[s_])
                o0 = od[s_][:, 0, :]
                h = op("dve", lambda e, s_=s_, o0=o0: e.tensor_add(out=o0, in0=o0, in1=od[s_][:, 1, :]), [hl])
                h = op("dve", lambda e, s_=s_, o0=o0: e.tensor_add(out=o0, in0=o0, in1=od[s_][:, 2, :]), [h])
                o3 = o0.rearrange("p (h d) -> p h d", h=16)
                h = op("dve", lambda e, s_=s_, o3=o3: e.reciprocal(out=rden[s_], in_=o3[:, :, 64]), [h, F(("rden", s_))])
                h = op("dve", lambda e, s_=s_, o3=o3: e.tensor_tensor(
                    out=of[s_], in0=o3[:, :, 0:64], in1=rden[s_].unsqueeze(2).to_broadcast([128, 16, 64]), op=ALU.mult),
                    [h, F(("of", s_))])
                fr_[("od", s_)] = h
                off = of[s_].rearrange("p h d -> p (h d)")
                ha = op("act", lambda e, s_=s_, off=off: e.activation(out=sqj, in_=off, func=AF.Square,
                                                                     accum_out=ssq[s_][:, 0:1]), [h, F(("ssq", s_)), F("sqj")])
                fr_["sqj"] = ha
                ha = op("act", lambda e, s_=s_: e.activation(out=ssq[s_][:, 1:2], in_=ssq[s_][:, 0:1], func=AF.Ln,
                                                             scale=1.0 / 1024, bias=EPS), [ha])
                ha = op("act", lambda e, s_=s_: e.activation(out=ssq[s_][:, 1:2], in_=ssq[s_][:, 1:2], func=AF.Exp, scale=-0.5), [ha])
                hya = op("dve", lambda e, s_=s_, off=off: e.scalar_tensor_tensor(
                    out=ya[s_], in0=off, scalar=ssq[s_][:, 1:2], in1=attnw_sb, op0=ALU.mult, op1=ALU.mult),
                    [ha, haw, F(("ya", s_))])
                fr_[("ssq", s_)] = hya
                fr_[("of", s_)] = hya
                fr_[("rden", s_)] = hya
                hlast = None
                for q4 in range(4):
                    tpi = rot["tp"] % 2
                    rot["tp"] += 1
                    tp = B.bank(6 + tpi, 1024, BF16)[:, 0:512]
                    hp = None
                    for k4 in range(4):
                        kc = q4 * 4 + k4
                        src = ys_[s_][:, kc * 128:(kc + 1) * 128] if kc < 8 else ya[s_][:, (kc - 8) * 128:(kc - 7) * 128]
                        hp = op("pe", lambda e, k4=k4, tp=tp, src=src: e.transpose(
                            out=tp[:, k4 * 128:(k4 + 1) * 128], in_=src, identity=ident),
                            [hl, hya, h_id, F(("tp", tpi))] if k4 == 0 else [], track=(k4 == 3))
                    he = op("dve", lambda e, tp=tp, b=b, q4=q4, st=st: e.tensor_copy(
                        out=yT[b][:, q4 * 4:q4 * 4 + 4, st * 128:(st + 1) * 128],
                        in_=tp.rearrange("p (k t) -> p k t", k=4)), [hp, F(("yT", b))])
                    fr_[("tp", tpi)] = he
                    hlast = hp
                    hyT.append(he)
                fr_[("ys", s_)] = hlast
                fr_[("ya", s_)] = hlast
            hyT_of[i] = hyT

        def finish_(i):
            b = i % 2
            c0t = i * 512
            hyT = hyT_of[i]
            hx = None
            for kc in range(8):
                hx = dma("sync", xt[:, kc, :], xT[kc * 128:(kc + 1) * 128, c0t:c0t + 512], deps=[F("xt")], dsem=ds_x)
            hx1 = None
            hmm_last = None
            for f in range(8):
                bk = 2 + rot["mb"] % 4
                rot["mb"] += 1
                hm = None
                for kc in range(16):
                    hm = op("pe", lambda e, kc=kc, f=f, bk=bk, b=b: e.matmul(
                        B.bank(bk), lhsT=Wo[:, kc, f * 128:(f + 1) * 128], rhs=yT[b][:, kc, :],
                        start=(kc == 0), stop=(kc == 15)), (hyT + [hw, bfree[bk]]) if kc == 0 else [], track=(kc == 15))
                hx1 = op("dve", lambda e, f=f, bk=bk: e.scalar_tensor_tensor(
                    out=xt[:, f, :], in0=B.bank(bk), scalar=modT[:, 16 + f:17 + f], in1=xt[:, f, :],
                    op0=ALU.mult, op1=ALU.add), [hm, hx, h_mod])
                bfree[bk] = hx1
                hmm_last = hm
            fr_[("yT", b)] = hmm_last
            hst1 = None
            for kc in range(8):
                hst1 = dma("sync", X1T[kc * 128:(kc + 1) * 128, c0t:c0t + 512], xt[:, kc, :], deps=[hx1], dsem=ds_o1)
            hst = None
            for kc in range(8):
                sl = kc % 2
                hs = op("pool", lambda e, kc=kc, sl=sl: e.tensor_tensor(out=sq[sl], in0=xt[:, kc, :], in1=xt[:, kc, :],
                                                                      op=ALU.mult), [hx1, F(("sq", sl))])
                hst = op("pe", lambda e, kc=kc, sl=sl: e.matmul(B.bank(0), lhsT=ones_bf, rhs=sq[sl],
                                                               start=(kc == 0), stop=(kc == 7)),
                         [hs, h_ones, bfree[0] if kc == 0 else None])
                fr_[("sq", sl)] = hst
            h = op("act", lambda e: e.activation(out=rstd, in_=B.bank(0), func=AF.Ln, scale=1.0 / D, bias=EPS),
                   [hst, F("rstd")])
            bfree[0] = h
            h = op("act", lambda e: e.activation(out=rstd, in_=rstd, func=AF.Exp, scale=-0.5), [h])
            hh = None
            hn = None
            for kc in range(8):
                sl = kc % 2
                hn = op("dve", lambda e, kc=kc, sl=sl: e.tensor_mul(out=tmp[sl], in0=xt[:, kc, :], in1=rstd),
                        [h, hx1, F(("tmp", sl))])
                hh = op("pool", lambda e, kc=kc, b=b, sl=sl: e.tensor_scalar(
                    out=h2[b][:, kc, :], in0=tmp[sl], scalar1=s2[:, kc:kc + 1], scalar2=modT[:, 24 + kc:25 + kc],
                    op0=ALU.mult, op1=ALU.add), [hn, h_mod, F(("h2", b))])
                fr_[("tmp", sl)] = hh
            fr_["rstd"] = hn
            hd = None
            for kc in range(8):
                hd = dma("sync", H2T[kc * 128:(kc + 1) * 128, c0t:c0t + 512], h2[b][:, kc, :], deps=[hh], dsem=ds_o2[b])
            fr_[("h2", b)] = hd
            fr_["xt"] = [hst1, hn, hst]

        combine(0)
        for i in range(NT):
            if i + 1 < NT:
                combine(i + 1)
            finish_(i)
        out_handles.append((ds_o1[0], ds_o1[1]))
        out_handles.append((ds_o2[0][0], ds_o2[0][1]))
        out_handles.append((ds_o2[1][0], ds_o2[1][1]))

    def phase_C2():
        B.release(persist_mark)
        B.phase_barrier(out_handles)
        W1 = B.alloc([8, 4096], BF16)
        W2 = B.alloc([32, 1024], BF16)
        TN = 256
        h2 = [B.alloc([8, TN], BF16) for _ in range(2)]
        u = B.alloc([32, TN], BF16)
        rr = [B.alloc([TN], F32) for _ in range(3)]
        x1f = [B.alloc([TN], F32) for _ in range(3)]
        oo = [B.alloc([TN], F32) for _ in range(3)]
        hw1c = []
        for cq in range(4):
            dsq = B.dsem("c2w1")
            hq_ = None
            for kc in range(8):
                hq_ = dma("sync", W1[:, kc, cq * 1024:(cq + 1) * 1024], W1B[kc * 128:(kc + 1) * 128, cq * 1024:(cq + 1) * 1024],
                          deps=[wcast["h"].get("w1")], dsem=dsq)
            hw1c.append(hq_)
        ds_w2 = B.dsem("c2w2")
        hw2 = None
        for kc in range(32):
            hw2 = dma("sync", W2[:, kc, :], W2B[kc * 128:(kc + 1) * 128, :], deps=[wcast["h"].get("w2")], dsem=ds_w2)
        ds_h = [B.dsem("c2h0"), B.dsem("c2h1")]
        ds_x = [B.dsem("c2x0"), B.dsem("c2x1"), B.dsem("c2x2")]
        ds_o = [B.dsem("c2o0"), B.dsem("c2o1"), B.dsem("c2o2")]
        fr_ = {}
        bfree = {b: None for b in range(8)}
        rot = {"mb": 0, "r": 0, "x": 0}

        def F(k):
            return fr_.get(k)

        for i in range(S // TN):
            b = i % 2
            c0 = i * TN
            hh = None
            for kc in range(8):
                hh = dma("sync", h2[b][:, kc, :], H2T[kc * 128:(kc + 1) * 128, c0:c0 + TN], deps=[F(("h2", b))], dsem=ds_h[b])
            hu_all = []
            hm = None
            for m in range(32):
                bk = rot["mb"] % 6
                rot["mb"] += 1
                for kc in range(8):
                    hm = op("pe", lambda e, kc=kc, m=m, bk=bk, b=b: e.matmul(
                        B.bank(bk, TN), lhsT=W1[:, kc, m * 128:(m + 1) * 128], rhs=h2[b][:, kc, :],
                        start=(kc == 0), stop=(kc == 7)), [hh, hw1c[m // 8], bfree[bk]] if kc == 0 else [], track=(kc == 7))
                ri = rot["r"] % 3
                rot["r"] += 1
                hr = op("act", lambda e, bk=bk, ri=ri: e.activation(out=rr[ri], in_=B.bank(bk, TN), func=AF.Relu),
                        [hm, F(("rr", ri))])
                bfree[bk] = hr
                hu = op("pool", lambda e, ri=ri, m=m: e.tensor_tensor(out=u[:, m, :], in0=rr[ri], in1=rr[ri], op=ALU.mult),
                        [hr, F("u")])
                fr_[("rr", ri)] = hu
                hu_all.append(hu)
            fr_[("h2", b)] = hm
            hm2 = None
            for f in range(8):
                bk = 6 + f % 2
                xi = rot["x"] % 3
                rot["x"] += 1
                hxl = dma("sync", x1f[xi], X1T[f * 128:(f + 1) * 128, c0:c0 + TN], deps=[F(("x1f", xi))], dsem=ds_x[xi])
                for kc in range(32):
                    hm2 = op("pe", lambda e, kc=kc, f=f, bk=bk: e.matmul(
                        B.bank(bk, TN), lhsT=W2[:, kc, f * 128:(f + 1) * 128], rhs=u[:, kc, :],
                        start=(kc == 0), stop=(kc == 31)), (hu_all + [hw2, bfree[bk]]) if kc == 0 else [], track=(kc == 31))
                ho = op("dve", lambda e, f=f, bk=bk, xi=xi: e.scalar_tensor_tensor(
                    out=oo[xi], in0=B.bank(bk, TN), scalar=modT[:, 40 + f:41 + f], in1=x1f[xi],
                    op0=ALU.mult, op1=ALU.add), [hm2, hxl, h_mod, F(("oo", xi))])
                bfree[bk] = ho
                fr_[("x1f", xi)] = ho
                hd = dma("sync", outT[f * 128:(f + 1) * 128, c0:c0 + TN], oo[xi], deps=[ho], dsem=ds_o[xi])
                fr_[("oo", xi)] = hd
            fr_["u"] = hm2
        for k in range(3):
            out_handles.append((ds_o[k][0], ds_o[k][1]))

    if "C" in phases:
        phase_C1()
        phase_C2()

    B.final_wait("sync", out_handles + B.barrier_handles())
    B.finish()
    return nc


def _host_consts():
    c = np.zeros((128, 6, 128), np.float32)
    c[:, 0, :] = np.eye(128)
    c[:, 1, :] = 1.0
    bd = np.zeros((128, 128), np.float32)
    bd[:64, :64] = 1.0 / 64
    bd[64:, 64:] = 1.0 / 64
    c[:, 2, :] = bd
    j = np.arange(128)
    c[:, 3, :] = (j[:, None] <= j[None, :]).astype(np.float32)
    c[:, 4, :] = (j[:, None] > j[None, :]).astype(np.float32)
    c[:, 5, :] = (j[:, None] >= j[None, :]).astype(np.float32)
    return c


def make_in_maps(inp):
    f = lambda a: np.ascontiguousarray(np.asarray(a, dtype=np.float32))
    x = f(inp["x"])
    c = f(inp["c"])
    shared = {
        "w_ada": f(inp["w_ada"][0]),
        "b_adaT": f(inp["b_ada"][0].reshape(48, 128).T),
        "n1w": f(inp["norm1_w"][0].reshape(8, 128).T),
        "n2w": f(inp["norm2_w"][0].reshape(8, 128).T),
        "w_in": f(inp["w_in"][0]),
        "convw": f(np.asarray(inp["conv_w"][0]).T.reshape(16, 128, 4).transpose(1, 0, 2)),
        "convb": f(np.asarray(inp["conv_b"][0]).reshape(16, 128).T),
        "dtb": f(np.broadcast_to(np.asarray(inp["dt_bias"][0])[None, :], (128, 16))),
        "alog": f(np.broadcast_to(np.asarray(inp["a_log"][0])[None, :], (128, 16))),
        "dskip": f(np.broadcast_to(np.asarray(inp["d_skip"][0])[None, :], (128, 16))),
        "ssdnw": f(np.broadcast_to(np.asarray(inp["ssd_norm_w"][0])[None, :], (128, 1024))),
        "qkw": f(np.stack([np.tile(np.asarray(inp["q_norm_w"][0]), 2),
                           np.tile(np.asarray(inp["k_norm_w"][0]), 2)], axis=1)),
        "attnw": f(np.broadcast_to(np.asarray(inp["attn_norm_w"][0])[None, :], (128, 1024))),
        "w_out": f(inp["w_out"][0]),
        "w_ff1": f(inp["w_ff1"][0]),
        "w_ff2": f(inp["w_ff2"][0]),
        "cmat": _host_consts(),
    }
    maps = []
    for b in range(8):
        m = dict(shared)
        m["xT"] = f(x[b].T)
        m["cT"] = f(c[b].reshape(8, 128).T)
        maps.append(m)
    return maps


_NC_CACHE = {}


def kernel(**inputs):
    key = (PHASES, DEBUG)
    if key not in _NC_CACHE:
        _NC_CACHE[key] = build_program(PHASES, DEBUG)
    nc = _NC_CACHE[key]
    in_maps = make_in_maps(inputs)
    res = run_bass_kernel_spmd(nc, in_maps, core_ids=list(range(8)))
    out = np.stack([np.ascontiguousarray(r["outT"].T) for r in res.results], axis=0)
    return out.astype(np.float32)
```

```python
import numpy as np
from contextlib import ExitStack
import concourse.bass as bass
import concourse.mybir as mybir
from concourse.bass_utils import run_bass_kernel_spmd

F32 = mybir.dt.float32
BF16 = mybir.dt.bfloat16
U8 = mybir.dt.uint8
AF = mybir.ActivationFunctionType
ALU = mybir.AluOpType
AX = mybir.AxisListType

S = 4096
D = 1024
NT = 8
TT = 512
INW = 6160
EPS = 1e-6
DSIZE = {F32: 4, BF16: 2, U8: 1}

PHASES = "MASBC"
DEBUG = False
GROUPS = "dqxzv"
NTILES = 8


class Builder:
    def __init__(self, nc):
        self.nc = nc
        self.es = ExitStack()
        self.engs = ["sync", "act", "dve", "pool", "pe"]
        self.q = {n: [] for n in self.engs}
        self.cnt = {n: 0 for n in self.engs}
        self.waited = {n: {} for n in self.engs}
        self.sem = {n: self.es.enter_context(nc.semaphore("prog_" + n)) for n in self.engs}
        self.arena = self.es.enter_context(nc.sbuf_tensor("arena", [128, 204 * 1024], U8))
        self.psum = self.es.enter_context(nc.psum_tensor("psum", [128, 4096], F32))
        self.aoff = 0
        self.nsem = 0

    def alloc(self, free_shape, dtype):
        n = int(np.prod(free_shape)) * DSIZE[dtype]
        off = (self.aoff + 63) // 64 * 64
        assert off + n <= 204 * 1024, ("SBUF arena overflow", off, n)
        self.aoff = off + n
        ap = self.arena[:, off:off + n].bitcast(dtype)
        if len(free_shape) == 2:
            ap = ap.rearrange("p (a b) -> p a b", a=free_shape[0])
        elif len(free_shape) == 3:
            ap = ap.rearrange("p (a b c) -> p a b c", a=free_shape[0], b=free_shape[1])
        elif len(free_shape) == 4:
            ap = ap.rearrange("p (a b c d) -> p a b c d", a=free_shape[0], b=free_shape[1], c=free_shape[2])
        return ap

    def mark(self):
        return self.aoff

    def release(self, m):
        self.aoff = m

    def bank(self, b, n=512, dtype=F32):
        ap = self.psum[:, b * 512:(b + 1) * 512]
        if dtype == BF16:
            return ap.bitcast(BF16)[:, 0:n]
        return ap[:, 0:n]

    def new_sem(self, name):
        self.nsem += 1
        return self.es.enter_context(self.nc.semaphore(f"{name}_{self.nsem}"))

    def _waits(self, eng, deps):
        waits = []
        for d in deps:
            if d is None:
                continue
            if isinstance(d, list):
                for dd in d:
                    waits += self._waits(eng, [dd])
                continue
            s, v = d
            key = s.num
            if self.waited[eng].get(key, 0) < v:
                self.waited[eng][key] = v
                waits.append((s, v))
        return waits

    def op(self, eng, fn, deps=(), track=True):
        waits = self._waits(eng, deps)
        h = None
        sem = self.sem[eng]
        if track:
            self.cnt[eng] += 1
            h = (sem, self.cnt[eng])

        def run(e, fn=fn, waits=waits, track=track, sem=sem):
            for s, v in waits:
                e.wait_ge(s, v)
            ins = fn(e)
            if track:
                ins.then_inc(sem, 1)
        self.q[eng].append(run)
        return h

    def dma(self, eng, out, in_, deps=(), dsem=None, **kw):
        waits = self._waits(eng, deps)
        dsem[1] += 16
        h = (dsem[0], dsem[1])
        sem = dsem[0]

        def run(e, waits=waits, sem=sem, out=out, in_=in_, kw=kw):
            for s, v in waits:
                e.wait_ge(s, v)
            e.dma_start(out=out, in_=in_, **kw).then_inc(sem, 16)
        self.q[eng].append(run)
        return h

    def dsem(self, name):
        return [self.new_sem(name), 0]

    def final_wait(self, eng, deps):
        waits = self._waits(eng, deps)

        def run(e, waits=waits):
            for s, v in waits:
                e.wait_ge(s, v)
        self.q[eng].append(run)

    def phase_barrier(self, extra=()):
        hs = self.barrier_handles() + list(extra)
        for n in self.engs:
            self.final_wait(n, hs)

    def barrier_handles(self):
        return [(self.sem[n], self.cnt[n]) for n in ["act", "dve", "pool", "pe"] if self.cnt[n] > 0]

    def finish(self):
        nc = self.nc
        with nc.Block() as block:
            @block.sync
            def _(e):
                for f in self.q["sync"]:
                    f(e)

            @block.scalar
            def _(e):
                for f in self.q["act"]:
                    f(e)

            @block.vector
            def _(e):
                for f in self.q["dve"]:
                    f(e)

            @block.gpsimd
            def _(e):
                for f in self.q["pool"]:
                    f(e)

            @block.tensor
            def _(e):
                for f in self.q["pe"]:
                    f(e)
        self.es.close()


def build_program(phases=PHASES, debug=DEBUG):
    nc = bass.Bass("TRN2", target_bir_lowering=False)
    dr = {}

    def din(name, shape, dt=F32):
        dr[name] = nc.dram_tensor(name, list(shape), dt, kind="ExternalInput").ap()
        return dr[name]

    def dscr(name, shape, dt):
        kind = "ExternalOutput" if (debug and name in debug) else "Internal"
        dr[name] = nc.dram_tensor(name, list(shape), dt, kind=kind).ap()
        return dr[name]

    xT = din("xT", [D, S])
    cT = din("cT", [128, 8])
    w_ada = din("w_ada", [D, 6 * D])
    b_adaT = din("b_adaT", [128, 48])
    n1w = din("n1w", [128, 8])
    n2w = din("n2w", [128, 8])
    w_in = din("w_in", [D, INW])
    convw = din("convw", [128, 16, 4])
    convb = din("convb", [128, 16])
    dtb = din("dtb", [128, 16])
    alog = din("alog", [128, 16])
    dskip = din("dskip", [128, 16])
    ssdnw = din("ssdnw", [128, 1024])
    qkw = din("qkw", [128, 2])
    attnw = din("attnw", [128, 1024])
    w_out = din("w_out", [2 * D, D])
    w_ff1 = din("w_ff1", [D, 4 * D])
    w_ff2 = din("w_ff2", [4 * D, D])
    cmat = din("cmat", [128, 6, 128])
    outT = nc.dram_tensor("outT", [D, S], F32, kind="ExternalOutput").ap()

    ZS = dscr("ZS", [S, 1024], BF16)
    XS = dscr("XS", [S, 1024], BF16)
    BM = dscr("BM", [S, 512], BF16)
    BCT = dscr("BCT", [1024, S], BF16)
    QT = dscr("QT", [1024, S], BF16)
    KT = dscr("KT", [1024, S], BF16)
    VA = dscr("VA", [S, 16 * 65], BF16)
    DTS = dscr("DTS", [S, 32], F32)
    YS = dscr("YS", [S, 1024], BF16)
    OD = dscr("OD", [3, S, 16 * 65], F32)
    X1T = dscr("X1T", [D, S], F32)
    H2T = dscr("H2T", [D, S], BF16)
    MODT = dscr("MODT", [128, 48], F32)
    WoB = dscr("WoB", [2 * D, D], BF16)
    W1B = dscr("W1B", [D, 4 * D], BF16)
    W2B = dscr("W2B", [4 * D, D], BF16)
    wcast = {"list": [], "sem": {}}

    B = Builder(nc)
    op, dma = B.op, B.dma

    modT = B.alloc([48], F32)
    s1 = B.alloc([8], F32)
    s2 = B.alloc([8], F32)
    ident = B.alloc([128], BF16)
    ones_bf = B.alloc([128], BF16)
    blk64 = B.alloc([128], BF16)
    ones_f = B.alloc([128], F32)
    cst_f = B.alloc([6, 128], F32)
    convw_sb = B.alloc([16, 4], F32)
    convb_sb = B.alloc([16], F32)
    qkw_sb = B.alloc([2], F32)
    dtb_sb = B.alloc([16], F32)
    A_sb = B.alloc([16], F32)
    dskip_sb = B.alloc([16], F32)
    n1w_sb = B.alloc([8], F32)
    n2w_sb = B.alloc([8], F32)
    persist_mark = B.mark()

    ds_c = B.dsem("const")
    hc = []
    hc.append(dma("sync", cst_f, cmat, dsem=ds_c))
    hc.append(dma("sync", convw_sb, convw, dsem=ds_c))
    hc.append(dma("sync", convb_sb, convb, dsem=ds_c))
    hc.append(dma("sync", qkw_sb, qkw, dsem=ds_c))
    hc.append(dma("sync", dtb_sb, dtb, dsem=ds_c))
    hc.append(dma("sync", A_sb, alog, dsem=ds_c))
    hc.append(dma("sync", dskip_sb, dskip, dsem=ds_c))
    hc.append(dma("sync", n1w_sb, n1w, dsem=ds_c))
    hc.append(dma("sync", n2w_sb, n2w, dsem=ds_c))
    hconst = hc[-1]

    h_id = op("dve", lambda e: e.tensor_copy(out=ident, in_=cst_f[:, 0, :]), [hconst])
    h_ones = op("dve", lambda e: e.tensor_copy(out=ones_bf, in_=cst_f[:, 1, :]), [hconst])
    h_blk = op("dve", lambda e: e.tensor_copy(out=blk64, in_=cst_f[:, 2, :]), [hconst])
    h_onesf = op("dve", lambda e: e.tensor_copy(out=ones_f, in_=cst_f[:, 1, :]), [hconst])
    h_A = op("act", lambda e: e.activation(out=A_sb, in_=A_sb, func=AF.Exp), [hconst])
    h_A = op("dve", lambda e: e.tensor_scalar_mul(out=A_sb, in0=A_sb, scalar1=-1.0), [h_A])
    h_qw = op("dve", lambda e: e.tensor_scalar_mul(out=qkw_sb[:, 0:1], in0=qkw_sb[:, 0:1], scalar1=0.125), [hconst])
    h_setup = [h_id, h_ones, h_blk, h_onesf, h_A, h_qw]

    out_handles = []

    Wsb = B.alloc([8, INW], BF16)
    pieces = [(0, 1540), (1540, 3080), (3080, 4620), (4620, 6160)]
    hW = []
    for (c0, c1) in pieces:
        dsw = B.dsem("W")
        h = None
        for kc in range(8):
            h = dma("pool", Wsb[:, kc, c0:c1], w_in[kc * 128:(kc + 1) * 128, c0:c1], dsem=dsw)
        hW.append(h)

    def wdeps(a, b_):
        return [hW[i] for i, (c0, c1) in enumerate(pieces) if a < c1 and b_ > c0]

    w_mark = B.mark()

    if "M" in phases:
        m0 = B.mark()
        c_sb = B.alloc([8], F32)
        cact = B.alloc([8], F32)
        bada = B.alloc([48], F32)
        wslM = [B.alloc([8, 1024], F32) for _ in range(2)]
        accM = [B.alloc([1024], F32) for _ in range(2)]
        ds_m = B.dsem("m_in")
        hcin = dma("sync", c_sb, cT, dsem=ds_m)
        hcin = dma("sync", bada, b_adaT, dsem=ds_m)
        h = op("act", lambda e: e.activation(out=cact, in_=c_sb, func=AF.Exp, scale=-1.0), [hcin])
        h = op("dve", lambda e: e.tensor_scalar_add(out=cact, in0=cact, scalar1=1.0), [h])
        h = op("dve", lambda e: e.reciprocal(out=cact, in_=cact), [h])
        hcact = op("dve", lambda e: e.tensor_mul(out=cact, in0=cact, in1=c_sb), [h])
        ds_w = [B.dsem("m_w0"), B.dsem("m_w1")]
        wfreeM = [None, None]
        accfreeM = [None, None]
        psM = B.bank(0, 48)
        hmm = None
        for sl in range(6):
            b = sl % 2
            hw = None
            for kc in range(8):
                hw = dma("sync", wslM[b][:, kc, :], w_ada[kc * 128:(kc + 1) * 128, sl * 1024:(sl + 1) * 1024],
                         deps=[wfreeM[b]], dsem=ds_w[b])
            eng = "dve"
            h = op(eng, lambda e, b=b: e.tensor_scalar_mul(out=accM[b], in0=wslM[b][:, 0, :], scalar1=cact[:, 0:1]),
                   [hw, hcact, accfreeM[b]])
            for kc in range(1, 8):
                h = op(eng, lambda e, b=b, kc=kc: e.scalar_tensor_tensor(
                    out=accM[b], in0=wslM[b][:, kc, :], scalar=cact[:, kc:kc + 1], in1=accM[b],
                    op0=ALU.mult, op1=ALU.add), [h])
            wfreeM[b] = h
            for jt in range(8):
                col = sl * 8 + jt
                hmm = op("pe", lambda e, b=b, jt=jt, col=col: e.matmul(
                    psM[:, col:col + 1], lhsT=accM[b][:, jt * 128:(jt + 1) * 128], rhs=ones_f[:, 0:1],
                    start=True, stop=True), [h, h_onesf])
            accfreeM[b] = hmm
        hmod = op("dve", lambda e: e.tensor_add(out=modT, in0=psM, in1=bada), [hmm, hcin])
        h = op("dve", lambda e: e.scalar_tensor_tensor(out=s1, in0=modT[:, 8:16], scalar=1.0, in1=n1w_sb,
                                                      op0=ALU.add, op1=ALU.mult), [hmod, hconst])
        hmod2 = op("dve", lambda e: e.scalar_tensor_tensor(out=s2, in0=modT[:, 32:40], scalar=1.0, in1=n2w_sb,
                                                          op0=ALU.add, op1=ALU.mult), [h])
        ds_mo = B.dsem("m_out")
        out_handles.append(dma("sync", MODT, modT, deps=[hmod2], dsem=ds_mo))
        h_mod = hmod2
        B.release(m0)
    else:
        h_mod = None


    if "A" in phases:
        mA = B.mark()
        B.phase_barrier(out_handles)
        xt = B.alloc([8, 512], F32)
        sq = [B.alloc([512], BF16) for _ in range(2)]
        h1 = [B.alloc([8, 512], BF16) for _ in range(2)]
        rstd = B.alloc([512], F32)
        xr = [B.alloc([515], F32) for _ in range(3)]
        acc = [B.alloc([512], F32) for _ in range(3)]
        xc = [B.alloc([512], BF16) for _ in range(5)]
        halo = B.alloc([16, 3], F32)
        qsq = [B.alloc([512], BF16) for _ in range(2)]
        qraw = [B.alloc([512], F32) for _ in range(2)]
        qrs = [B.alloc([512], F32) for _ in range(2)]
        qn = [B.alloc([512], BF16) for _ in range(3)]
        zs = [B.alloc([512], BF16) for _ in range(3)]
        VAst = B.alloc([4, 16, 65], BF16)
        XSst = B.alloc([4, 1024], BF16)
        BMst = B.alloc([4, 512], BF16)
        tdt = B.alloc([4, 16], F32)
        adt = B.alloc([4, 16], F32)
        dts = B.alloc([4, 32], F32)

        h_va1 = op("pool", lambda e: e.memset(VAst, 1.0))
        h_halo0 = op("pool", lambda e: e.memset(halo, 0.0))
        ds_x = B.dsem("x")
        dsd = {}

        def sd(key):
            if key not in dsd:
                dsd[key] = B.dsem("o")
            return dsd[key]
        bank_free = {b: None for b in range(8)}
        free = {}

        def fr(name):
            return free.get(name)

        state = {"xt_free": None, "h1": [None, None], "h1_free": [None, None]}
        phaseA_out = []

        def prep(i):
            b = i % 2
            hx = None
            for kc in range(8):
                hx = dma("sync", xt[:, kc, :], xT[kc * 128:(kc + 1) * 128, i * 512:(i + 1) * 512],
                         deps=[state["xt_free"]], dsem=ds_x)
            hst = None
            for kc in range(8):
                sl = kc % 2
                hs = op("pool", lambda e, kc=kc, sl=sl: e.tensor_tensor(out=sq[sl], in0=xt[:, kc, :], in1=xt[:, kc, :],
                                                                      op=ALU.mult), [hx, fr(("sq", sl))])
                hst = op("pe", lambda e, kc=kc, sl=sl: e.matmul(B.bank(0), lhsT=ones_bf, rhs=sq[sl],
                                                               start=(kc == 0), stop=(kc == 7)),
                         [hs, h_ones, bank_free[0] if kc == 0 else None])
                free[("sq", sl)] = hst
            h = op("act", lambda e: e.activation(out=rstd, in_=B.bank(0), func=AF.Ln, scale=1.0 / D, bias=EPS),
                   [hst, fr("rstd")])
            bank_free[0] = h
            h = op("act", lambda e: e.activation(out=rstd, in_=rstd, func=AF.Exp, scale=-0.5), [h])
            hh = None
            for kc in range(8):
                hn = op("dve", lambda e, kc=kc: e.tensor_mul(out=xt[:, kc, :], in0=xt[:, kc, :], in1=rstd), [h, hst])
                hh = op("pool", lambda e, kc=kc, b=b: e.tensor_scalar(
                    out=h1[b][:, kc, :], in0=xt[:, kc, :], scalar1=s1[:, kc:kc + 1], scalar2=modT[:, kc:kc + 1],
                    op0=ALU.mult, op1=ALU.add), [hn, h_mod, state["h1_free"][b]])
            free["rstd"] = hn
            state["xt_free"] = hh
            state["h1"][b] = hh

        rot = {"main": 0, "qk": 0, "x3": 0, "qn": 0, "zs": 0, "tp": 0}
        MAIN_BANKS = [2, 3, 4, 5]

        def next_bank():
            b = MAIN_BANKS[rot["main"] % len(MAIN_BANKS)]
            rot["main"] += 1
            return b

        def mm_group_f(i, col0):
            b = i % 2
            bk = next_bank()
            h = None
            for kc in range(8):
                h = op("pe", lambda e, kc=kc, bk=bk, b=b: e.matmul(
                    B.bank(bk), lhsT=Wsb[:, kc, col0:col0 + 128], rhs=h1[b][:, kc, :],
                    start=(kc == 0), stop=(kc == 7)),
                    ([state["h1"][b], bank_free[bk]] + wdeps(col0, col0 + 128)) if kc == 0 else [],
                    track=(kc == 7))
            return bk, h

        def mm_group_t(i, st, col0, ncols):
            b = i % 2
            bk = next_bank()
            h = None
            for kc in range(8):
                h = op("pe", lambda e, kc=kc, bk=bk, b=b: e.matmul(
                    B.bank(bk, ncols), lhsT=h1[b][:, kc, st * 128:(st + 1) * 128], rhs=Wsb[:, kc, col0:col0 + ncols],
                    start=(kc == 0), stop=(kc == 7)),
                    ([state["h1"][b], bank_free[bk]] + wdeps(col0, col0 + ncols)) if kc == 0 else [],
                    track=(kc == 7))
            return bk, h

        for i in range(NTILES):
            if i == 0:
                prep(0)
            b = i % 2
            c0t = i * 512
            for st in (range(4) if "d" in GROUPS else []):
                bk, hm = mm_group_t(i, st, 3072, 16)
                h = op("dve", lambda e, bk=bk, st=st: e.tensor_add(out=tdt[:, st, :], in0=B.bank(bk, 16), in1=dtb_sb),
                       [hm, hconst, fr("dts")])
                bank_free[bk] = h
                h2 = op("act", lambda e, st=st: e.activation(out=adt[:, st, :], in_=tdt[:, st, :], func=AF.Abs), [h])
                h2 = op("act", lambda e, st=st: e.activation(out=adt[:, st, :], in_=adt[:, st, :], func=AF.Exp, scale=-1.0), [h2])
                h2 = op("act", lambda e, st=st: e.activation(out=adt[:, st, :], in_=adt[:, st, :], func=AF.Ln, bias=1.0), [h2])
                h2 = op("dve", lambda e, st=st: e.scalar_tensor_tensor(out=dts[:, st, 0:16], in0=tdt[:, st, :], scalar=0.0,
                                                                      in1=adt[:, st, :], op0=ALU.max, op1=ALU.add), [h2])
                h2 = op("dve", lambda e, st=st: e.tensor_mul(out=dts[:, st, 16:32], in0=dts[:, st, 0:16], in1=A_sb), [h2, h_A])
            if "d" in GROUPS:
                free["dts"] = dma("sync", DTS[c0t:c0t + 512, :].rearrange("(st p) c -> p st c", p=128), dts, deps=[h2], dsem=sd("dts"))
            pend_qk = []
            for j in (range(16) if "q" in GROUPS else range(4)):
                if "q" not in GROUPS:
                    if j == 3 and i + 1 < NTILES:
                        prep(i + 1)
                    continue
                isq = j < 8
                col0 = (3088 if isq else 4112) + (j % 8) * 128
                bk, hm = mm_group_f(i, col0)
                s2_ = rot["qk"] % 2
                rot["qk"] += 1
                hs = op("act", lambda e, bk=bk, s2_=s2_: e.activation(out=qsq[s2_], in_=B.bank(bk), func=AF.Square),
                        [hm, fr(("qsq", s2_))])
                hr = op("act", lambda e, bk=bk, s2_=s2_: e.activation(out=qraw[s2_], in_=B.bank(bk), func=AF.Copy),
                        [hm, hs, fr(("qraw", s2_))])
                bank_free[bk] = [hs, hr]

                def qk_part2(s2_=s2_, hs=hs, hr=hr, isq=isq, j=j):
                    hp = op("pe", lambda e: e.matmul(B.bank(1), lhsT=blk64, rhs=qsq[s2_], start=True, stop=True),
                            [hs, h_blk, bank_free[1]])
                    free[("qsq", s2_)] = hp
                    hl = op("act", lambda e: e.activation(out=qrs[s2_], in_=B.bank(1), func=AF.Ln, bias=EPS),
                            [hp, fr(("qrs", s2_))])
                    bank_free[1] = hl
                    hl = op("act", lambda e: e.activation(out=qrs[s2_], in_=qrs[s2_], func=AF.Exp, scale=-0.5), [hl])
                    s3 = rot["qn"] % 3
                    rot["qn"] += 1
                    wc = 0 if isq else 1
                    hq = op("dve", lambda e: e.scalar_tensor_tensor(
                        out=qn[s3], in0=qraw[s2_], scalar=qkw_sb[:, wc:wc + 1], in1=qrs[s2_], op0=ALU.mult, op1=ALU.mult),
                        [hl, hr, h_qw, fr(("qn", s3))])
                    free[("qraw", s2_)] = hq
                    free[("qrs", s2_)] = hq
                    dst = (QT if isq else KT)[(j % 8) * 128:(j % 8 + 1) * 128, c0t:c0t + 512]
                    free[("qn", s3)] = dma("sync", dst, qn[s3], deps=[hq], dsem=sd(("qn", s3)))
                if pend_qk:
                    pend_qk.pop(0)()
                pend_qk.append(qk_part2)
                if j == 3 and i + 1 < NTILES:
                    prep(i + 1)
            while pend_qk:
                pend_qk.pop(0)()
            pend_x = []
            for ct in (range(16) if "x" in GROUPS else []):
                col0 = 1024 + ct * 128
                bk, hm = mm_group_f(i, col0)
                s3 = rot["x3"] % 3
                rot["x3"] += 1
                ha = op("act", lambda e, bk=bk, s3=s3, ct=ct: e.activation(
                    out=acc[s3], in_=B.bank(bk), func=AF.Identity, scale=convw_sb[:, ct, 3:4], bias=convb_sb[:, ct:ct + 1]),
                    [hm, hconst, fr(("acc", s3))])
                hc_ = op("act", lambda e, bk=bk, s3=s3: e.activation(out=xr[s3][:, 3:515], in_=B.bank(bk), func=AF.Copy),
                         [hm, fr(("xr", s3))])
                bank_free[bk] = hc_
                hh1 = op("pool", lambda e, s3=s3, ct=ct: e.tensor_copy(out=xr[s3][:, 0:3], in_=halo[:, ct, :]),
                         [fr(("xr", s3)), h_halo0, fr(("halo", ct))])
                hh2 = op("pool", lambda e, s3=s3, ct=ct: e.tensor_copy(out=halo[:, ct, :], in_=xr[s3][:, 512:515]), [hc_, hh1])
                free[("halo", ct)] = hh2
                ht = ha
                for k in range(3):
                    ht = op("dve", lambda e, s3=s3, ct=ct, k=k: e.scalar_tensor_tensor(
                        out=acc[s3], in0=xr[s3][:, k:k + 512], scalar=convw_sb[:, ct, k:k + 1], in1=acc[s3],
                        op0=ALU.mult, op1=ALU.add), [ht, hc_, hh1])
                free[("xr", s3)] = [ht, hh2]
                hx_ = op("act", lambda e, s3=s3: e.activation(out=xc[s3], in_=acc[s3], func=AF.Silu),
                         [ht, fr(("xc", s3))])
                free[("acc", s3)] = hx_
                def x_part2(ct=ct, s3=s3, hx_=hx_):
                    last = [hx_]
                    if ct < 12:
                        tpi = rot["tp"] % 2
                        rot["tp"] += 1
                        tp = B.bank(6 + tpi, 1024, BF16)[:, 0:512]
                        hp = None
                        for st in range(4):
                            hp = op("pe", lambda e, st=st: e.transpose(
                                out=tp[:, st * 128:(st + 1) * 128], in_=xc[s3][:, st * 128:(st + 1) * 128], identity=ident),
                                [hx_, h_id, fr(("tp", tpi))] if st == 0 else [], track=(st == 3))
                        if ct < 8:
                            dst_ap = XSst[:, :, ct * 128:(ct + 1) * 128]
                            fkey = "XSst"
                        else:
                            dst_ap = BMst[:, :, (ct - 8) * 128:(ct - 7) * 128]
                            fkey = "BMst"
                        he = op("dve", lambda e: e.tensor_copy(
                            out=dst_ap, in_=tp.rearrange("p (s c) -> p s c", s=4)), [hp, fr(fkey)])
                        free[("tp", tpi)] = he
                        last.append(hp)
                        free.setdefault("st_writes", []).append(he)
                    if ct >= 8:
                        hd = dma("sync", BCT[(ct - 8) * 128:(ct - 7) * 128, c0t:c0t + 512], xc[s3], deps=[hx_],
                                 dsem=sd(("xc", s3)))
                        last.append(hd)
                    free[("xc", s3)] = last
                pend_x.append(x_part2)
                if len(pend_x) > 2:
                    pend_x.pop(0)()
            while pend_x:
                pend_x.pop(0)()
            if "x" not in GROUPS:
                free["st_writes"] = []
            hd = dma("sync", XS[c0t:c0t + 512, :].rearrange("(st p) c -> p st c", p=128), XSst,
                     deps=free["st_writes"], dsem=sd("XSst"))
            free["XSst"] = hd
            phaseA_out.append(hd)
            hd = dma("sync", BM[c0t:c0t + 512, :].rearrange("(st p) c -> p st c", p=128), BMst,
                     deps=free["st_writes"], dsem=sd("BMst"))
            free["BMst"] = hd
            phaseA_out.append(hd)
            free["st_writes"] = []
            for st in (range(4) if "z" in GROUPS else []):
                for half in range(2):
                    bk, hm = mm_group_t(i, st, half * 512, 512)
                    s3 = rot["zs"] % 3
                    rot["zs"] += 1
                    hz = op("act", lambda e, bk=bk, s3=s3: e.activation(out=zs[s3], in_=B.bank(bk), func=AF.Silu),
                            [hm, fr(("zs", s3))])
                    bank_free[bk] = hz
                    r0 = c0t + st * 128
                    free[("zs", s3)] = dma("sync", ZS[r0:r0 + 128, half * 512:(half + 1) * 512], zs[s3], deps=[hz], dsem=sd(("zs", s3)))
                    phaseA_out.append(free[("zs", s3)])
            hv = []
            for st in (range(4) if "v" in GROUPS else []):
                for half in range(2):
                    bk, hm = mm_group_t(i, st, 5136 + half * 512, 512)
                    h = op("dve", lambda e, bk=bk, st=st, half=half: e.tensor_copy(
                        out=VAst[:, st, half * 8:(half + 1) * 8, 0:64],
                        in_=B.bank(bk).rearrange("p (h d) -> p h d", h=8)), [hm, h_va1, fr("VAst")])
                    bank_free[bk] = h
                    hv.append(h)
            free["VAst"] = dma("sync", VA[c0t:c0t + 512, :].rearrange("(st p) c -> p st c", p=128),
                               VAst.rearrange("p s h d -> p s (h d)"), deps=hv, dsem=sd("VAst"))
            phaseA_out.append(free["VAst"])
            state["h1_free"][b] = (B.sem["pe"], B.cnt["pe"])
        phaseA_done = [(v[0], v[1]) for v in dsd.values()]
        out_handles += phaseA_done
        B.release(mA)


    def phase_S():
        B.release(persist_mark)
        mS = B.mark()
        B.phase_barrier(out_handles)
        NCH = 32
        tri_f = cst_f[:, 3, :]
        strict_f = cst_f[:, 4, :]
        tri_b = B.alloc([128], BF16)
        ssdnw_sb = B.alloc([1024], F32)
        dts_all = B.alloc([NCH, 32], F32)
        xs_ = [B.alloc([16, 64], BF16) for _ in range(2)]
        bm_ = [B.alloc([512], BF16) for _ in range(2)]
        bct_ = [B.alloc([8, 128], BF16) for _ in range(2)]
        zs_ = [B.alloc([1024], BF16) for _ in range(2)]
        R = B.alloc([16, 128], F32)
        Lt = [B.alloc([16, 128], BF16) for _ in range(2)]
        cbm = [B.alloc([4, 128], BF16) for _ in range(2)]
        G = [B.alloc([16, 128], BF16) for _ in range(2)]
        xdt = [B.alloc([16, 64], BF16) for _ in range(2)]
        xdec = [B.alloc([16, 64], BF16) for _ in range(2)]
        t1 = [B.alloc([16, 64], F32) for _ in range(2)]
        yo = [B.alloc([1024], BF16) for _ in range(2)]
        sqj = B.alloc([256], BF16)
        ed = [B.alloc([32], F32) for _ in range(2)]
        ss = [B.alloc([4], F32) for _ in range(2)]
        Sst = B.alloc([16, 64], F32)
        Sb = B.alloc([16, 64], BF16)

        ds0 = B.dsem("s_c")
        hc0 = dma("sync", ssdnw_sb, ssdnw, dsem=ds0)
        hc0 = dma("sync", dts_all, DTS.rearrange("(c p) k -> p c k", p=128), dsem=ds0)
        h_trib = op("dve", lambda e: e.tensor_copy(out=tri_b, in_=tri_f))
        h_s0 = op("pool", lambda e: e.memset(Sst, 0.0))
        h_sb0 = op("pool", lambda e: e.memset(Sb, 0.0))
        ds_l = [B.dsem("s_l0"), B.dsem("s_l1")]
        ds_y = [B.dsem("s_y0"), B.dsem("s_y1")]
        fr_ = {}
        bfree = {b: None for b in range(8)}

        def F(k):
            return fr_.get(k)

        hS = h_s0
        hSb = h_sb0
        for c in range(NCH):
            b = c % 2
            r0 = c * 128
            hl = dma("sync", xs_[b].rearrange("p h d -> p (h d)"), XS[r0:r0 + 128, :], deps=[F(("ld", b))], dsem=ds_l[b])
            hl = dma("sync", bm_[b], BM[r0:r0 + 128, :], dsem=ds_l[b])
            hl = dma("sync", bct_[b], BCT[:, r0:r0 + 128].rearrange("(g p) t -> p g t", p=128), dsem=ds_l[b])
            hl = dma("sync", zs_[b], ZS[r0:r0 + 128, :], dsem=ds_l[b])
            dA = dts_all[:, c, 16:32]
            dtc = dts_all[:, c, 0:16]
            hR = op("pool", lambda e, dA=dA: e.tensor_tensor(
                out=R, in0=tri_f.unsqueeze(1).to_broadcast([128, 16, 128]),
                in1=dA.unsqueeze(2).to_broadcast([128, 16, 128]), op=ALU.mult), [hc0, hconst, F("R")])
            hseg = []
            for q4 in range(4):
                hseg.append(op("pe", lambda e, q4=q4: e.matmul(
                    B.bank(q4), lhsT=strict_f, rhs=R[:, q4 * 4:(q4 + 1) * 4, :].rearrange("p h l -> p (h l)"),
                    start=True, stop=True), [hR, bfree[q4]]))
            fr_["R"] = hseg[-1]
            hsm = op("pe", lambda e, dA=dA: e.matmul(B.bank(5, 16), lhsT=tri_f, rhs=dA, start=True, stop=True),
                     [hc0, bfree[5]], track=False)
            hsm = op("pe", lambda e, dA=dA: e.matmul(B.bank(5, 32)[:, 16:32], lhsT=ones_f, rhs=dA, start=True, stop=True),
                     [h_onesf])
            hL = None
            for q4 in range(4):
                hL = op("act", lambda e, q4=q4, b=b: e.activation(
                    out=Lt[b][:, q4 * 4:(q4 + 1) * 4, :].rearrange("p h l -> p (h l)"), in_=B.bank(q4), func=AF.Exp),
                    [hseg[q4], F(("Lt", b))])
                bfree[q4] = hL
            hed = op("act", lambda e, b=b: e.activation(out=ed[b], in_=B.bank(5, 32), func=AF.Exp), [hsm, F(("ed", b))])
            bfree[5] = hed
            hcb = None
            for g in range(4):
                hcb = op("pe", lambda e, g=g, b=b: e.matmul(
                    B.bank(4)[:, g * 128:(g + 1) * 128], lhsT=bct_[b][:, g, :], rhs=bct_[b][:, 4 + g, :],
                    start=True, stop=True), [hl, bfree[4]] if g == 0 else [], track=(g == 3))
            hcbm = op("dve", lambda e, b=b: e.tensor_tensor(
                out=cbm[b], in0=B.bank(4).rearrange("p (g l) -> p g l", g=4),
                in1=tri_b.unsqueeze(1).to_broadcast([128, 4, 128]), op=ALU.mult), [hcb, h_trib, F(("cbm", b))])
            bfree[4] = hcbm
            hG = None
            for g in range(4):
                hG = op("dve", lambda e, g=g, b=b: e.tensor_tensor(
                    out=G[b][:, 4 * g:4 * g + 4, :], in0=Lt[b][:, 4 * g:4 * g + 4, :],
                    in1=cbm[b][:, g, :].unsqueeze(1).to_broadcast([128, 4, 128]), op=ALU.mult),
                    [hL, hcbm, F(("G", b))])
            hxdt = op("pool", lambda e, b=b, dtc=dtc: e.tensor_tensor(
                out=xdt[b], in0=xs_[b], in1=dtc.unsqueeze(2).to_broadcast([128, 16, 64]), op=ALU.mult),
                [hl, hc0, F(("xdt", b))])
            hxdec = op("pool", lambda e, b=b: e.tensor_tensor(
                out=xdec[b], in0=xdt[b], in1=Lt[b][:, :, 127:128].to_broadcast([128, 16, 64]), op=ALU.mult),
                [hxdt, hL, F(("xdec", b))])
            hyd = None
            for h_ in range(16):
                hyd = op("pe", lambda e, h_=h_, b=b: e.matmul(
                    B.psum[:, 3072 + h_ * 64:3072 + (h_ + 1) * 64], lhsT=G[b][:, h_, :], rhs=xdt[b][:, h_, :],
                    start=True, stop=True), [hG, hxdt, bfree[6], bfree[7]] if h_ == 0 else [], track=(h_ == 15))
            fr_[("G", b)] = hyd
            hyo = None
            for g in range(4):
                hyo = op("pe", lambda e, g=g, b=b: e.matmul(
                    B.psum[:, g * 256:(g + 1) * 256], lhsT=bct_[b][:, 4 + g, :],
                    rhs=Sb[:, 4 * g:4 * g + 4, :].rearrange("p h d -> p (h d)"), start=True, stop=True),
                    [hSb, bfree[0], bfree[1]] if g == 0 else [], track=(g == 3))
            hst = None
            for g in range(4):
                hst = op("pe", lambda e, g=g, b=b: e.matmul(
                    B.psum[:, 1024 + g * 256:1024 + (g + 1) * 256], lhsT=bm_[b][:, g * 128:(g + 1) * 128],
                    rhs=xdec[b][:, 4 * g:4 * g + 4, :].rearrange("p h d -> p (h d)"), start=True, stop=True),
                    [hxdec, bfree[2], bfree[3]] if g == 0 else [], track=(g == 3))
            fr_[("xdec", b)] = hst
            fr_[("xdt", b)] = [hyd, hxdec]
            fr_[("Lt", b)] = [hG, hxdec]
            fr_[("cbm", b)] = hG
            yoff = B.psum[:, 0:1024].rearrange("p (h d) -> p h d", h=16)
            ydg = B.psum[:, 3072:4096].rearrange("p (h d) -> p h d", h=16)
            h1_ = op("dve", lambda e, b=b: e.tensor_tensor(
                out=t1[b], in0=yoff, in1=ed[b][:, 0:16].unsqueeze(2).to_broadcast([128, 16, 64]), op=ALU.mult),
                [hyo, hed, F(("t1", b))])
            bfree[0] = h1_
            bfree[1] = h1_
            h2_ = op("dve", lambda e, b=b: e.tensor_tensor(out=t1[b], in0=t1[b], in1=ydg, op=ALU.add), [h1_, hyd])
            bfree[6] = h2_
            bfree[7] = h2_
            h3_ = op("dve", lambda e, b=b: e.tensor_tensor(
                out=yo[b].rearrange("p (h d) -> p h d", h=16), in0=xs_[b],
                in1=dskip_sb.unsqueeze(2).to_broadcast([128, 16, 64]), op=ALU.mult), [hl, hconst, F(("yo", b))])
            h3_ = op("dve", lambda e, b=b: e.tensor_tensor(
                out=t1[b], in0=t1[b], in1=yo[b].rearrange("p (h d) -> p h d", h=16), op=ALU.add), [h2_, h3_])
            h4_ = op("pool", lambda e, b=b: e.tensor_tensor(
                out=t1[b], in0=t1[b], in1=zs_[b].rearrange("p (h d) -> p h d", h=16), op=ALU.mult), [h3_, hl])
            t1f = t1[b].rearrange("p h d -> p (h d)")
            hq_ = None
            for g in range(4):
                hq_ = op("act", lambda e, g=g, b=b, t1f=t1f: e.activation(
                    out=sqj, in_=t1f[:, g * 256:(g + 1) * 256], func=AF.Square, accum_out=ss[b][:, g:g + 1]),
                    [h4_, hq_, F(("ss", b))])
            hq_ = op("act", lambda e, b=b: e.activation(out=ss[b], in_=ss[b], func=AF.Ln, scale=1.0 / 256, bias=EPS), [hq_])
            hq_ = op("act", lambda e, b=b: e.activation(out=ss[b], in_=ss[b], func=AF.Exp, scale=-0.5), [hq_])
            hy = None
            for g in range(4):
                hy = op("dve", lambda e, g=g, b=b, t1f=t1f: e.scalar_tensor_tensor(
                    out=yo[b][:, g * 256:(g + 1) * 256], in0=t1f[:, g * 256:(g + 1) * 256], scalar=ss[b][:, g:g + 1],
                    in1=ssdnw_sb[:, g * 256:(g + 1) * 256], op0=ALU.mult, op1=ALU.mult), [hq_, h4_, hc0])
            fr_[("ss", b)] = hy
            fr_[("t1", b)] = hy
            hdy = dma("sync", YS[r0:r0 + 128, :], yo[b], deps=[hy], dsem=ds_y[b])
            fr_[("yo", b)] = hdy
            hu = op("dve", lambda e, b=b: e.tensor_tensor(
                out=Sst, in0=Sst, in1=ed[b][:, 16:32].unsqueeze(2).to_broadcast([128, 16, 64]), op=ALU.mult),
                [hS, hed])
            hu = op("dve", lambda e: e.tensor_tensor(
                out=Sst, in0=Sst, in1=B.psum[:, 1024:2048].rearrange("p (h d) -> p h d", h=16), op=ALU.add), [hu, hst])
            bfree[2] = hu
            bfree[3] = hu
            hS = hu
            hSb = op("dve", lambda e: e.tensor_copy(out=Sb, in_=Sst), [hu, hyo])
            fr_[("ed", b)] = [h1_, hu]
            fr_[("ld", b)] = [hy, hyd, hst, hcb, hyo, h4_, hxdt]
        out_handles.append((ds_y[0][0], ds_y[0][1]))
        out_handles.append((ds_y[1][0], ds_y[1][1]))
        B.release(mS)

    if "S" in phases:
        phase_S()


    def phase_B():
        B.release(persist_mark)
        mB = B.mark()
        B.phase_barrier(out_handles)
        mask2 = B.alloc([2, 256], BF16)
        qt = [B.alloc([4096], BF16) for _ in range(2)]
        kt = [B.alloc([4096], BF16) for _ in range(2)]
        vv = [[B.alloc([32, 130], BF16) for _ in range(3)] for _ in range(2)]
        ost = [B.alloc([32, 130], F32) for _ in range(2)]
        P = [B.alloc([2, 256], BF16) for _ in range(6)]
        hm_ = None
        for h_ in range(2):
            hm_ = op("dve", lambda e, h_=h_: e.tensor_copy(out=mask2[:, h_, 0:128], in_=cst_f[:, 3, :]), [hconst])
            hm_ = op("dve", lambda e, h_=h_: e.tensor_copy(out=mask2[:, h_, 128:256], in_=cst_f[:, 5, :]), [hconst])
        ds_in = [B.dsem("b_in0"), B.dsem("b_in1")]
        ds_od = [B.dsem("b_od0"), B.dsem("b_od1")]
        VAv = VA
        ODv = OD
        fr_ = {}
        bfree = {b: None for b in range(8)}
        rot = {"S": 0, "O": 0, "P": 0, "ost": 0, "m": 0}
        DIL = [(1, 32), (4, 8), (16, 2)]

        def F(k):
            return fr_.get(k)

        for nm in ("wo", "w1", "w2"):
            wcast["sem"][nm] = B.dsem("wc_" + nm)
        for kc in range(16):
            wcast["list"].append(("wo", WoB[kc * 128:(kc + 1) * 128, :], w_out[kc * 128:(kc + 1) * 128, :]))
        for cq in range(4):
            for kc in range(8):
                wcast["list"].append(("w1", W1B[kc * 128:(kc + 1) * 128, cq * 1024:(cq + 1) * 1024],
                                      w_ff1[kc * 128:(kc + 1) * 128, cq * 1024:(cq + 1) * 1024]))
        for kc in range(32):
            wcast["list"].append(("w2", W2B[kc * 128:(kc + 1) * 128, :], w_ff2[kc * 128:(kc + 1) * 128, :]))
        wcast["h"] = {}

        def drip(n):
            for _ in range(n):
                if wcast["list"]:
                    nm, o_, i_ = wcast["list"].pop(0)
                    wcast["h"][nm] = dma("pool", o_, i_, dsem=wcast["sem"][nm])

        for hp in range(8):
            b = hp % 2
            c0 = hp * 130
            hin = dma("sync", qt[b], QT[hp * 128:(hp + 1) * 128, :], deps=[F(("in", b))], dsem=ds_in[b])
            hin = dma("sync", kt[b], KT[hp * 128:(hp + 1) * 128, :], dsem=ds_in[b])
            for jq in range(4):
                hin = dma("sync", vv[b][0][:, jq * 8:(jq + 1) * 8, :],
                          VAv[jq * 1024:(jq + 1) * 1024, c0:c0 + 130].rearrange("(j i) c -> i j c", i=128), dsem=ds_in[b])
            for r in range(4):
                hin = dma("sync", vv[b][1][:, r * 8:(r + 1) * 8, :],
                          VAv[:, c0:c0 + 130].rearrange("(j i r) c -> i r j c", i=128, r=4)[:, r, :, :], dsem=ds_in[b])
            for r in range(16):
                hin = dma("sync", vv[b][2][:, r * 2:(r + 1) * 2, :],
                          VAv[:, c0:c0 + 130].rearrange("(j i r) c -> i r j c", i=128, r=16)[:, r, :, :], dsem=ds_in[b])
            last_reads = []
            steps = []
            for di, (d, NB) in enumerate(DIL):
                os_i = rot["ost"] % 2
                rot["ost"] += 1
                for r in range(d):
                    for j in range(NB):
                        steps.append(dict(di=di, d=d, NB=NB, r=r, j=j, os_i=os_i, last=(r == d - 1 and j == NB - 1)))

            def emit_qk(st_, b=b):
                d, NB, r, j = st_["d"], st_["NB"], st_["r"], st_["j"]
                nq = 256 if j + 1 < NB else 128
                t0 = r + d * 128 * j
                ktok = slice(t0, t0 + d * 127 + 1, d)
                qtok = slice(t0, t0 + d * (nq - 1) + 1, d)
                bs = rot["S"] % 2
                rot["S"] += 1
                hqk = None
                for h_ in range(2):
                    hqk = op("pe", lambda e, h_=h_: e.matmul(
                        B.bank(2 * bs + h_)[:, 0:nq], lhsT=kt[b][64 * h_:64 * h_ + 64, ktok],
                        rhs=qt[b][64 * h_:64 * h_ + 64, qtok], start=True, stop=True),
                        [hin, bfree[2 * bs], bfree[2 * bs + 1]] if h_ == 0 else [], track=(h_ == 1))
                pi = rot["P"] % 6
                rot["P"] += 1
                Pc = P[pi]
                hp_ = op("act", lambda e: e.activation(
                    out=Pc[:, :, 0:nq],
                    in_=B.psum[:, 2 * bs * 512:(2 * bs + 2) * 512].rearrange("p (h q) -> p h q", h=2)[:, :, 0:nq],
                    func=AF.Exp), [hqk, F(("P", pi))])
                bfree[2 * bs] = hp_
                bfree[2 * bs + 1] = hp_
                meng = "dve" if rot["m"] % 2 == 0 else "pool"
                rot["m"] += 1
                hmk = op(meng, lambda e: e.tensor_tensor(
                    out=Pc[:, :, 0:nq], in0=Pc[:, :, 0:nq], in1=mask2[:, :, 0:nq], op=ALU.mult), [hp_, hm_])
                return dict(Pc=Pc, pi=pi, hmk=hmk)

            def emit_pv(st_, cur, prev, b=b, c0=c0):
                di, d, NB, r, j, os_i = st_["di"], st_["d"], st_["NB"], st_["r"], st_["j"], st_["os_i"]
                bo = 4 + rot["O"] % 4
                rot["O"] += 1
                tile_c = r * NB + j
                Pc = cur["Pc"]
                hpv = None
                for h_ in range(2):
                    oap = B.bank(bo)[:, h_ * 65:(h_ + 1) * 65]
                    if j > 0:
                        Pp = prev["Pc"]
                        op("pe", lambda e, h_=h_, oap=oap, Pp=Pp: e.matmul(
                            oap, lhsT=Pp[:, h_, 128:256], rhs=vv[b][di][:, tile_c - 1, h_ * 65:(h_ + 1) * 65],
                            start=True, stop=False), [prev["hmk"], cur["hmk"], bfree[bo]] if h_ == 0 else [], track=False)
                    hpv = op("pe", lambda e, h_=h_, oap=oap: e.matmul(
                        oap, lhsT=Pc[:, h_, 0:128], rhs=vv[b][di][:, tile_c, h_ * 65:(h_ + 1) * 65],
                        start=(j == 0), stop=True), [cur["hmk"], bfree[bo]] if (h_ == 0 and j == 0) else [],
                        track=(h_ == 1))
                if j > 0:
                    fr_[("P", prev["pi"])] = hpv
                fr_[("P", cur["pi"])] = hpv
                hev = op("dve", lambda e: e.tensor_copy(out=ost[os_i][:, tile_c, :], in_=B.bank(bo)[:, 0:130]),
                         [hpv, F(("ost", os_i))])
                bfree[bo] = hev
                last_reads[:] = [hpv]
                if st_["last"]:
                    dst = ODv[di, :, c0:c0 + 130].rearrange("(j i r) c -> i r j c", i=128, r=d)
                    hod = None
                    for r2 in range(d):
                        for jq in range(0, NB, 8):
                            je = min(NB, jq + 8)
                            hod = dma("sync", dst[:, r2, jq:je, :], ost[os_i][:, r2 * NB + jq:r2 * NB + je, :], deps=[hev],
                                      dsem=ds_od[os_i])
                    fr_[("ost", os_i)] = hod

            nst = len(steps)
            info = [None] * nst
            LA = 2
            for t in range(nst + LA):
                if t % 6 == 0:
                    drip(1 if hp > 0 else 2)
                if t < nst:
                    info[t] = emit_qk(steps[t])
                u_ = t - LA
                if u_ >= 0:
                    emit_pv(steps[u_], info[u_], info[u_ - 1] if steps[u_]["j"] > 0 else None)
            fr_[("in", b)] = last_reads
        drip(1000)
        out_handles.append((ds_od[0][0], ds_od[0][1]))
        out_handles.append((ds_od[1][0], ds_od[1][1]))
        B.release(mB)

    if "B" in phases:
        phase_B()


    def phase_C1():
        B.release(persist_mark)
        B.phase_barrier(out_handles)
        Wo = B.alloc([16, 1024], BF16)
        attnw_sb = B.alloc([1024], F32)
        od = [B.alloc([3, 1040], F32) for _ in range(2)]
        ys_ = [B.alloc([1024], BF16) for _ in range(2)]
        rden = [B.alloc([16], F32) for _ in range(2)]
        of = [B.alloc([16, 64], F32) for _ in range(2)]
        ya = [B.alloc([1024], BF16) for _ in range(2)]
        ssq = [B.alloc([2], F32) for _ in range(2)]
        sqj = B.alloc([1024], BF16)
        yT = [B.alloc([16, 512], BF16) for _ in range(2)]
        xt = B.alloc([8, 512], F32)
        sq = [B.alloc([512], BF16) for _ in range(2)]
        rstd = B.alloc([512], F32)
        tmp = [B.alloc([512], F32) for _ in range(2)]
        h2 = [B.alloc([8, 512], BF16) for _ in range(2)]
        ds_w = B.dsem("c1w")
        hw = None
        for kc in range(16):
            hw = dma("sync", Wo[:, kc, :], WoB[kc * 128:(kc + 1) * 128, :], deps=[wcast["h"].get("wo")], dsem=ds_w)
        ds_aw = B.dsem("c1aw")
        haw = dma("sync", attnw_sb, attnw, dsem=ds_aw)
        ds_l = [B.dsem("c1l0"), B.dsem("c1l1")]
        ds_x = B.dsem("c1x")
        ds_o1 = B.dsem("c1o1")
        ds_o2 = [B.dsem("c1o2a"), B.dsem("c1o2b")]
        fr_ = {}
        bfree = {b: None for b in range(8)}
        rot = {"tp": 0, "mb": 0}

        def F(k):
            return fr_.get(k)

        sidx_ = {"v": 0}
        hyT_of = {}

        def combine(i):
            b = i % 2
            c0t = i * 512
            hyT = []
            for st in range(4):
                s_ = sidx_["v"] % 2
                sidx_["v"] += 1
                r0 = c0t + st * 128
                hl = dma("sync", od[s_], OD[:, r0:r0 + 128, :].rearrange("d t c -> t d c"), deps=[F(("od", s_))], dsem=ds_l[s_])
                hl = dma("sync", ys_[s_], YS[r0:r0 + 128, :], deps=[F(("ys", s_))], dsem=ds_l[s_])
                o0 = od[s_][:, 0, :]
                h = op("dve", lambda e, s_=s_, o0=o0: e.tensor_add(out=o0, in0=o0, in1=od[s_][:, 1, :]), [hl])
                h = op("dve", lambda e, s_=s_, o0=o0: e.tensor_add(out=o0, in0=o0, in1=od[s_][:, 2, :]), [h])
                o3 = o0.rearrange("p (h d) -> p h d", h=16)
                h = op("dve", lambda e, s_=s_, o3=o3: e.reciprocal(out=rden[s_], in_=o3[:, :, 64]), [h, F(("rden", s_))])
                h = op("dve", lambda e, s_=s_, o3=o3: e.tensor_tensor(
                    out=of[s_], in0=o3[:, :, 0:64], in1=rden[s_].unsqueeze(2).to_broadcast([128, 16, 64]), op=ALU.mult),
                    [h, F(("of", s_))])
                fr_[("od", s_)] = h
                off = of[s_].rearrange("p h d -> p (h d)")
                ha = op("act", lambda e, s_=s_, off=off: e.activation(out=sqj, in_=off, func=AF.Square,
                                                                     accum_out=ssq[s_][:, 0:1]), [h, F(("ssq", s_)), F("sqj")])
                fr_["sqj"] = ha
                ha = op("act", lambda e, s_=s_: e.activation(out=ssq[s_][:, 1:2], in_=ssq[s_][:, 0:1], func=AF.Ln,
                                                             scale=1.0 / 1024, bias=EPS), [ha])
                ha = op("act", lambda e, s_=s_: e.activation(out=ssq[s_][:, 1:2], in_=ssq[s_][:, 1:2], func=AF.Exp, scale=-0.5), [ha])
                hya = op("dve", lambda e, s_=s_, off=off: e.scalar_tensor_tensor(
                    out=ya[s_], in0=off, scalar=ssq[s_][:, 1:2], in1=attnw_sb, op0=ALU.mult, op1=ALU.mult),
                    [ha, haw, F(("ya", s_))])
                fr_[("ssq", s_)] = hya
                fr_[("of", s_)] = hya
                fr_[("rden", s_)] = hya
                hlast = None
                for q4 in range(4):
                    tpi = rot["tp"] % 2
                    rot["tp"] += 1
                    tp = B.bank(6 + tpi, 1024, BF16)[:, 0:512]
                    hp = None
                    for k4 in range(4):
                        kc = q4 * 4 + k4
                        src = ys_[s_][:, kc * 128:(kc + 1) * 128] if kc < 8 else ya[s_][:, (kc - 8) * 128:(kc - 7) * 128]
                        hp = op("pe", lambda e, k4=k4, tp=tp, src=src: e.transpose(
                            out=tp[:, k4 * 128:(k4 + 1) * 128], in_=src, identity=ident),
                            [hl, hya, h_id, F(("tp", tpi))] if k4 == 0 else [], track=(k4 == 3))
                    he = op("dve", lambda e, tp=tp, b=b, q4=q4, st=st: e.tensor_copy(
                        out=yT[b][:, q4 * 4:q4 * 4 + 4, st * 128:(st + 1) * 128],
                        in_=tp.rearrange("p (k t) -> p k t", k=4)), [hp, F(("yT", b))])
                    fr_[("tp", tpi)] = he
                    hlast = hp
                    hyT.append(he)
                fr_[("ys", s_)] = hlast
                fr_[("ya", s_)] = hlast
            hyT_of[i] = hyT

        def finish_(i):
            b = i % 2
            c0t = i * 512
            hyT = hyT_of[i]
            hx = None
            for kc in range(8):
                hx = dma("sync", xt[:, kc, :], xT[kc * 128:(kc + 1) * 128, c0t:c0t + 512], deps=[F("xt")], dsem=ds_x)
            hx1 = None
            hmm_last = None
            for f in range(8):
                bk = 2 + rot["mb"] % 4
                rot["mb"] += 1
                hm = None
                for kc in range(16):
                    hm = op("pe", lambda e, kc=kc, f=f, bk=bk, b=b: e.matmul(
                        B.bank(bk), lhsT=Wo[:, kc, f * 128:(f + 1) * 128], rhs=yT[b][:, kc, :],
                        start=(kc == 0), stop=(kc == 15)), (hyT + [hw, bfree[bk]]) if kc == 0 else [], track=(kc == 15))
                hx1 = op("dve", lambda e, f=f, bk=bk: e.scalar_tensor_tensor(
                    out=xt[:, f, :], in0=B.bank(bk), scalar=modT[:, 16 + f:17 + f], in1=xt[:, f, :],
                    op0=ALU.mult, op1=ALU.add), [hm, hx, h_mod])
                bfree[bk] = hx1
                hmm_last = hm
            fr_[("yT", b)] = hmm_last
            hst1 = None
            for kc in range(8):
                hst1 = dma("sync", X1T[kc * 128:(kc + 1) * 128, c0t:c0t + 512], xt[:, kc, :], deps=[hx1], dsem=ds_o1)
            hst = None
            for kc in range(8):
                sl = kc % 2
                hs = op("pool", lambda e, kc=kc, sl=sl: e.tensor_tensor(out=sq[sl], in0=xt[:, kc, :], in1=xt[:, kc, :],
                                                                      op=ALU.mult), [hx1, F(("sq", sl))])
                hst = op("pe", lambda e, kc=kc, sl=sl: e.matmul(B.bank(0), lhsT=ones_bf, rhs=sq[sl],
                                                               start=(kc == 0), stop=(kc == 7)),
                         [hs, h_ones, bfree[0] if kc == 0 else None])
                fr_[("sq", sl)] = hst
            h = op("act", lambda e: e.activation(out=rstd, in_=B.bank(0), func=AF.Ln, scale=1.0 / D, bias=EPS),
                   [hst, F("rstd")])
            bfree[0] = h
            h = op("act", lambda e: e.activation(out=rstd, in_=rstd, func=AF.Exp, scale=-0.5), [h])
            hh = None
            hn = None
            for kc in range(8):
                sl = kc % 2
                hn = op("dve", lambda e, kc=kc, sl=sl: e.tensor_mul(out=tmp[sl], in0=xt[:, kc, :], in1=rstd),
                        [h, hx1, F(("tmp", sl))])
                hh = op("pool", lambda e, kc=kc, b=b, sl=sl: e.tensor_scalar(
                    out=h2[b][:, kc, :], in0=tmp[sl], scalar1=s2[:, kc:kc + 1], scalar2=modT[:, 24 + kc:25 + kc],
                    op0=ALU.mult, op1=ALU.add), [hn, h_mod, F(("h2", b))])
                fr_[("tmp", sl)] = hh
            fr_["rstd"] = hn
            hd = None
            for kc in range(8):
                hd = dma("sync", H2T[kc * 128:(kc + 1) * 128, c0t:c0t + 512], h2[b][:, kc, :], deps=[hh], dsem=ds_o2[b])
            fr_[("h2", b)] = hd
            fr_["xt"] = [hst1, hn, hst]

        combine(0)
        for i in range(NT):
            if i + 1 < NT:
                combine(i + 1)
            finish_(i)
        out_handles.append((ds_o1[0], ds_o1[1]))
        out_handles.append((ds_o2[0][0], ds_o2[0][1]))
        out_handles.append((ds_o2[1][0], ds_o2[1][1]))

    def phase_C2():
        B.release(persist_mark)
        B.phase_barrier(out_handles)
        W1 = B.alloc([8, 4096], BF16)
        W2 = B.alloc([32, 1024], BF16)
        TN = 256
        h2 = [B.alloc([8, TN], BF16) for _ in range(2)]
        u = B.alloc([32, TN], BF16)
        rr = [B.alloc([TN], F32) for _ in range(3)]
        x1f = [B.alloc([TN], F32) for _ in range(3)]
        oo = [B.alloc([TN], F32) for _ in range(3)]
        hw1c = []
        for cq in range(4):
            dsq = B.dsem("c2w1")
            hq_ = None
            for kc in range(8):
                hq_ = dma("sync", W1[:, kc, cq * 1024:(cq + 1) * 1024], W1B[kc * 128:(kc + 1) * 128, cq * 1024:(cq + 1) * 1024],
                          deps=[wcast["h"].get("w1")], dsem=dsq)
            hw1c.append(hq_)
        ds_w2 = B.dsem("c2w2")
        hw2 = None
        for kc in range(32):
            hw2 = dma("sync", W2[:, kc, :], W2B[kc * 128:(kc + 1) * 128, :], deps=[wcast["h"].get("w2")], dsem=ds_w2)
        ds_h = [B.dsem("c2h0"), B.dsem("c2h1")]
        ds_x = [B.dsem("c2x0"), B.dsem("c2x1"), B.dsem("c2x2")]
        ds_o = [B.dsem("c2o0"), B.dsem("c2o1"), B.dsem("c2o2")]
        fr_ = {}
        bfree = {b: None for b in range(8)}
        rot = {"mb": 0, "r": 0, "x": 0}

        def F(k):
            return fr_.get(k)

        for i in range(S // TN):
            b = i % 2
            c0 = i * TN
            hh = None
            for kc in range(8):
                hh = dma("sync", h2[b][:, kc, :], H2T[kc * 128:(kc + 1) * 128, c0:c0 + TN], deps=[F(("h2", b))], dsem=ds_h[b])
            hu_all = []
            hm = None
            for m in range(32):
                bk = rot["mb"] % 6
                rot["mb"] += 1
                for kc in range(8):
                    hm = op("pe", lambda e, kc=kc, m=m, bk=bk, b=b: e.matmul(
                        B.bank(bk, TN), lhsT=W1[:, kc, m * 128:(m + 1) * 128], rhs=h2[b][:, kc, :],
                        start=(kc == 0), stop=(kc == 7)), [hh, hw1c[m // 8], bfree[bk]] if kc == 0 else [], track=(kc == 7))
                ri = rot["r"] % 3
                rot["r"] += 1
                hr = op("act", lambda e, bk=bk, ri=ri: e.activation(out=rr[ri], in_=B.bank(bk, TN), func=AF.Relu),
                        [hm, F(("rr", ri))])
                bfree[bk] = hr
                hu = op("pool", lambda e, ri=ri, m=m: e.tensor_tensor(out=u[:, m, :], in0=rr[ri], in1=rr[ri], op=ALU.mult),
                        [hr, F("u")])
                fr_[("rr", ri)] = hu
                hu_all.append(hu)
            fr_[("h2", b)] = hm
            hm2 = None
            for f in range(8):
                bk = 6 + f % 2
                xi = rot["x"] % 3
                rot["x"] += 1
                hxl = dma("sync", x1f[xi], X1T[f * 128:(f + 1) * 128, c0:c0 + TN], deps=[F(("x1f", xi))], dsem=ds_x[xi])
                for kc in range(32):
                    hm2 = op("pe", lambda e, kc=kc, f=f, bk=bk: e.matmul(
                        B.bank(bk, TN), lhsT=W2[:, kc, f * 128:(f + 1) * 128], rhs=u[:, kc, :],
                        start=(kc == 0), stop=(kc == 31)), (hu_all + [hw2, bfree[bk]]) if kc == 0 else [], track=(kc == 31))
                ho = op("dve", lambda e, f=f, bk=bk, xi=xi: e.scalar_tensor_tensor(
                    out=oo[xi], in0=B.bank(bk, TN), scalar=modT[:, 40 + f:41 + f], in1=x1f[xi],
                    op0=ALU.mult, op1=ALU.add), [hm2, hxl, h_mod, F(("oo", xi))])
                bfree[bk] = ho
                fr_[("x1f", xi)] = ho
                hd = dma("sync", outT[f * 128:(f + 1) * 128, c0:c0 + TN], oo[xi], deps=[ho], dsem=ds_o[xi])
                fr_[("oo", xi)] = hd
            fr_["u"] = hm2
        for k in range(3):
            out_handles.append((ds_o[k][0], ds_o[k][1]))

    if "C" in phases:
        phase_C1()
        phase_C2()

    B.final_wait("sync", out_handles + B.barrier_handles())
    B.finish()
    return nc


def _host_consts():
    c = np.zeros((128, 6, 128), np.float32)
    c[:, 0, :] = np.eye(128)
    c[:, 1, :] = 1.0
    bd = np.zeros((128, 128), np.float32)
    bd[:64, :64] = 1.0 / 64
    bd[64:, 64:] = 1.0 / 64
    c[:, 2, :] = bd
    j = np.arange(128)
    c[:, 3, :] = (j[:, None] <= j[None, :]).astype(np.float32)
    c[:, 4, :] = (j[:, None] > j[None, :]).astype(np.float32)
    c[:, 5, :] = (j[:, None] >= j[None, :]).astype(np.float32)
    return c


def make_in_maps(inp):
    f = lambda a: np.ascontiguousarray(np.asarray(a, dtype=np.float32))
    x = f(inp["x"])
    c = f(inp["c"])
    shared = {
        "w_ada": f(inp["w_ada"][0]),
        "b_adaT": f(inp["b_ada"][0].reshape(48, 128).T),
        "n1w": f(inp["norm1_w"][0].reshape(8, 128).T),
        "n2w": f(inp["norm2_w"][0].reshape(8, 128).T),
        "w_in": f(inp["w_in"][0]),
        "convw": f(np.asarray(inp["conv_w"][0]).T.reshape(16, 128, 4).transpose(1, 0, 2)),
        "convb": f(np.asarray(inp["conv_b"][0]).reshape(16, 128).T),
        "dtb": f(np.broadcast_to(np.asarray(inp["dt_bias"][0])[None, :], (128, 16))),
        "alog": f(np.broadcast_to(np.asarray(inp["a_log"][0])[None, :], (128, 16))),
        "dskip": f(np.broadcast_to(np.asarray(inp["d_skip"][0])[None, :], (128, 16))),
        "ssdnw": f(np.broadcast_to(np.asarray(inp["ssd_norm_w"][0])[None, :], (128, 1024))),
        "qkw": f(np.stack([np.tile(np.asarray(inp["q_norm_w"][0]), 2),
                           np.tile(np.asarray(inp["k_norm_w"][0]), 2)], axis=1)),
        "attnw": f(np.broadcast_to(np.asarray(inp["attn_norm_w"][0])[None, :], (128, 1024))),
        "w_out": f(inp["w_out"][0]),
        "w_ff1": f(inp["w_ff1"][0]),
        "w_ff2": f(inp["w_ff2"][0]),
        "cmat": _host_consts(),
    }
    maps = []
    for b in range(8):
        m = dict(shared)
        m["xT"] = f(x[b].T)
        m["cT"] = f(c[b].reshape(8, 128).T)
        maps.append(m)
    return maps


_NC_CACHE = {}


def kernel(**inputs):
    key = (PHASES, DEBUG)
    if key not in _NC_CACHE:
        _NC_CACHE[key] = build_program(PHASES, DEBUG)
    nc = _NC_CACHE[key]
    in_maps = make_in_maps(inputs)
    res = run_bass_kernel_spmd(nc, in_maps, core_ids=list(range(8)))
    out = np.stack([np.ascontiguousarray(r["outT"].T) for r in res.results], axis=0)
    return out.astype(np.float32)
```

```python
import numpy as np
from contextlib import ExitStack
import concourse.bass as bass
import concourse.mybir as mybir
from concourse.bass_utils import run_bass_kernel_spmd

F32 = mybir.dt.float32
BF16 = mybir.dt.bfloat16
U8 = mybir.dt.uint8
AF = mybir.ActivationFunctionType
ALU = mybir.AluOpType
AX = mybir.AxisListType

S = 4096
D = 1024
NT = 8
TT = 512
INW = 6160
EPS = 1e-6
DSIZE = {F32: 4, BF16: 2, U8: 1}

PHASES = "MASBC"
DEBUG = False
GROUPS = "dqxzv"
NTILES = 8


class Builder:
    def __init__(self, nc):
        self.nc = nc
        self.es = ExitStack()
        self.engs = ["sync", "act", "dve", "pool", "pe"]
        self.q = {n: [] for n in self.engs}
        self.cnt = {n: 0 for n in self.engs}
        self.waited = {n: {} for n in self.engs}
        self.sem = {n: self.es.enter_context(nc.semaphore("prog_" + n)) for n in self.engs}
        self.arena = self.es.enter_context(nc.sbuf_tensor("arena", [128, 204 * 1024], U8))
        self.psum = self.es.enter_context(nc.psum_tensor("psum", [128, 4096], F32))
        self.aoff = 0
        self.nsem = 0

    def alloc(self, free_shape, dtype):
        n = int(np.prod(free_shape)) * DSIZE[dtype]
        off = (self.aoff + 63) // 64 * 64
        assert off + n <= 204 * 1024, ("SBUF arena overflow", off, n)
        self.aoff = off + n
        ap = self.arena[:, off:off + n].bitcast(dtype)
        if len(free_shape) == 2:
            ap = ap.rearrange("p (a b) -> p a b", a=free_shape[0])
        elif len(free_shape) == 3:
            ap = ap.rearrange("p (a b c) -> p a b c", a=free_shape[0], b=free_shape[1])
        elif len(free_shape) == 4:
            ap = ap.rearrange("p (a b c d) -> p a b c d", a=free_shape[0], b=free_shape[1], c=free_shape[2])
        return ap

    def mark(self):
        return self.aoff

    def release(self, m):
        self.aoff = m

    def bank(self, b, n=512, dtype=F32):
        ap = self.psum[:, b * 512:(b + 1) * 512]
        if dtype == BF16:
            return ap.bitcast(BF16)[:, 0:n]
        return ap[:, 0:n]

    def new_sem(self, name):
        self.nsem += 1
        return self.es.enter_context(self.nc.semaphore(f"{name}_{self.nsem}"))

    def _waits(self, eng, deps):
        waits = []
        for d in deps:
            if d is None:
                continue
            if isinstance(d, list):
                for dd in d:
                    waits += self._waits(eng, [dd])
                continue
            s, v = d
            key = s.num
            if self.waited[eng].get(key, 0) < v:
                self.waited[eng][key] = v
                waits.append((s, v))
        return waits

    def op(self, eng, fn, deps=(), track=True):
        waits = self._waits(eng, deps)
        h = None
        sem = self.sem[eng]
        if track:
            self.cnt[eng] += 1
            h = (sem, self.cnt[eng])

        def run(e, fn=fn, waits=waits, track=track, sem=sem):
            for s, v in waits:
                e.wait_ge(s, v)
            ins = fn(e)
            if track:
                ins.then_inc(sem, 1)
        self.q[eng].append(run)
        return h

    def dma(self, eng, out, in_, deps=(), dsem=None, **kw):
        waits = self._waits(eng, deps)
        dsem[1] += 16
        h = (dsem[0], dsem[1])
        sem = dsem[0]

        def run(e, waits=waits, sem=sem, out=out, in_=in_, kw=kw):
            for s, v in waits:
                e.wait_ge(s, v)
            e.dma_start(out=out, in_=in_, **kw).then_inc(sem, 16)
        self.q[eng].append(run)
        return h

    def dsem(self, name):
        return [self.new_sem(name), 0]

    def final_wait(self, eng, deps):
        waits = self._waits(eng, deps)

        def run(e, waits=waits):
            for s, v in waits:
                e.wait_ge(s, v)
        self.q[eng].append(run)

    def phase_barrier(self, extra=()):
        hs = self.barrier_handles() + list(extra)
        for n in self.engs:
            self.final_wait(n, hs)

    def barrier_handles(self):
        return [(self.sem[n], self.cnt[n]) for n in ["act", "dve", "pool", "pe"] if self.cnt[n] > 0]

    def finish(self):
        nc = self.nc
        with nc.Block() as block:
            @block.sync
            def _(e):
                for f in self.q["sync"]:
                    f(e)

            @block.scalar
            def _(e):
                for f in self.q["act"]:
                    f(e)

            @block.vector
            def _(e):
                for f in self.q["dve"]:
                    f(e)

            @block.gpsimd
            def _(e):
                for f in self.q["pool"]:
                    f(e)

            @block.tensor
            def _(e):
                for f in self.q["pe"]:
                    f(e)
        self.es.close()


def build_program(phases=PHASES, debug=DEBUG):
    nc = bass.Bass("TRN2", target_bir_lowering=False)
    dr = {}

    def din(name, shape, dt=F32):
        dr[name] = nc.dram_tensor(name, list(shape), dt, kind="ExternalInput").ap()
        return dr[name]

    def dscr(name, shape, dt):
        kind = "ExternalOutput" if (debug and name in debug) else "Internal"
        dr[name] = nc.dram_tensor(name, list(shape), dt, kind=kind).ap()
        return dr[name]

    xT = din("xT", [D, S])
    cT = din("cT", [128, 8])
    w_ada = din("w_ada", [D, 6 * D])
    b_adaT = din("b_adaT", [128, 48])
    n1w = din("n1w", [128, 8])
    n2w = din("n2w", [128, 8])
    w_in = din("w_in", [D, INW])
    convw = din("convw", [128, 16, 4])
    convb = din("convb", [128, 16])
    dtb = din("dtb", [128, 16])
    alog = din("alog", [128, 16])
    dskip = din("dskip", [128, 16])
    ssdnw = din("ssdnw", [128, 1024])
    qkw = din("qkw", [128, 2])
    attnw = din("attnw", [128, 1024])
    w_out = din("w_out", [2 * D, D])
    w_ff1 = din("w_ff1", [D, 4 * D])
    w_ff2 = din("w_ff2", [4 * D, D])
    cmat = din("cmat", [128, 6, 128])
    outT = nc.dram_tensor("outT", [D, S], F32, kind="ExternalOutput").ap()

    ZS = dscr("ZS", [S, 1024], BF16)
    XS = dscr("XS", [S, 1024], BF16)
    BM = dscr("BM", [S, 512], BF16)
    BCT = dscr("BCT", [1024, S], BF16)
    QT = dscr("QT", [1024, S], BF16)
    KT = dscr("KT", [1024, S], BF16)
    VA = dscr("VA", [S, 16 * 65], BF16)
    DTS = dscr("DTS", [S, 32], F32)
    YS = dscr("YS", [S, 1024], BF16)
    OD = dscr("OD", [3, S, 16 * 65], F32)
    X1T = dscr("X1T", [D, S], F32)
    H2T = dscr("H2T", [D, S], BF16)
    MODT = dscr("MODT", [128, 48], F32)
    WoB = dscr("WoB", [2 * D, D], BF16)
    W1B = dscr("W1B", [D, 4 * D], BF16)
    W2B = dscr("W2B", [4 * D, D], BF16)
    wcast = {"list": [], "sem": {}}

    B = Builder(nc)
    op, dma = B.op, B.dma

    modT = B.alloc([48], F32)
    s1 = B.alloc([8], F32)
    s2 = B.alloc([8], F32)
    ident = B.alloc([128], BF16)
    ones_bf = B.alloc([128], BF16)
    blk64 = B.alloc([128], BF16)
    ones_f = B.alloc([128], F32)
    cst_f = B.alloc([6, 128], F32)
    convw_sb = B.alloc([16, 4], F32)
    convb_sb = B.alloc([16], F32)
    qkw_sb = B.alloc([2], F32)
    dtb_sb = B.alloc([16], F32)
    A_sb = B.alloc([16], F32)
    dskip_sb = B.alloc([16], F32)
    n1w_sb = B.alloc([8], F32)
    n2w_sb = B.alloc([8], F32)
    persist_mark = B.mark()

    ds_c = B.dsem("const")
    hc = []
    hc.append(dma("sync", cst_f, cmat, dsem=ds_c))
    hc.append(dma("sync", convw_sb, convw, dsem=ds_c))
    hc.append(dma("sync", convb_sb, convb, dsem=ds_c))
    hc.append(dma("sync", qkw_sb, qkw, dsem=ds_c))
    hc.append(dma("sync", dtb_sb, dtb, dsem=ds_c))
    hc.append(dma("sync", A_sb, alog, dsem=ds_c))
    hc.append(dma("sync", dskip_sb, dskip, dsem=ds_c))
    hc.append(dma("sync", n1w_sb, n1w, dsem=ds_c))
    hc.append(dma("sync", n2w_sb, n2w, dsem=ds_c))
    hconst = hc[-1]

    h_id = op("dve", lambda e: e.tensor_copy(out=ident, in_=cst_f[:, 0, :]), [hconst])
    h_ones = op("dve", lambda e: e.tensor_copy(out=ones_bf, in_=cst_f[:, 1, :]), [hconst])
    h_blk = op("dve", lambda e: e.tensor_copy(out=blk64, in_=cst_f[:, 2, :]), [hconst])
    h_onesf = op("dve", lambda e: e.tensor_copy(out=ones_f, in_=cst_f[:, 1, :]), [hconst])
    h_A = op("act", lambda e: e.activation(out=A_sb, in_=A_sb, func=AF.Exp), [hconst])
    h_A = op("dve", lambda e: e.tensor_scalar_mul(out=A_sb, in0=A_sb, scalar1=-1.0), [h_A])
    h_qw = op("dve", lambda e: e.tensor_scalar_mul(out=qkw_sb[:, 0:1], in0=qkw_sb[:, 0:1], scalar1=0.125), [hconst])
    h_setup = [h_id, h_ones, h_blk, h_onesf, h_A, h_qw]

    out_handles = []

    Wsb = B.alloc([8, INW], BF16)
    pieces = [(0, 1540), (1540, 3080), (3080, 4620), (4620, 6160)]
    hW = []
    for (c0, c1) in pieces:
        dsw = B.dsem("W")
        h = None
        for kc in range(8):
            h = dma("pool", Wsb[:, kc, c0:c1], w_in[kc * 128:(kc + 1) * 128, c0:c1], dsem=dsw)
        hW.append(h)

    def wdeps(a, b_):
        return [hW[i] for i, (c0, c1) in enumerate(pieces) if a < c1 and b_ > c0]

    w_mark = B.mark()

    if "M" in phases:
        m0 = B.mark()
        c_sb = B.alloc([8], F32)
        cact = B.alloc([8], F32)
        bada = B.alloc([48], F32)
        wslM = [B.alloc([8, 1024], F32) for _ in range(2)]
        accM = [B.alloc([1024], F32) for _ in range(2)]
        ds_m = B.dsem("m_in")
        hcin = dma("sync", c_sb, cT, dsem=ds_m)
        hcin = dma("sync", bada, b_adaT, dsem=ds_m)
        h = op("act", lambda e: e.activation(out=cact, in_=c_sb, func=AF.Exp, scale=-1.0), [hcin])
        h = op("dve", lambda e: e.tensor_scalar_add(out=cact, in0=cact, scalar1=1.0), [h])
        h = op("dve", lambda e: e.reciprocal(out=cact, in_=cact), [h])
        hcact = op("dve", lambda e: e.tensor_mul(out=cact, in0=cact, in1=c_sb), [h])
        ds_w = [B.dsem("m_w0"), B.dsem("m_w1")]
        wfreeM = [None, None]
        accfreeM = [None, None]
        psM = B.bank(0, 48)
        hmm = None
        for sl in range(6):
            b = sl % 2
            hw = None
            for kc in range(8):
                hw = dma("sync", wslM[b][:, kc, :], w_ada[kc * 128:(kc + 1) * 128, sl * 1024:(sl + 1) * 1024],
                         deps=[wfreeM[b]], dsem=ds_w[b])
            eng = "dve"
            h = op(eng, lambda e, b=b: e.tensor_scalar_mul(out=accM[b], in0=wslM[b][:, 0, :], scalar1=cact[:, 0:1]),
                   [hw, hcact, accfreeM[b]])
            for kc in range(1, 8):
                h = op(eng, lambda e, b=b, kc=kc: e.scalar_tensor_tensor(
                    out=accM[b], in0=wslM[b][:, kc, :], scalar=cact[:, kc:kc + 1], in1=accM[b],
                    op0=ALU.mult, op1=ALU.add), [h])
            wfreeM[b] = h
            for jt in range(8):
                col = sl * 8 + jt
                hmm = op("pe", lambda e, b=b, jt=jt, col=col: e.matmul(
                    psM[:, col:col + 1], lhsT=accM[b][:, jt * 128:(jt + 1) * 128], rhs=ones_f[:, 0:1],
                    start=True, stop=True), [h, h_onesf])
            accfreeM[b] = hmm
        hmod = op("dve", lambda e: e.tensor_add(out=modT, in0=psM, in1=bada), [hmm, hcin])
        h = op("dve", lambda e: e.scalar_tensor_tensor(out=s1, in0=modT[:, 8:16], scalar=1.0, in1=n1w_sb,
                                                      op0=ALU.add, op1=ALU.mult), [hmod, hconst])
        hmod2 = op("dve", lambda e: e.scalar_tensor_tensor(out=s2, in0=modT[:, 32:40], scalar=1.0, in1=n2w_sb,
                                                          op0=ALU.add, op1=ALU.mult), [h])
        ds_mo = B.dsem("m_out")
        out_handles.append(dma("sync", MODT, modT, deps=[hmod2], dsem=ds_mo))
        h_mod = hmod2
        B.release(m0)
    else:
        h_mod = None


    if "A" in phases:
        mA = B.mark()
        B.phase_barrier(out_handles)
        xt = B.alloc([8, 512], F32)
        sq = [B.alloc([512], BF16) for _ in range(2)]
        h1 = [B.alloc([8, 512], BF16) for _ in range(2)]
        rstd = B.alloc([512], F32)
        xr = [B.alloc([515], F32) for _ in range(3)]
        acc = [B.alloc([512], F32) for _ in range(3)]
        xc = [B.alloc([512], BF16) for _ in range(5)]
        halo = B.alloc([16, 3], F32)
        qsq = [B.alloc([512], BF16) for _ in range(2)]
        qraw = [B.alloc([512], F32) for _ in range(2)]
        qrs = [B.alloc([512], F32) for _ in range(2)]
        qn = [B.alloc([512], BF16) for _ in range(3)]
        zs = [B.alloc([512], BF16) for _ in range(3)]
        VAst = B.alloc([4, 16, 65], BF16)
        XSst = B.alloc([4, 1024], BF16)
        BMst = B.alloc([4, 512], BF16)
        tdt = B.alloc([4, 16], F32)
        adt = B.alloc([4, 16], F32)
        dts = B.alloc([4, 32], F32)

        h_va1 = op("pool", lambda e: e.memset(VAst, 1.0))
        h_halo0 = op("pool", lambda e: e.memset(halo, 0.0))
        ds_x = B.dsem("x")
        dsd = {}

        def sd(key):
            if key not in dsd:
                dsd[key] = B.dsem("o")
            return dsd[key]
        bank_free = {b: None for b in range(8)}
        free = {}

        def fr(name):
            return free.get(name)

        state = {"xt_free": None, "h1": [None, None], "h1_free": [None, None]}
        phaseA_out = []

        def prep(i):
            b = i % 2
            hx = None
            for kc in range(8):
                hx = dma("sync", xt[:, kc, :], xT[kc * 128:(kc + 1) * 128, i * 512:(i + 1) * 512],
                         deps=[state["xt_free"]], dsem=ds_x)
            hst = None
            for kc in range(8):
                sl = kc % 2
                hs = op("pool", lambda e, kc=kc, sl=sl: e.tensor_tensor(out=sq[sl], in0=xt[:, kc, :], in1=xt[:, kc, :],
                                                                      op=ALU.mult), [hx, fr(("sq", sl))])
                hst = op("pe", lambda e, kc=kc, sl=sl: e.matmul(B.bank(0), lhsT=ones_bf, rhs=sq[sl],
                                                               start=(kc == 0), stop=(kc == 7)),
                         [hs, h_ones, bank_free[0] if kc == 0 else None])
                free[("sq", sl)] = hst
            h = op("act", lambda e: e.activation(out=rstd, in_=B.bank(0), func=AF.Ln, scale=1.0 / D, bias=EPS),
                   [hst, fr("rstd")])
            bank_free[0] = h
            h = op("act", lambda e: e.activation(out=rstd, in_=rstd, func=AF.Exp, scale=-0.5), [h])
            hh = None
            for kc in range(8):
                hn = op("dve", lambda e, kc=kc: e.tensor_mul(out=xt[:, kc, :], in0=xt[:, kc, :], in1=rstd), [h, hst])
                hh = op("pool", lambda e, kc=kc, b=b: e.tensor_scalar(
                    out=h1[b][:, kc, :], in0=xt[:, kc, :], scalar1=s1[:, kc:kc + 1], scalar2=modT[:, kc:kc + 1],
                    op0=ALU.mult, op1=ALU.add), [hn, h_mod, state["h1_free"][b]])
            free["rstd"] = hn
            state["xt_free"] = hh
            state["h1"][b] = hh

        rot = {"main": 0, "qk": 0, "x3": 0, "qn": 0, "zs": 0, "tp": 0}
        MAIN_BANKS = [2, 3, 4, 5]

        def next_bank():
            b = MAIN_BANKS[rot["main"] % len(MAIN_BANKS)]
            rot["main"] += 1
            return b

        def mm_group_f(i, col0):
            b = i % 2
            bk = next_bank()
            h = None
            for kc in range(8):
                h = op("pe", lambda e, kc=kc, bk=bk, b=b: e.matmul(
                    B.bank(bk), lhsT=Wsb[:, kc, col0:col0 + 128], rhs=h1[b][:, kc, :],
                    start=(kc == 0), stop=(kc == 7)),
                    ([state["h1"][b], bank_free[bk]] + wdeps(col0, col0 + 128)) if kc == 0 else [],
                    track=(kc == 7))
            return bk, h

        def mm_group_t(i, st, col0, ncols):
            b = i % 2
            bk = next_bank()
            h = None
            for kc in range(8):
                h = op("pe", lambda e, kc=kc, bk=bk, b=b: e.matmul(
                    B.bank(bk, ncols), lhsT=h1[b][:, kc, st * 128:(st + 1) * 128], rhs=Wsb[:, kc, col0:col0 + ncols],
                    start=(kc == 0), stop=(kc == 7)),
                    ([state["h1"][b], bank_free[bk]] + wdeps(col0, col0 + ncols)) if kc == 0 else [],
                    track=(kc == 7))
            return bk, h

        for i in range(NTILES):
            if i == 0:
                prep(0)
            b = i % 2
            c0t = i * 512
            for st in (range(4) if "d" in GROUPS else []):
                bk, hm = mm_group_t(i, st, 3072, 16)
                h = op("dve", lambda e, bk=bk, st=st: e.tensor_add(out=tdt[:, st, :], in0=B.bank(bk, 16), in1=dtb_sb),
                       [hm, hconst, fr("dts")])
                bank_free[bk] = h
                h2 = op("act", lambda e, st=st: e.activation(out=adt[:, st, :], in_=tdt[:, st, :], func=AF.Abs), [h])
                h2 = op("act", lambda e, st=st: e.activation(out=adt[:, st, :], in_=adt[:, st, :], func=AF.Exp, scale=-1.0), [h2])
                h2 = op("act", lambda e, st=st: e.activation(out=adt[:, st, :], in_=adt[:, st, :], func=AF.Ln, bias=1.0), [h2])
                h2 = op("dve", lambda e, st=st: e.scalar_tensor_tensor(out=dts[:, st, 0:16], in0=tdt[:, st, :], scalar=0.0,
                                                                      in1=adt[:, st, :], op0=ALU.max, op1=ALU.add), [h2])
                h2 = op("dve", lambda e, st=st: e.tensor_mul(out=dts[:, st, 16:32], in0=dts[:, st, 0:16], in1=A_sb), [h2, h_A])
            if "d" in GROUPS:
                free["dts"] = dma("sync", DTS[c0t:c0t + 512, :].rearrange("(st p) c -> p st c", p=128), dts, deps=[h2], dsem=sd("dts"))
            pend_qk = []
            for j in (range(16) if "q" in GROUPS else range(4)):
                if "q" not in GROUPS:
                    if j == 3 and i + 1 < NTILES:
                        prep(i + 1)
                    continue
                isq = j < 8
                col0 = (3088 if isq else 4112) + (j % 8) * 128
                bk, hm = mm_group_f(i, col0)
                s2_ = rot["qk"] % 2
                rot["qk"] += 1
                hs = op("act", lambda e, bk=bk, s2_=s2_: e.activation(out=qsq[s2_], in_=B.bank(bk), func=AF.Square),
                        [hm, fr(("qsq", s2_))])
                hr = op("act", lambda e, bk=bk, s2_=s2_: e.activation(out=qraw[s2_], in_=B.bank(bk), func=AF.Copy),
                        [hm, hs, fr(("qraw", s2_))])
                bank_free[bk] = [hs, hr]

                def qk_part2(s2_=s2_, hs=hs, hr=hr, isq=isq, j=j):
                    hp = op("pe", lambda e: e.matmul(B.bank(1), lhsT=blk64, rhs=qsq[s2_], start=True, stop=True),
                            [hs, h_blk, bank_free[1]])
                    free[("qsq", s2_)] = hp
                    hl = op("act", lambda e: e.activation(out=qrs[s2_], in_=B.bank(1), func=AF.Ln, bias=EPS),
                            [hp, fr(("qrs", s2_))])
                    bank_free[1] = hl
                    hl = op("act", lambda e: e.activation(out=qrs[s2_], in_=qrs[s2_], func=AF.Exp, scale=-0.5), [hl])
                    s3 = rot["qn"] % 3
                    rot["qn"] += 1
                    wc = 0 if isq else 1
                    hq = op("dve", lambda e: e.scalar_tensor_tensor(
                        out=qn[s3], in0=qraw[s2_], scalar=qkw_sb[:, wc:wc + 1], in1=qrs[s2_], op0=ALU.mult, op1=ALU.mult),
                        [hl, hr, h_qw, fr(("qn", s3))])
                    free[("qraw", s2_)] = hq
                    free[("qrs", s2_)] = hq
                    dst = (QT if isq else KT)[(j % 8) * 128:(j % 8 + 1) * 128, c0t:c0t + 512]
                    free[("qn", s3)] = dma("sync", dst, qn[s3], deps=[hq], dsem=sd(("qn", s3)))
                if pend_qk:
                    pend_qk.pop(0)()
                pend_qk.append(qk_part2)
                if j == 3 and i + 1 < NTILES:
                    prep(i + 1)
            while pend_qk:
                pend_qk.pop(0)()
            pend_x = []
            pend_mid = []
            for ct in (range(16) if "x" in GROUPS else []):
                col0 = 1024 + ct * 128
                bk, hm = mm_group_f(i, col0)
                s3 = rot["x3"] % 3
                rot["x3"] += 1
                ha = op("act", lambda e, bk=bk, s3=s3, ct=ct: e.activation(
                    out=acc[s3], in_=B.bank(bk), func=AF.Identity, scale=convw_sb[:, ct, 3:4], bias=convb_sb[:, ct:ct + 1]),
                    [hm, hconst, fr(("acc", s3))])
                hc_ = op("act", lambda e, bk=bk, s3=s3: e.activation(out=xr[s3][:, 3:515], in_=B.bank(bk), func=AF.Copy),
                         [hm, fr(("xr", s3))])
                bank_free[bk] = hc_
                hh1 = op("pool", lambda e, s3=s3, ct=ct: e.tensor_copy(out=xr[s3][:, 0:3], in_=halo[:, ct, :]),
                         [fr(("xr", s3)), h_halo0, fr(("halo", ct))])
                hh2 = op("pool", lambda e, s3=s3, ct=ct: e.tensor_copy(out=halo[:, ct, :], in_=xr[s3][:, 512:515]), [hc_, hh1])
                free[("halo", ct)] = hh2
                ht = ha
                for k in range(3):
                    ht = op("dve", lambda e, s3=s3, ct=ct, k=k: e.scalar_tensor_tensor(
                        out=acc[s3], in0=xr[s3][:, k:k + 512], scalar=convw_sb[:, ct, k:k + 1], in1=acc[s3],
                        op0=ALU.mult, op1=ALU.add), [ht, hc_, hh1])
                free[("xr", s3)] = [ht, hh2]
                def x_mid(ct=ct, s3=s3, ht=ht):
                    s5 = ct % 5
                    hx_ = op("act", lambda e: e.activation(out=xc[s5], in_=acc[s3], func=AF.Silu),
                             [ht, fr(("xc", s5))])
                    free[("acc", s3)] = hx_

                    def x_part2():
                        last = [hx_]
                        if ct < 12:
                            tpi = rot["tp"] % 2
                            rot["tp"] += 1
                            tp = B.bank(6 + tpi, 1024, BF16)[:, 0:512]
                            hp = None
                            for st in range(4):
                                hp = op("pe", lambda e, st=st: e.transpose(
                                    out=tp[:, st * 128:(st + 1) * 128], in_=xc[s5][:, st * 128:(st + 1) * 128], identity=ident),
                                    [hx_, h_id, fr(("tp", tpi))] if st == 0 else [], track=(st == 3))
                            if ct < 8:
                                dst_ap = XSst[:, :, ct * 128:(ct + 1) * 128]
                                fkey = "XSst"
                            else:
                                dst_ap = BMst[:, :, (ct - 8) * 128:(ct - 7) * 128]
                                fkey = "BMst"
                            he = op("dve", lambda e: e.tensor_copy(
                                out=dst_ap, in_=tp.rearrange("p (s c) -> p s c", s=4)), [hp, fr(fkey)])
                            free[("tp", tpi)] = he
                            last.append(hp)
                            free.setdefault("st_writes", []).append(he)
                        if ct >= 8:
                            hd = dma("sync", BCT[(ct - 8) * 128:(ct - 7) * 128, c0t:c0t + 512], xc[s5], deps=[hx_],
                                     dsem=sd(("xc", s5)))
                            last.append(hd)
                        free[("xc", s5)] = last

                    pend_x.append(x_part2)
                    if len(pend_x) > 2:
                        pend_x.pop(0)()
                if pend_mid:
                    pend_mid.pop(0)()
                pend_mid.append(x_mid)
            while pend_mid:
                pend_mid.pop(0)()
            while pend_x:
                pend_x.pop(0)()
            if "x" not in GROUPS:
                free["st_writes"] = []
            hd = dma("sync", XS[c0t:c0t + 512, :].rearrange("(st p) c -> p st c", p=128), XSst,
                     deps=free["st_writes"], dsem=sd("XSst"))
            free["XSst"] = hd
            phaseA_out.append(hd)
            hd = dma("sync", BM[c0t:c0t + 512, :].rearrange("(st p) c -> p st c", p=128), BMst,
                     deps=free["st_writes"], dsem=sd("BMst"))
            free["BMst"] = hd
            phaseA_out.append(hd)
            free["st_writes"] = []
            for st in (range(4) if "z" in GROUPS else []):
                for half in range(2):
                    bk, hm = mm_group_t(i, st, half * 512, 512)
                    s3 = rot["zs"] % 3
                    rot["zs"] += 1
                    hz = op("act", lambda e, bk=bk, s3=s3: e.activation(out=zs[s3], in_=B.bank(bk), func=AF.Silu),
                            [hm, fr(("zs", s3))])
                    bank_free[bk] = hz
                    r0 = c0t + st * 128
                    free[("zs", s3)] = dma("sync", ZS[r0:r0 + 128, half * 512:(half + 1) * 512], zs[s3], deps=[hz], dsem=sd(("zs", s3)))
                    phaseA_out.append(free[("zs", s3)])
            hv = []
            for st in (range(4) if "v" in GROUPS else []):
                for half in range(2):
                    bk, hm = mm_group_t(i, st, 5136 + half * 512, 512)
                    h = op("dve", lambda e, bk=bk, st=st, half=half: e.tensor_copy(
                        out=VAst[:, st, half * 8:(half + 1) * 8, 0:64],
                        in_=B.bank(bk).rearrange("p (h d) -> p h d", h=8)), [hm, h_va1, fr("VAst")])
                    bank_free[bk] = h
                    hv.append(h)
            free["VAst"] = dma("sync", VA[c0t:c0t + 512, :].rearrange("(st p) c -> p st c", p=128),
                               VAst.rearrange("p s h d -> p s (h d)"), deps=hv, dsem=sd("VAst"))
            phaseA_out.append(free["VAst"])
            state["h1_free"][b] = (B.sem["pe"], B.cnt["pe"])
        phaseA_done = [(v[0], v[1]) for v in dsd.values()]
        out_handles += phaseA_done
        B.release(mA)


    def phase_S():
        B.release(persist_mark)
        mS = B.mark()
        B.phase_barrier(out_handles)
        NCH = 32
        tri_f = cst_f[:, 3, :]
        strict_f = cst_f[:, 4, :]
        tri_b = B.alloc([128], BF16)
        ssdnw_sb = B.alloc([1024], F32)
        dts_all = B.alloc([NCH, 32], F32)
        xs_ = [B.alloc([16, 64], BF16) for _ in range(2)]
        bm_ = [B.alloc([512], BF16) for _ in range(2)]
        bct_ = [B.alloc([8, 128], BF16) for _ in range(2)]
        zs_ = [B.alloc([1024], BF16) for _ in range(2)]
        R = B.alloc([16, 128], F32)
        Lt = [B.alloc([16, 128], BF16) for _ in range(2)]
        cbm = [B.alloc([4, 128], BF16) for _ in range(2)]
        G = [B.alloc([16, 128], BF16) for _ in range(2)]
        xdt = [B.alloc([16, 64], BF16) for _ in range(2)]
        xdec = [B.alloc([16, 64], BF16) for _ in range(2)]
        t1 = [B.alloc([16, 64], F32) for _ in range(2)]
        yo = [B.alloc([1024], BF16) for _ in range(2)]
        sqj = B.alloc([256], BF16)
        ed = [B.alloc([32], F32) for _ in range(2)]
        ss = [B.alloc([4], F32) for _ in range(2)]
        Sst = B.alloc([16, 64], F32)
        Sb = B.alloc([16, 64], BF16)

        ds0 = B.dsem("s_c")
        hc0 = dma("sync", ssdnw_sb, ssdnw, dsem=ds0)
        hc0 = dma("sync", dts_all, DTS.rearrange("(c p) k -> p c k", p=128), dsem=ds0)
        h_trib = op("dve", lambda e: e.tensor_copy(out=tri_b, in_=tri_f))
        h_s0 = op("pool", lambda e: e.memset(Sst, 0.0))
        h_sb0 = op("pool", lambda e: e.memset(Sb, 0.0))
        ds_l = [B.dsem("s_l0"), B.dsem("s_l1")]
        ds_y = [B.dsem("s_y0"), B.dsem("s_y1")]
        fr_ = {}
        bfree = {b: None for b in range(8)}

        def F(k):
            return fr_.get(k)

        hS = h_s0
        hSb = h_sb0
        for c in range(NCH):
            b = c % 2
            r0 = c * 128
            hl = dma("sync", xs_[b].rearrange("p h d -> p (h d)"), XS[r0:r0 + 128, :], deps=[F(("ld", b))], dsem=ds_l[b])
            hl = dma("sync", bm_[b], BM[r0:r0 + 128, :], dsem=ds_l[b])
            hl = dma("sync", bct_[b], BCT[:, r0:r0 + 128].rearrange("(g p) t -> p g t", p=128), dsem=ds_l[b])
            hl = dma("sync", zs_[b], ZS[r0:r0 + 128, :], dsem=ds_l[b])
            dA = dts_all[:, c, 16:32]
            dtc = dts_all[:, c, 0:16]
            hR = op("pool", lambda e, dA=dA: e.tensor_tensor(
                out=R, in0=tri_f.unsqueeze(1).to_broadcast([128, 16, 128]),
                in1=dA.unsqueeze(2).to_broadcast([128, 16, 128]), op=ALU.mult), [hc0, hconst, F("R")])
            hseg = []
            for q4 in range(4):
                hseg.append(op("pe", lambda e, q4=q4: e.matmul(
                    B.bank(q4), lhsT=strict_f, rhs=R[:, q4 * 4:(q4 + 1) * 4, :].rearrange("p h l -> p (h l)"),
                    start=True, stop=True), [hR, bfree[q4]]))
            fr_["R"] = hseg[-1]
            hsm = op("pe", lambda e, dA=dA: e.matmul(B.bank(5, 16), lhsT=tri_f, rhs=dA, start=True, stop=True),
                     [hc0, bfree[5]], track=False)
            hsm = op("pe", lambda e, dA=dA: e.matmul(B.bank(5, 32)[:, 16:32], lhsT=ones_f, rhs=dA, start=True, stop=True),
                     [h_onesf])
            hL = None
            for q4 in range(4):
                hL = op("act", lambda e, q4=q4, b=b: e.activation(
                    out=Lt[b][:, q4 * 4:(q4 + 1) * 4, :].rearrange("p h l -> p (h l)"), in_=B.bank(q4), func=AF.Exp),
                    [hseg[q4], F(("Lt", b))])
                bfree[q4] = hL
            hed = op("act", lambda e, b=b: e.activation(out=ed[b], in_=B.bank(5, 32), func=AF.Exp), [hsm, F(("ed", b))])
            bfree[5] = hed
            hcb = None
            for g in range(4):
                hcb = op("pe", lambda e, g=g, b=b: e.matmul(
                    B.bank(4)[:, g * 128:(g + 1) * 128], lhsT=bct_[b][:, g, :], rhs=bct_[b][:, 4 + g, :],
                    start=True, stop=True), [hl, bfree[4]] if g == 0 else [], track=(g == 3))
            hcbm = op("dve", lambda e, b=b: e.tensor_tensor(
                out=cbm[b], in0=B.bank(4).rearrange("p (g l) -> p g l", g=4),
                in1=tri_b.unsqueeze(1).to_broadcast([128, 4, 128]), op=ALU.mult), [hcb, h_trib, F(("cbm", b))])
            bfree[4] = hcbm
            hG = None
            for g in range(4):
                hG = op("dve", lambda e, g=g, b=b: e.tensor_tensor(
                    out=G[b][:, 4 * g:4 * g + 4, :], in0=Lt[b][:, 4 * g:4 * g + 4, :],
                    in1=cbm[b][:, g, :].unsqueeze(1).to_broadcast([128, 4, 128]), op=ALU.mult),
                    [hL, hcbm, F(("G", b))])
            hxdt = op("pool", lambda e, b=b, dtc=dtc: e.tensor_tensor(
                out=xdt[b], in0=xs_[b], in1=dtc.unsqueeze(2).to_broadcast([128, 16, 64]), op=ALU.mult),
                [hl, hc0, F(("xdt", b))])
            hxdec = op("pool", lambda e, b=b: e.tensor_tensor(
                out=xdec[b], in0=xdt[b], in1=Lt[b][:, :, 127:128].to_broadcast([128, 16, 64]), op=ALU.mult),
                [hxdt, hL, F(("xdec", b))])
            hyd = None
            for h_ in range(16):
                hyd = op("pe", lambda e, h_=h_, b=b: e.matmul(
                    B.psum[:, 3072 + h_ * 64:3072 + (h_ + 1) * 64], lhsT=G[b][:, h_, :], rhs=xdt[b][:, h_, :],
                    start=True, stop=True), [hG, hxdt, bfree[6], bfree[7]] if h_ == 0 else [], track=(h_ == 15))
            fr_[("G", b)] = hyd
            hyo = None
            for g in range(4):
                hyo = op("pe", lambda e, g=g, b=b: e.matmul(
                    B.psum[:, g * 256:(g + 1) * 256], lhsT=bct_[b][:, 4 + g, :],
                    rhs=Sb[:, 4 * g:4 * g + 4, :].rearrange("p h d -> p (h d)"), start=True, stop=True),
                    [hSb, bfree[0], bfree[1]] if g == 0 else [], track=(g == 3))
            hst = None
            for g in range(4):
                hst = op("pe", lambda e, g=g, b=b: e.matmul(
                    B.psum[:, 1024 + g * 256:1024 + (g + 1) * 256], lhsT=bm_[b][:, g * 128:(g + 1) * 128],
                    rhs=xdec[b][:, 4 * g:4 * g + 4, :].rearrange("p h d -> p (h d)"), start=True, stop=True),
                    [hxdec, bfree[2], bfree[3]] if g == 0 else [], track=(g == 3))
            fr_[("xdec", b)] = hst
            fr_[("xdt", b)] = [hyd, hxdec]
            fr_[("Lt", b)] = [hG, hxdec]
            fr_[("cbm", b)] = hG
            yoff = B.psum[:, 0:1024].rearrange("p (h d) -> p h d", h=16)
            ydg = B.psum[:, 3072:4096].rearrange("p (h d) -> p h d", h=16)
            h1_ = op("dve", lambda e, b=b: e.tensor_tensor(
                out=t1[b], in0=yoff, in1=ed[b][:, 0:16].unsqueeze(2).to_broadcast([128, 16, 64]), op=ALU.mult),
                [hyo, hed, F(("t1", b))])
            bfree[0] = h1_
            bfree[1] = h1_
            h2_ = op("dve", lambda e, b=b: e.tensor_tensor(out=t1[b], in0=t1[b], in1=ydg, op=ALU.add), [h1_, hyd])
            bfree[6] = h2_
            bfree[7] = h2_
            h3_ = op("dve", lambda e, b=b: e.tensor_tensor(
                out=yo[b].rearrange("p (h d) -> p h d", h=16), in0=xs_[b],
                in1=dskip_sb.unsqueeze(2).to_broadcast([128, 16, 64]), op=ALU.mult), [hl, hconst, F(("yo", b))])
            h3_ = op("dve", lambda e, b=b: e.tensor_tensor(
                out=t1[b], in0=t1[b], in1=yo[b].rearrange("p (h d) -> p h d", h=16), op=ALU.add), [h2_, h3_])
            h4_ = op("pool", lambda e, b=b: e.tensor_tensor(
                out=t1[b], in0=t1[b], in1=zs_[b].rearrange("p (h d) -> p h d", h=16), op=ALU.mult), [h3_, hl])
            t1f = t1[b].rearrange("p h d -> p (h d)")
            hq_ = None
            for g in range(4):
                hq_ = op("act", lambda e, g=g, b=b, t1f=t1f: e.activation(
                    out=sqj, in_=t1f[:, g * 256:(g + 1) * 256], func=AF.Square, accum_out=ss[b][:, g:g + 1]),
                    [h4_, hq_, F(("ss", b))])
            hq_ = op("act", lambda e, b=b: e.activation(out=ss[b], in_=ss[b], func=AF.Ln, scale=1.0 / 256, bias=EPS), [hq_])
            hq_ = op("act", lambda e, b=b: e.activation(out=ss[b], in_=ss[b], func=AF.Exp, scale=-0.5), [hq_])
            hy = None
            for g in range(4):
                hy = op("dve", lambda e, g=g, b=b, t1f=t1f: e.scalar_tensor_tensor(
                    out=yo[b][:, g * 256:(g + 1) * 256], in0=t1f[:, g * 256:(g + 1) * 256], scalar=ss[b][:, g:g + 1],
                    in1=ssdnw_sb[:, g * 256:(g + 1) * 256], op0=ALU.mult, op1=ALU.mult), [hq_, h4_, hc0])
            fr_[("ss", b)] = hy
            fr_[("t1", b)] = hy
            hdy = dma("sync", YS[r0:r0 + 128, :], yo[b], deps=[hy], dsem=ds_y[b])
            fr_[("yo", b)] = hdy
            hu = op("dve", lambda e, b=b: e.tensor_tensor(
                out=Sst, in0=Sst, in1=ed[b][:, 16:32].unsqueeze(2).to_broadcast([128, 16, 64]), op=ALU.mult),
                [hS, hed])
            hu = op("dve", lambda e: e.tensor_tensor(
                out=Sst, in0=Sst, in1=B.psum[:, 1024:2048].rearrange("p (h d) -> p h d", h=16), op=ALU.add), [hu, hst])
            bfree[2] = hu
            bfree[3] = hu
            hS = hu
            hSb = op("dve", lambda e: e.tensor_copy(out=Sb, in_=Sst), [hu, hyo])
            fr_[("ed", b)] = [h1_, hu]
            fr_[("ld", b)] = [hy, hyd, hst, hcb, hyo, h4_, hxdt]
        out_handles.append((ds_y[0][0], ds_y[0][1]))
        out_handles.append((ds_y[1][0], ds_y[1][1]))
        B.release(mS)

    if "S" in phases:
        phase_S()


    def phase_B():
        B.release(persist_mark)
        mB = B.mark()
        B.phase_barrier(out_handles)
        mask2 = B.alloc([2, 256], BF16)
        qt = [B.alloc([4096], BF16) for _ in range(2)]
        kt = [B.alloc([4096], BF16) for _ in range(2)]
        vv = [[B.alloc([32, 130], BF16) for _ in range(3)] for _ in range(2)]
        ost = [B.alloc([32, 130], F32) for _ in range(2)]
        P = [B.alloc([2, 256], BF16) for _ in range(6)]
        hm_ = None
        for h_ in range(2):
            hm_ = op("dve", lambda e, h_=h_: e.tensor_copy(out=mask2[:, h_, 0:128], in_=cst_f[:, 3, :]), [hconst])
            hm_ = op("dve", lambda e, h_=h_: e.tensor_copy(out=mask2[:, h_, 128:256], in_=cst_f[:, 5, :]), [hconst])
        ds_in = [B.dsem("b_in0"), B.dsem("b_in1")]
        ds_od = [B.dsem("b_od0"), B.dsem("b_od1")]
        VAv = VA
        ODv = OD
        fr_ = {}
        bfree = {b: None for b in range(8)}
        rot = {"S": 0, "O": 0, "P": 0, "ost": 0, "m": 0}
        DIL = [(1, 32), (4, 8), (16, 2)]

        def F(k):
            return fr_.get(k)

        for nm in ("wo", "w1", "w2"):
            wcast["sem"][nm] = B.dsem("wc_" + nm)
        for kc in range(16):
            wcast["list"].append(("wo", WoB[kc * 128:(kc + 1) * 128, :], w_out[kc * 128:(kc + 1) * 128, :]))
        for cq in range(4):
            for kc in range(8):
                wcast["list"].append(("w1", W1B[kc * 128:(kc + 1) * 128, cq * 1024:(cq + 1) * 1024],
                                      w_ff1[kc * 128:(kc + 1) * 128, cq * 1024:(cq + 1) * 1024]))
        for kc in range(32):
            wcast["list"].append(("w2", W2B[kc * 128:(kc + 1) * 128, :], w_ff2[kc * 128:(kc + 1) * 128, :]))
        wcast["h"] = {}

        def drip(n):
            for _ in range(n):
                if wcast["list"]:
                    nm, o_, i_ = wcast["list"].pop(0)
                    wcast["h"][nm] = dma("pool", o_, i_, dsem=wcast["sem"][nm])

        for hp in range(8):
            b = hp % 2
            c0 = hp * 130
            hin = dma("sync", qt[b], QT[hp * 128:(hp + 1) * 128, :], deps=[F(("in", b))], dsem=ds_in[b])
            hin = dma("sync", kt[b], KT[hp * 128:(hp + 1) * 128, :], dsem=ds_in[b])
            for jq in range(4):
                hin = dma("sync", vv[b][0][:, jq * 8:(jq + 1) * 8, :],
                          VAv[jq * 1024:(jq + 1) * 1024, c0:c0 + 130].rearrange("(j i) c -> i j c", i=128), dsem=ds_in[b])
            for r in range(4):
                hin = dma("sync", vv[b][1][:, r * 8:(r + 1) * 8, :],
                          VAv[:, c0:c0 + 130].rearrange("(j i r) c -> i r j c", i=128, r=4)[:, r, :, :], dsem=ds_in[b])
            for r in range(16):
                hin = dma("sync", vv[b][2][:, r * 2:(r + 1) * 2, :],
                          VAv[:, c0:c0 + 130].rearrange("(j i r) c -> i r j c", i=128, r=16)[:, r, :, :], dsem=ds_in[b])
            last_reads = []
            steps = []
            for di, (d, NB) in enumerate(DIL):
                os_i = rot["ost"] % 2
                rot["ost"] += 1
                for r in range(d):
                    for j in range(NB):
                        steps.append(dict(di=di, d=d, NB=NB, r=r, j=j, os_i=os_i, last=(r == d - 1 and j == NB - 1)))

            def emit_qk(st_, b=b):
                d, NB, r, j = st_["d"], st_["NB"], st_["r"], st_["j"]
                nq = 256 if j + 1 < NB else 128
                t0 = r + d * 128 * j
                ktok = slice(t0, t0 + d * 127 + 1, d)
                qtok = slice(t0, t0 + d * (nq - 1) + 1, d)
                bs = rot["S"] % 2
                rot["S"] += 1
                hqk = None
                for h_ in range(2):
                    hqk = op("pe", lambda e, h_=h_: e.matmul(
                        B.bank(2 * bs + h_)[:, 0:nq], lhsT=kt[b][64 * h_:64 * h_ + 64, ktok],
                        rhs=qt[b][64 * h_:64 * h_ + 64, qtok], start=True, stop=True),
                        [hin, bfree[2 * bs], bfree[2 * bs + 1]] if h_ == 0 else [], track=(h_ == 1))
                pi = rot["P"] % 6
                rot["P"] += 1
                Pc = P[pi]
                hp_ = op("act", lambda e: e.activation(
                    out=Pc[:, :, 0:nq],
                    in_=B.psum[:, 2 * bs * 512:(2 * bs + 2) * 512].rearrange("p (h q) -> p h q", h=2)[:, :, 0:nq],
                    func=AF.Exp), [hqk, F(("P", pi))])
                bfree[2 * bs] = hp_
                bfree[2 * bs + 1] = hp_
                meng = "dve" if rot["m"] % 2 == 0 else "pool"
                rot["m"] += 1
                hmk = op(meng, lambda e: e.tensor_tensor(
                    out=Pc[:, :, 0:nq], in0=Pc[:, :, 0:nq], in1=mask2[:, :, 0:nq], op=ALU.mult), [hp_, hm_])
                return dict(Pc=Pc, pi=pi, hmk=hmk)

            def emit_pv(st_, cur, prev, b=b, c0=c0):
                di, d, NB, r, j, os_i = st_["di"], st_["d"], st_["NB"], st_["r"], st_["j"], st_["os_i"]
                bo = 4 + rot["O"] % 4
                rot["O"] += 1
                tile_c = r * NB + j
                Pc = cur["Pc"]
                hpv = None
                for h_ in range(2):
                    oap = B.bank(bo)[:, h_ * 65:(h_ + 1) * 65]
                    if j > 0:
                        Pp = prev["Pc"]
                        op("pe", lambda e, h_=h_, oap=oap, Pp=Pp: e.matmul(
                            oap, lhsT=Pp[:, h_, 128:256], rhs=vv[b][di][:, tile_c - 1, h_ * 65:(h_ + 1) * 65],
                            start=True, stop=False), [prev["hmk"], cur["hmk"], bfree[bo]] if h_ == 0 else [], track=False)
                    hpv = op("pe", lambda e, h_=h_, oap=oap: e.matmul(
                        oap, lhsT=Pc[:, h_, 0:128], rhs=vv[b][di][:, tile_c, h_ * 65:(h_ + 1) * 65],
                        start=(j == 0), stop=True), [cur["hmk"], bfree[bo]] if (h_ == 0 and j == 0) else [],
                        track=(h_ == 1))
                if j > 0:
                    fr_[("P", prev["pi"])] = hpv
                fr_[("P", cur["pi"])] = hpv
                hev = op("dve", lambda e: e.tensor_copy(out=ost[os_i][:, tile_c, :], in_=B.bank(bo)[:, 0:130]),
                         [hpv, F(("ost", os_i))])
                bfree[bo] = hev
                last_reads[:] = [hpv]
                if st_["last"]:
                    dst = ODv[di, :, c0:c0 + 130].rearrange("(j i r) c -> i r j c", i=128, r=d)
                    hod = None
                    for r2 in range(d):
                        for jq in range(0, NB, 8):
                            je = min(NB, jq + 8)
                            hod = dma("sync", dst[:, r2, jq:je, :], ost[os_i][:, r2 * NB + jq:r2 * NB + je, :], deps=[hev],
                                      dsem=ds_od[os_i])
                    fr_[("ost", os_i)] = hod

            nst = len(steps)
            info = [None] * nst
            LA = 2
            for t in range(nst + LA):
                if t % 6 == 0:
                    drip(1 if hp > 0 else 2)
                if t < nst:
                    info[t] = emit_qk(steps[t])
                u_ = t - LA
                if u_ >= 0:
                    emit_pv(steps[u_], info[u_], info[u_ - 1] if steps[u_]["j"] > 0 else None)
            fr_[("in", b)] = last_reads
        drip(1000)
        out_handles.append((ds_od[0][0], ds_od[0][1]))
        out_handles.append((ds_od[1][0], ds_od[1][1]))
        B.release(mB)

    if "B" in phases:
        phase_B()


    def phase_C1():
        B.release(persist_mark)
        B.phase_barrier(out_handles)
        Wo = B.alloc([16, 1024], BF16)
        attnw_sb = B.alloc([1024], F32)
        od = [B.alloc([3, 1040], F32) for _ in range(2)]
        ys_ = [B.alloc([1024], BF16) for _ in range(2)]
        rden = [B.alloc([16], F32) for _ in range(2)]
        of = [B.alloc([16, 64], F32) for _ in range(2)]
        ya = [B.alloc([1024], BF16) for _ in range(2)]
        ssq = [B.alloc([2], F32) for _ in range(2)]
        sqj = B.alloc([1024], BF16)
        yT = [B.alloc([16, 512], BF16) for _ in range(2)]
        xt = B.alloc([8, 512], F32)
        sq = [B.alloc([512], BF16) for _ in range(2)]
        rstd = B.alloc([512], F32)
        tmp = [B.alloc([512], F32) for _ in range(2)]
        h2 = [B.alloc([8, 512], BF16) for _ in range(2)]
        ds_w = B.dsem("c1w")
        hw = None
        for kc in range(16):
            hw = dma("sync", Wo[:, kc, :], WoB[kc * 128:(kc + 1) * 128, :], deps=[wcast["h"].get("wo")], dsem=ds_w)
        ds_aw = B.dsem("c1aw")
        haw = dma("sync", attnw_sb, attnw, dsem=ds_aw)
        ds_l = [B.dsem("c1l0"), B.dsem("c1l1")]
        ds_x = B.dsem("c1x")
        ds_o1 = B.dsem("c1o1")
        ds_o2 = [B.dsem("c1o2a"), B.dsem("c1o2b")]
        fr_ = {}
        bfree = {b: None for b in range(8)}
        rot = {"tp": 0, "mb": 0}

        def F(k):
            return fr_.get(k)

        sidx_ = {"v": 0}
        hyT_of = {}

        def combine(i):
            b = i % 2
            c0t = i * 512
            hyT = []
            for st in range(4):
                s_ = sidx_["v"] % 2
                sidx_["v"] += 1
                r0 = c0t + st * 128
                hl = dma("sync", od[s_], OD[:, r0:r0 + 128, :].rearrange("d t c -> t d c"), deps=[F(("od", s_))], dsem=ds_l[s_])
                hl = dma("sync", ys_[s_], YS[r0:r0 + 128, :], deps=[F(("ys", s_))], dsem=ds_l[s_])
                o0 = od[s_][:, 0, :]
                h = op("dve", lambda e, s_=s_, o0=o0: e.tensor_add(out=o0, in0=o0, in1=od[s_][:, 1, :]), [hl])
                h = op("dve", lambda e, s_=s_, o0=o0: e.tensor_add(out=o0, in0=o0, in1=od[s_][:, 2, :]), [h])
                o3 = o0.rearrange("p (h d) -> p h d", h=16)
                h = op("dve", lambda e, s_=s_, o3=o3: e.reciprocal(out=rden[s_], in_=o3[:, :, 64]), [h, F(("rden", s_))])
                h = op("dve", lambda e, s_=s_, o3=o3: e.tensor_tensor(
                    out=of[s_], in0=o3[:, :, 0:64], in1=rden[s_].unsqueeze(2).to_broadcast([128, 16, 64]), op=ALU.mult),
                    [h, F(("of", s_))])
                fr_[("od", s_)] = h
                off = of[s_].rearrange("p h d -> p (h d)")
                ha = op("act", lambda e, s_=s_, off=off: e.activation(out=sqj, in_=off, func=AF.Square,
                                                                     accum_out=ssq[s_][:, 0:1]), [h, F(("ssq", s_)), F("sqj")])
                fr_["sqj"] = ha
                ha = op("act", lambda e, s_=s_: e.activation(out=ssq[s_][:, 1:2], in_=ssq[s_][:, 0:1], func=AF.Ln,
                                                             scale=1.0 / 1024, bias=EPS), [ha])
                ha = op("act", lambda e, s_=s_: e.activation(out=ssq[s_][:, 1:2], in_=ssq[s_][:, 1:2], func=AF.Exp, scale=-0.5), [ha])
                hya = op("dve", lambda e, s_=s_, off=off: e.scalar_tensor_tensor(
                    out=ya[s_], in0=off, scalar=ssq[s_][:, 1:2], in1=attnw_sb, op0=ALU.mult, op1=ALU.mult),
                    [ha, haw, F(("ya", s_))])
                fr_[("ssq", s_)] = hya
                fr_[("of", s_)] = hya
                fr_[("rden", s_)] = hya
                hlast = None
                for q4 in range(4):
                    tpi = rot["tp"] % 2
                    rot["tp"] += 1
                    tp = B.bank(6 + tpi, 1024, BF16)[:, 0:512]
                    hp = None
                    for k4 in range(4):
                        kc = q4 * 4 + k4
                        src = ys_[s_][:, kc * 128:(kc + 1) * 128] if kc < 8 else ya[s_][:, (kc - 8) * 128:(kc - 7) * 128]
                        hp = op("pe", lambda e, k4=k4, tp=tp, src=src: e.transpose(
                            out=tp[:, k4 * 128:(k4 + 1) * 128], in_=src, identity=ident),
                            [hl, hya, h_id, F(("tp", tpi))] if k4 == 0 else [], track=(k4 == 3))
                    he = op("dve", lambda e, tp=tp, b=b, q4=q4, st=st: e.tensor_copy(
                        out=yT[b][:, q4 * 4:q4 * 4 + 4, st * 128:(st + 1) * 128],
                        in_=tp.rearrange("p (k t) -> p k t", k=4)), [hp, F(("yT", b))])
                    fr_[("tp", tpi)] = he
                    hlast = hp
                    hyT.append(he)
                fr_[("ys", s_)] = hlast
                fr_[("ya", s_)] = hlast
            hyT_of[i] = hyT

        def finish_(i):
            b = i % 2
            c0t = i * 512
            hyT = hyT_of[i]
            hx = None
            for kc in range(8):
                hx = dma("sync", xt[:, kc, :], xT[kc * 128:(kc + 1) * 128, c0t:c0t + 512], deps=[F("xt")], dsem=ds_x)
            hx1 = None
            hmm_last = None
            for f in range(8):
                bk = 2 + rot["mb"] % 4
                rot["mb"] += 1
                hm = None
                for kc in range(16):
                    hm = op("pe", lambda e, kc=kc, f=f, bk=bk, b=b: e.matmul(
                        B.bank(bk), lhsT=Wo[:, kc, f * 128:(f + 1) * 128], rhs=yT[b][:, kc, :],
                        start=(kc == 0), stop=(kc == 15)), (hyT + [hw, bfree[bk]]) if kc == 0 else [], track=(kc == 15))
                hx1 = op("dve", lambda e, f=f, bk=bk: e.scalar_tensor_tensor(
                    out=xt[:, f, :], in0=B.bank(bk), scalar=modT[:, 16 + f:17 + f], in1=xt[:, f, :],
                    op0=ALU.mult, op1=ALU.add), [hm, hx, h_mod])
                bfree[bk] = hx1
                hmm_last = hm
            fr_[("yT", b)] = hmm_last
            hst1 = None
            for kc in range(8):
                hst1 = dma("sync", X1T[kc * 128:(kc + 1) * 128, c0t:c0t + 512], xt[:, kc, :], deps=[hx1], dsem=ds_o1)
            hst = None
            for kc in range(8):
                sl = kc % 2
                hs = op("pool", lambda e, kc=kc, sl=sl: e.tensor_tensor(out=sq[sl], in0=xt[:, kc, :], in1=xt[:, kc, :],
                                                                      op=ALU.mult), [hx1, F(("sq", sl))])
                hst = op("pe", lambda e, kc=kc, sl=sl: e.matmul(B.bank(0), lhsT=ones_bf, rhs=sq[sl],
                                                               start=(kc == 0), stop=(kc == 7)),
                         [hs, h_ones, bfree[0] if kc == 0 else None])
                fr_[("sq", sl)] = hst
            h = op("act", lambda e: e.activation(out=rstd, in_=B.bank(0), func=AF.Ln, scale=1.0 / D, bias=EPS),
                   [hst, F("rstd")])
            bfree[0] = h
            h = op("act", lambda e: e.activation(out=rstd, in_=rstd, func=AF.Exp, scale=-0.5), [h])
            hh = None
            hn = None
            for kc in range(8):
                sl = kc % 2
                hn = op("dve", lambda e, kc=kc, sl=sl: e.tensor_mul(out=tmp[sl], in0=xt[:, kc, :], in1=rstd),
                        [h, hx1, F(("tmp", sl))])
                hh = op("pool", lambda e, kc=kc, b=b, sl=sl: e.tensor_scalar(
                    out=h2[b][:, kc, :], in0=tmp[sl], scalar1=s2[:, kc:kc + 1], scalar2=modT[:, 24 + kc:25 + kc],
                    op0=ALU.mult, op1=ALU.add), [hn, h_mod, F(("h2", b))])
                fr_[("tmp", sl)] = hh
            fr_["rstd"] = hn
            hd = None
            for kc in range(8):
                hd = dma("sync", H2T[kc * 128:(kc + 1) * 128, c0t:c0t + 512], h2[b][:, kc, :], deps=[hh], dsem=ds_o2[b])
            fr_[("h2", b)] = hd
            fr_["xt"] = [hst1, hn, hst]

        combine(0)
        for i in range(NT):
            if i + 1 < NT:
                combine(i + 1)
            finish_(i)
        out_handles.append((ds_o1[0], ds_o1[1]))
        out_handles.append((ds_o2[0][0], ds_o2[0][1]))
        out_handles.append((ds_o2[1][0], ds_o2[1][1]))

    def phase_C2():
        B.release(persist_mark)
        B.phase_barrier(out_handles)
        W1 = B.alloc([8, 4096], BF16)
        W2 = B.alloc([32, 1024], BF16)
        TN = 256
        h2 = [B.alloc([8, TN], BF16) for _ in range(2)]
        u = B.alloc([32, TN], BF16)
        rr = [B.alloc([TN], F32) for _ in range(3)]
        x1f = [B.alloc([TN], F32) for _ in range(3)]
        oo = [B.alloc([TN], F32) for _ in range(3)]
        hw1c = []
        for cq in range(4):
            dsq = B.dsem("c2w1")
            hq_ = None
            for kc in range(8):
                hq_ = dma("sync", W1[:, kc, cq * 1024:(cq + 1) * 1024], W1B[kc * 128:(kc + 1) * 128, cq * 1024:(cq + 1) * 1024],
                          deps=[wcast["h"].get("w1")], dsem=dsq)
            hw1c.append(hq_)
        ds_w2 = B.dsem("c2w2")
        hw2 = None
        for kc in range(32):
            hw2 = dma("sync", W2[:, kc, :], W2B[kc * 128:(kc + 1) * 128, :], deps=[wcast["h"].get("w2")], dsem=ds_w2)
        ds_h = [B.dsem("c2h0"), B.dsem("c2h1")]
        ds_x = [B.dsem("c2x0"), B.dsem("c2x1"), B.dsem("c2x2")]
        ds_o = [B.dsem("c2o0"), B.dsem("c2o1"), B.dsem("c2o2")]
        fr_ = {}
        bfree = {b: None for b in range(8)}
        rot = {"mb": 0, "r": 0, "x": 0}

        def F(k):
            return fr_.get(k)

        for i in range(S // TN):
            b = i % 2
            c0 = i * TN
            hh = None
            for kc in range(8):
                hh = dma("sync", h2[b][:, kc, :], H2T[kc * 128:(kc + 1) * 128, c0:c0 + TN], deps=[F(("h2", b))], dsem=ds_h[b])
            hu_all = []
            hm = None
            for m in range(32):
                bk = rot["mb"] % 6
                rot["mb"] += 1
                for kc in range(8):
                    hm = op("pe", lambda e, kc=kc, m=m, bk=bk, b=b: e.matmul(
                        B.bank(bk, TN), lhsT=W1[:, kc, m * 128:(m + 1) * 128], rhs=h2[b][:, kc, :],
                        start=(kc == 0), stop=(kc == 7)), [hh, hw1c[m // 8], bfree[bk]] if kc == 0 else [], track=(kc == 7))
                ri = rot["r"] % 3
                rot["r"] += 1
                hr = op("act", lambda e, bk=bk, ri=ri: e.activation(out=rr[ri], in_=B.bank(bk, TN), func=AF.Relu),
                        [hm, F(("rr", ri))])
                bfree[bk] = hr
                hu = op("pool", lambda e, ri=ri, m=m: e.tensor_tensor(out=u[:, m, :], in0=rr[ri], in1=rr[ri], op=ALU.mult),
                        [hr, F("u")])
                fr_[("rr", ri)] = hu
                hu_all.append(hu)
            fr_[("h2", b)] = hm
            hm2 = None
            for f in range(8):
                bk = 6 + f % 2
                xi = rot["x"] % 3
                rot["x"] += 1
                hxl = dma("sync", x1f[xi], X1T[f * 128:(f + 1) * 128, c0:c0 + TN], deps=[F(("x1f", xi))], dsem=ds_x[xi])
                for kc in range(32):
                    hm2 = op("pe", lambda e, kc=kc, f=f, bk=bk: e.matmul(
                        B.bank(bk, TN), lhsT=W2[:, kc, f * 128:(f + 1) * 128], rhs=u[:, kc, :],
                        start=(kc == 0), stop=(kc == 31)), (hu_all + [hw2, bfree[bk]]) if kc == 0 else [], track=(kc == 31))
                ho = op("dve", lambda e, f=f, bk=bk, xi=xi: e.scalar_tensor_tensor(
                    out=oo[xi], in0=B.bank(bk, TN), scalar=modT[:, 40 + f:41 + f], in1=x1f[xi],
                    op0=ALU.mult, op1=ALU.add), [hm2, hxl, h_mod, F(("oo", xi))])
                bfree[bk] = ho
                fr_[("x1f", xi)] = ho
                hd = dma("sync", outT[f * 128:(f + 1) * 128, c0:c0 + TN], oo[xi], deps=[ho], dsem=ds_o[xi])
                fr_[("oo", xi)] = hd
            fr_["u"] = hm2
        for k in range(3):
            out_handles.append((ds_o[k][0], ds_o[k][1]))

    if "C" in phases:
        phase_C1()
        phase_C2()

    B.final_wait("sync", out_handles + B.barrier_handles())
    B.finish()
    return nc


def _host_consts():
    c = np.zeros((128, 6, 128), np.float32)
    c[:, 0, :] = np.eye(128)
    c[:, 1, :] = 1.0
    bd = np.zeros((128, 128), np.float32)
    bd[:64, :64] = 1.0 / 64
    bd[64:, 64:] = 1.0 / 64
    c[:, 2, :] = bd
    j = np.arange(128)
    c[:, 3, :] = (j[:, None] <= j[None, :]).astype(np.float32)
    c[:, 4, :] = (j[:, None] > j[None, :]).astype(np.float32)
    c[:, 5, :] = (j[:, None] >= j[None, :]).astype(np.float32)
    return c


def make_in_maps(inp):
    f = lambda a: np.ascontiguousarray(np.asarray(a, dtype=np.float32))
    x = f(inp["x"])
    c = f(inp["c"])
    shared = {
        "w_ada": f(inp["w_ada"][0]),
        "b_adaT": f(inp["b_ada"][0].reshape(48, 128).T),
        "n1w": f(inp["norm1_w"][0].reshape(8, 128).T),
        "n2w": f(inp["norm2_w"][0].reshape(8, 128).T),
        "w_in": f(inp["w_in"][0]),
        "convw": f(np.asarray(inp["conv_w"][0]).T.reshape(16, 128, 4).transpose(1, 0, 2)),
        "convb": f(np.asarray(inp["conv_b"][0]).reshape(16, 128).T),
        "dtb": f(np.broadcast_to(np.asarray(inp["dt_bias"][0])[None, :], (128, 16))),
        "alog": f(np.broadcast_to(np.asarray(inp["a_log"][0])[None, :], (128, 16))),
        "dskip": f(np.broadcast_to(np.asarray(inp["d_skip"][0])[None, :], (128, 16))),
        "ssdnw": f(np.broadcast_to(np.asarray(inp["ssd_norm_w"][0])[None, :], (128, 1024))),
        "qkw": f(np.stack([np.tile(np.asarray(inp["q_norm_w"][0]), 2),
                           np.tile(np.asarray(inp["k_norm_w"][0]), 2)], axis=1)),
        "attnw": f(np.broadcast_to(np.asarray(inp["attn_norm_w"][0])[None, :], (128, 1024))),
        "w_out": f(inp["w_out"][0]),
        "w_ff1": f(inp["w_ff1"][0]),
        "w_ff2": f(inp["w_ff2"][0]),
        "cmat": _host_consts(),
    }
    maps = []
    for b in range(8):
        m = dict(shared)
        m["xT"] = f(x[b].T)
        m["cT"] = f(c[b].reshape(8, 128).T)
        maps.append(m)
    return maps


_NC_CACHE = {}


def kernel(**inputs):
    key = (PHASES, DEBUG)
    if key not in _NC_CACHE:
        _NC_CACHE[key] = build_program(PHASES, DEBUG)
    nc = _NC_CACHE[key]
    in_maps = make_in_maps(inputs)
    res = run_bass_kernel_spmd(nc, in_maps, core_ids=list(range(8)))
    out = np.stack([np.ascontiguousarray(r["outT"].T) for r in res.results], axis=0)
    return out.astype(np.float32)
```
